# Optimizing a Trainium2 kernel written in Bass

```python
import math
import jax, jax.numpy as jnp
from jax import lax
import numpy as np

D_MODEL = 1024
BATCH = 4
SEQ = 8192
DEPTH = 1
DEC_BATCH = 16
DEC_SEQ = 32
PAST_LEN = 2048

CHUNK = 64
Q_BLOCK = 128
H_A = 4
DK_A = 64
DV_A = 2 * DK_A
H_R = 4
DK_R = 128
DV_R = 128
D_MIX = H_A * DV_A + H_R * DV_R
D_FF = -(-8 * D_MODEL // (3 * 256)) * 256
N_BUCKETS = 32
MAX_DISTANCE = 128
ROPE_BASE = 10000.0
EPS = 1e-6
NEG_INF = -1e30
COLS = [H_A * 2 * DK_A, H_A * 2 * DK_A, H_A * DV_A, H_R * DK_R, H_R * DK_R, H_R * DV_R, H_R * DV_R]
D_IN = sum(COLS)
SPLITS = [int(s) for s in np.cumsum(COLS)[:-1]]

kernel_name = 'hybrid_diffattn_retention_stream_step'


def lambda_init_of(layer):
    return 0.8 - 0.6 * math.exp(-0.3 * layer)


def rms_norm(x, g):
    xf = x.astype(jnp.float32)
    y = xf * lax.rsqrt(jnp.mean(xf * xf, axis=-1, keepdims=True) + EPS)
    return (y * g.astype(jnp.float32)).astype(x.dtype)


def rms_unit(x):
    xf = x.astype(jnp.float32)
    return xf * lax.rsqrt(jnp.mean(xf * xf, axis=-1, keepdims=True) + EPS)


def t5_bucket(rel):
    nb = N_BUCKETS // 2
    ret = jnp.where(rel > 0, nb, 0)
    n = jnp.abs(rel)
    max_exact = nb // 2
    large = max_exact + (jnp.log(jnp.maximum(n, 1).astype(jnp.float32) / max_exact)
                         / math.log(MAX_DISTANCE / max_exact) * (nb - max_exact)).astype(jnp.int32)
    large = jnp.minimum(large, nb - 1)
    return ret + jnp.where(n < max_exact, n, large)


def rotate(x, pos):
    d = x.shape[-1]
    inv = 1.0 / (ROPE_BASE ** jnp.linspace(0.0, 1.0, d // 2, dtype=jnp.float32))
    ang = pos.astype(jnp.float32)[:, None] * inv[None, :]
    sin = jnp.sin(ang)[None, :, None, :]
    cos = jnp.cos(ang)[None, :, None, :]
    xf = x.astype(jnp.float32)
    x1 = xf[..., 0::2]
    x2 = xf[..., 1::2]
    out = jnp.stack([x1 * cos - x2 * sin, x1 * sin + x2 * cos], axis=-1).reshape(x.shape)
    return out.astype(x.dtype)


def diff_attention(q, k, v, q_pos, k_pos, rel_bias, lam, g_subln, lam_init):
    rel = k_pos[None, :] - q_pos[:, None]
    bias = jnp.transpose(rel_bias[t5_bucket(rel)], (2, 0, 1)).astype(jnp.float32)
    visible = (k_pos[None, :] // CHUNK) <= (q_pos[:, None] // CHUNK)
    logits = jnp.einsum('bqhmd,bkhmd->bhmqk', q, k, preferred_element_type=jnp.float32) * (DK_A ** -0.5)
    logits = jnp.where(visible, logits + bias[None, :, None], NEG_INF)
    p = jax.nn.softmax(logits, axis=-1)
    attn = p[:, :, 0] - lam.astype(jnp.float32) * p[:, :, 1]
    o = jnp.einsum('bhqk,bkhd->bqhd', attn.astype(v.dtype), v)
    return rms_norm(o, g_subln) * (1.0 - lam_init)


def diff_attention_blocked(q, k, v, pos, rel_bias, lam, g_subln, lam_init):
    B, S = q.shape[0], q.shape[1]
    nb = S // Q_BLOCK
    qb = jnp.moveaxis(q.reshape(B, nb, Q_BLOCK, H_A, 2, DK_A), 1, 0)
    pb = pos.reshape(nb, Q_BLOCK)
    ob = lax.map(lambda a: diff_attention(a[0], k, v, a[1], pos, rel_bias, lam, g_subln, lam_init), (qb, pb))
    return jnp.moveaxis(ob, 0, 1).reshape(B, S, H_A, DV_A)


def retention_chunk(S, q, k, v, log_gamma):
    q = q.astype(jnp.float32)
    k = k.astype(jnp.float32)
    v = v.astype(jnp.float32)
    L = q.shape[1]
    i = jnp.arange(L, dtype=jnp.float32)
    diff = i[:, None] - i[None, :]
    decay = jnp.where(diff >= 0, jnp.exp(jnp.maximum(diff, 0.0)[None] * log_gamma[:, None, None]), 0.0)
    scores = jnp.einsum('blhd,bmhd->bhlm', q, k) * decay[None]
    o_inner = jnp.einsum('bhlm,bmhe->blhe', scores, v)
    q_decay = jnp.exp((i + 1.0)[:, None] * log_gamma[None, :])
    o_cross = jnp.einsum('blhd,bhde->blhe', q * q_decay[None, :, :, None], S)
    k_decay = jnp.exp((L - 1.0 - i)[:, None] * log_gamma[None, :])
    S_new = jnp.exp(L * log_gamma)[None, :, None, None] * S + jnp.einsum('blhd,blhe->bhde', k * k_decay[None, :, :, None], v)
    return S_new, o_inner + o_cross


def retention_scan(S0, q, k, v, log_gamma):
    B, S = q.shape[0], q.shape[1]
    nc = S // CHUNK

    def split(t):
        return jnp.moveaxis(t.reshape(B, nc, CHUNK, t.shape[2], t.shape[3]), 1, 0)

    def step(Sc, xs):
        return retention_chunk(Sc, xs[0], xs[1], xs[2], log_gamma)

    S_fin, o = lax.scan(step, S0, (split(q), split(k), split(v)))
    return S_fin, jnp.moveaxis(o, 0, 1).reshape(B, S, H_R, DV_R)


def trunk_layer(x, c, pos, k_hist, v_hist, s_hist, w_ada, b_ada, g_norm1, g_norm2, w_in, g_q, g_k,
                lam_q1, lam_k1, lam_q2, lam_k2, g_subln, w_out, w_ff_gate, w_ff_up, w_ff_down,
                rel_bias, lam_init):
    B, L, _ = x.shape
    mod = jnp.einsum('bd,de->be', jax.nn.silu(c), w_ada) + b_ada
    shift1, scale1, gate1, shift2, scale2, gate2 = jnp.split(mod[:, None, :], 6, axis=-1)
    h = rms_norm(x, g_norm1) * (1.0 + scale1) + shift1
    z = jnp.einsum('bld,de->ble', h, w_in)
    qa, ka, va, qr, kr, vr, gr = jnp.split(z, SPLITS, axis=-1)
    qa = rms_norm(qa.reshape(B, L, H_A, 2, DK_A), g_q)
    ka = rms_norm(ka.reshape(B, L, H_A, 2, DK_A), g_k)
    va = va.reshape(B, L, H_A, DV_A)
    qr = rotate(qr.reshape(B, L, H_R, DK_R), pos)
    kr = rotate(kr.reshape(B, L, H_R, DK_R), pos) * (DK_R ** -0.5)
    vr = vr.reshape(B, L, H_R, DV_R)
    lam = jnp.exp(jnp.sum(lam_q1 * lam_k1)) - jnp.exp(jnp.sum(lam_q2 * lam_k2)) + lam_init
    log_gamma = jnp.log(1.0 - 2.0 ** (-5.0 - jnp.arange(H_R, dtype=jnp.float32)))
    if k_hist is None:
        oa = diff_attention_blocked(qa, ka, va, pos, rel_bias, lam, g_subln, lam_init)
        S0 = jnp.zeros((B, H_R, DK_R, DV_R), jnp.float32)
        S_new, orr = retention_scan(S0, qr, kr, vr, log_gamma)
    else:
        P = k_hist.shape[1]
        k_all = jnp.concatenate([k_hist.astype(ka.dtype), ka], axis=1)
        v_all = jnp.concatenate([v_hist.astype(va.dtype), va], axis=1)
        k_pos = jnp.arange(P + L, dtype=jnp.int32)
        oa = diff_attention(qa, k_all, v_all, pos, k_pos, rel_bias, lam, g_subln, lam_init)
        S_new, orr = retention_chunk(s_hist.astype(jnp.float32), qr, kr, vr, log_gamma)
    orr = rms_unit(orr).astype(x.dtype).reshape(B, L, H_R * DV_R) * jax.nn.silu(gr)
    o = jnp.concatenate([oa.reshape(B, L, H_A * DV_A), orr], axis=-1)
    x = x + gate1 * jnp.einsum('ble,ed->bld', o, w_out)
    h2 = rms_norm(x, g_norm2) * (1.0 + scale2) + shift2
    ff = jax.nn.silu(jnp.einsum('bld,df->blf', h2, w_ff_gate)) * jnp.einsum('bld,df->blf', h2, w_ff_up)
    x = x + gate2 * jnp.einsum('blf,fd->bld', ff, w_ff_down)
    return x, ka, va, S_new.astype(x.dtype)


def setup_inputs(seed: int = 0) -> dict:
    key = jax.random.key(seed)
    ks = jax.random.split(key, 24)
    f32 = jnp.float32

    def nrm(k, shape, s):
        return s * jax.random.normal(k, shape, f32)

    return {
        'x_prompt': nrm(ks[0], (BATCH, SEQ, D_MODEL), 1.0),
        'x_sample': nrm(ks[1], (DEC_BATCH, DEC_SEQ, D_MODEL), 1.0),
        'c_prompt': nrm(ks[2], (BATCH, D_MODEL), 1.0),
        'c_sample': nrm(ks[3], (DEC_BATCH, D_MODEL), 1.0),
        'cache_k': nrm(ks[4], (DEPTH, DEC_BATCH, PAST_LEN, H_A, 2, DK_A), 1.0),
        'cache_v': nrm(ks[5], (DEPTH, DEC_BATCH, PAST_LEN, H_A, DV_A), 1.0),
        'state_ret': nrm(ks[6], (DEPTH, DEC_BATCH, H_R, DK_R, DV_R), 0.5),
        'w_ada': nrm(ks[7], (DEPTH, D_MODEL, 6 * D_MODEL), 0.5 * D_MODEL ** -0.5),
        'b_ada': nrm(ks[8], (DEPTH, 6 * D_MODEL), 0.01),
        'g_norm1': 1.0 + nrm(ks[9], (DEPTH, D_MODEL), 0.05),
        'g_norm2': 1.0 + nrm(ks[10], (DEPTH, D_MODEL), 0.05),
        'w_in': nrm(ks[11], (DEPTH, D_MODEL, D_IN), D_MODEL ** -0.5),
        'g_q': 1.0 + nrm(ks[12], (DEPTH, DK_A), 0.05),
        'g_k': 1.0 + nrm(ks[13], (DEPTH, DK_A), 0.05),
        'lam_q1': nrm(ks[14], (DEPTH, DK_A), 0.1),
        'lam_k1': nrm(ks[15], (DEPTH, DK_A), 0.1),
        'lam_q2': nrm(ks[16], (DEPTH, DK_A), 0.1),
        'lam_k2': nrm(ks[17], (DEPTH, DK_A), 0.1),
        'g_subln': 1.0 + nrm(ks[18], (DEPTH, DV_A), 0.05),
        'w_out': nrm(ks[19], (DEPTH, D_MIX, D_MODEL), D_MIX ** -0.5),
        'w_ff_gate': nrm(ks[20], (DEPTH, D_MODEL, D_FF), D_MODEL ** -0.5),
        'w_ff_up': nrm(ks[21], (DEPTH, D_MODEL, D_FF), D_MODEL ** -0.5),
        'w_ff_down': nrm(ks[22], (DEPTH, D_FF, D_MODEL), D_FF ** -0.5),
        'rel_bias': nrm(ks[23], (N_BUCKETS, H_A), 0.5),
    }


def reference(x_prompt, x_sample, c_prompt, c_sample, cache_k, cache_v, state_ret, w_ada, b_ada,
              g_norm1, g_norm2, w_in, g_q, g_k, lam_q1, lam_k1, lam_q2, lam_k2, g_subln, w_out,
              w_ff_gate, w_ff_up, w_ff_down, rel_bias):
    S = x_prompt.shape[1]
    L = x_sample.shape[1]
    P = cache_k.shape[2]
    pos_prompt = jnp.arange(S, dtype=jnp.int32)
    pos_sample = P + jnp.arange(L, dtype=jnp.int32)
    xp, xs = x_prompt, x_sample
    kp_l, vp_l, sp_l, ks_l, vs_l, ss_l = [], [], [], [], [], []
    for l in range(DEPTH):
        lam_init = lambda_init_of(l)
        w = (w_ada[l], b_ada[l], g_norm1[l], g_norm2[l], w_in[l], g_q[l], g_k[l], lam_q1[l], lam_k1[l],
             lam_q2[l], lam_k2[l], g_subln[l], w_out[l], w_ff_gate[l], w_ff_up[l], w_ff_down[l], rel_bias)
        xp, kp, vp, sp = trunk_layer(xp, c_prompt, pos_prompt, None, None, None, *w, lam_init)
        xs, ksn, vsn, ssn = trunk_layer(xs, c_sample, pos_sample, cache_k[l], cache_v[l], state_ret[l], *w, lam_init)
        kp_l.append(kp)
        vp_l.append(vp)
        sp_l.append(sp)
        ks_l.append(ksn)
        vs_l.append(vsn)
        ss_l.append(ssn)
    return (xp, xs, jnp.stack(kp_l), jnp.stack(vp_l), jnp.stack(sp_l), jnp.stack(ks_l), jnp.stack(vs_l), jnp.stack(ss_l))
```

```python
import math
import numpy as np
import concourse.bass as bass
import concourse.mybir as mybir
from concourse.bass_utils import run_bass_kernel_spmd

F32 = mybir.dt.float32
BF16 = mybir.dt.bfloat16
AF = mybir.ActivationFunctionType
ALU = mybir.AluOpType
AX = mybir.AxisListType
AP = bass.AP

D = 1024
DIN = 3584
DFF = 2816
NKF = DFF // 128
SEQ = 8192
TT = 512
NT = SEQ // TT
PAST = 2048
LS = 32
EPS = 1e-6
LAM_INIT = 0.8 - 0.6 * math.exp(-0.3 * 0)
NEG = -30000.0
WARM_EPI = 12
WARM_NORM = 16
ENGS = ("pe", "act", "dve", "pool", "sp")


class Op:
    __slots__ = ("eng", "fn", "deps", "is_dma", "signal", "val", "sem")

    def __init__(self, eng, fn, deps, is_dma):
        self.eng = eng
        self.fn = fn
        self.deps = deps
        self.is_dma = is_dma
        self.signal = is_dma
        self.val = None
        self.sem = None


class Prog:
    def __init__(self, nc, n_dma_sems=24):
        self.nc = nc
        self.ops = {e: [] for e in ENGS}
        self.all_ops = []
        self.bufs = {}
        self.n_dma_sems = n_dma_sems
        self.dma_rr = {"sp": 0, "pool": 0, "act": 0}
        self.dma_last = {}

    @staticmethod
    def _flat(keys):
        out = []
        for k in keys:
            if isinstance(k, list):
                out.extend(k)
            else:
                out.append(k)
        return out

    def _deps_for(self, reads, writes):
        deps = []
        for k in reads:
            ent = self.bufs.get(k)
            if ent is not None and ent[0] is not None:
                deps.append(ent[0])
        for k in writes:
            ent = self.bufs.get(k)
            if ent is not None:
                if ent[0] is not None:
                    deps.append(ent[0])
                deps.extend(ent[1])
        return deps

    def _commit(self, op, reads, writes):
        for k in reads:
            ent = self.bufs.setdefault(k, [None, []])
            if not op.is_dma:
                ent[1] = [r for r in ent[1] if r.is_dma or r.eng != op.eng]
            ent[1].append(op)
        for k in writes:
            self.bufs[k] = [op, []]

    def op(self, eng, fn, reads=(), writes=()):
        reads, writes = self._flat(reads), self._flat(writes)
        o = Op(eng, fn, self._deps_for(reads, writes), False)
        self.all_ops.append(o)
        self.ops[eng].append(o)
        self._commit(o, reads, writes)
        return o

    def dma(self, queue, fn, reads=(), writes=()):
        reads, writes = self._flat(reads), self._flat(writes)
        deps = self._deps_for(reads, writes)
        j = self.dma_rr[queue]
        nq = self.n_dma_sems if queue == "sp" else 4
        self.dma_rr[queue] = (j + 1) % nq
        prev = self.dma_last.get((queue, j))
        if prev is not None:
            deps.append(prev)
        o = Op(queue, fn, deps, True)
        o.sem = (queue, j)
        self.all_ops.append(o)
        self.ops[queue].append(o)
        self.dma_last[(queue, j)] = o
        self._commit(o, reads, writes)
        return o

    def emit(self, final_wait_ops=()):
        nc = self.nc
        for o in self.all_ops:
            for d in o.deps:
                if d.is_dma:
                    continue
                if d.eng == o.eng and d.eng == "pe" and not o.is_dma:
                    continue
                d.signal = True
        for o in final_wait_ops:
            if not o.is_dma:
                o.signal = True
        for e in ("pe", "act", "dve", "pool"):
            comp = [o for o in self.ops[e] if not o.is_dma]
            if comp:
                comp[-1].signal = True
        sem_h = {}
        for e in ("pe", "act", "dve", "pool"):
            sem_h[("eng", e)] = nc.alloc_semaphore(f"s_{e}")
        for q in ("sp", "pool", "act"):
            used = set(o.sem[1] for o in self.ops[q] if o.is_dma)
            for j in sorted(used):
                sem_h[(q, j)] = nc.alloc_semaphore(f"d_{q}{j}")
        cnt = {}
        for e in ENGS:
            for o in self.ops[e]:
                if o.is_dma:
                    k = o.sem
                    cnt[k] = cnt.get(k, 0) + 16
                    o.val = cnt[k]
                elif o.signal:
                    k = ("eng", e)
                    cnt[k] = cnt.get(k, 0) + 1
                    o.val = cnt[k]
                    o.sem = k
        self.stats = dict(n_ops={e: len(self.ops[e]) for e in ENGS}, max_sem=max(cnt.values()),
                          n_sig={e: cnt.get(("eng", e), 0) for e in ENGS}, n_wait={})
        prog = self

        def run_engine(e, h):
            seen = {}
            nwait = 0
            for o in prog.ops[e]:
                need = {}
                for d in o.deps:
                    if (not d.is_dma) and d.eng == e and e == "pe" and not o.is_dma:
                        continue
                    k = d.sem
                    if d.val > seen.get(k, 0) and d.val > need.get(k, 0):
                        need[k] = d.val
                for k, v in need.items():
                    h.wait_ge(sem_h[k], v)
                    seen[k] = v
                    nwait += 1
                ins = o.fn(h)
                if o.is_dma:
                    ins.then_inc(sem_h[o.sem], 16)
                elif o.signal:
                    ins.then_inc(sem_h[o.sem], 1)
            if e == "sp":
                for k, v in cnt.items():
                    h.wait_ge(sem_h[k], v)
            prog.stats["n_wait"][e] = nwait

        with nc.Block() as block:
            @block.tensor
            def _(h):
                run_engine("pe", h)

            @block.scalar
            def _(h):
                run_engine("act", h)

            @block.vector
            def _(h):
                run_engine("dve", h)

            @block.gpsimd
            def _(h):
                run_engine("pool", h)

            @block.sync
            def _(h):
                run_engine("sp", h)


def _t5_bucket_np(rel):
    import jax
    import jax.numpy as jnp
    cpu = jax.devices("cpu")[0]
    with jax.default_device(cpu):
        rel = jnp.asarray(rel, dtype=jnp.int32)
        nb = 16
        ret = jnp.where(rel > 0, nb, 0)
        n = jnp.abs(rel)
        max_exact = nb // 2
        large = max_exact + (jnp.log(jnp.maximum(n, 1).astype(jnp.float32) / max_exact)
                             / math.log(128 / max_exact) * (nb - max_exact)).astype(jnp.int32)
        large = jnp.minimum(large, nb - 1)
        out = ret + jnp.where(n < max_exact, n, large)
        return np.asarray(out)


_CONST_CACHE = {}


def _gammas():
    return (1.0 - 2.0 ** (-5.0 - np.arange(4, dtype=np.float64)))


def _rope_tables(pos, L):
    n = pos.shape[0]
    inv = (np.float32(1.0) / (np.float32(10000.0) ** np.linspace(0.0, 1.0, 64, dtype=np.float32))).astype(np.float32)
    ang = pos.astype(np.float32)[:, None] * inv[None, :]
    cos = np.cos(ang).astype(np.float64)
    sin = np.sin(ang).astype(np.float64)
    l = (np.arange(n) % L).astype(np.float64)
    lg = np.log(_gammas())
    qd = np.exp((l[:, None] + 1.0) * lg[None, :])
    kd = np.exp(-(l[:, None] + 1.0) * lg[None, :]) / math.sqrt(128.0)
    out = np.zeros((n, 4, 4, 64), np.float64)
    out[:, 0] = cos[:, None, :] * qd[:, :, None]
    out[:, 1] = sin[:, None, :] * qd[:, :, None]
    out[:, 2] = cos[:, None, :] * kd[:, :, None]
    out[:, 3] = sin[:, None, :] * kd[:, :, None]
    return out.reshape(n, 4, 256).astype(np.float32)


def _static_consts():
    if "c" in _CONST_CACHE:
        return _CONST_CACHE["c"]
    g = _gammas()
    rel = 127 - np.arange(384)
    bk = _t5_bucket_np(rel)
    ohr = np.zeros((32, 384), np.float32)
    ohr[bk, np.arange(384)] = 1.0
    oh15 = np.zeros((32, 128), np.float32)
    b_far = int(_t5_bucket_np(np.array([-200]))[0])
    oh15[b_far, :] = 1.0
    maskadd = np.zeros((128, 256), np.float32)
    k = np.arange(128)[:, None]
    c = np.arange(256)[None, :]
    maskadd[(k // 64) > (c // 64)] = NEG
    kc = np.zeros((128, 448), np.float32)
    kc[0, 0:128] = 1.0
    kc[32, 128:256] = 1.0
    m = np.arange(128)[:, None]
    l = np.arange(128)[None, :]
    kc[:, 256:384] = (l >= m).astype(np.float32)
    kc2 = np.zeros((128, 256), np.float32)
    kc2[np.arange(128), 127 - np.arange(128)] = 1.0
    kc2[np.arange(128), 128 + np.arange(128)] = 1.0
    kc = np.concatenate([kc[:, :384], kc2], axis=1)
    dco = np.zeros((2, 128, 32), np.float32)
    for p in range(2):
        a = np.ones(4) if p == 0 else g ** 512
        b = np.zeros(4) if p == 0 else g ** 512
        e = g ** 1024 if p == 0 else g ** 512
        f = g ** 512 if p == 0 else g ** 1024
        dco[p, :, 0:4] = a
        dco[p, :, 4:8] = b
        dco[p, :, 8:12] = g ** 1024
        dco[p, :, 12:16] = e
        dco[p, :, 16:20] = f
        sel = [1, 0, 0, 0, 0, NEG, 0, NEG] if p == 0 else [0, 1, 0, 1, 0, 0, 1, 0]
        dco[p, :, 20:28] = np.array(sel, np.float32)
        dco[p, :, 28:32] = g ** 32
    rope = np.zeros((2, NT, TT, 4, 256), np.float32)
    for p in range(2):
        for i in range(NT // 2):
            for j, tile in enumerate((2 * i + 1 - p, 2 * i + p)):
                pos = tile * TT + np.arange(TT)
                rope[p, 2 * i + j] = _rope_tables(pos, TT)
    rope_s = _rope_tables(PAST + (np.arange(64) % LS), LS)
    res = dict(ohr=ohr, oh15=oh15, maskadd=maskadd, kc=kc, dco=dco, rope=rope, rope_s=rope_s)
    _CONST_CACHE["c"] = res
    return res


class _StopBuild(Exception):
    pass


def build_program(n_steps=NT // 2, do_sample=True, cut=0):
    def chk(n):
        if cut == n:
            raise _StopBuild()
    nc = bass.Bass("TRN2", target_bir_lowering=False)
    P = Prog(nc)

    def din(name, shape, dt=F32):
        return nc.dram_tensor(name, list(shape), dt, kind="ExternalInput").ap()

    def dout(name, shape, dt=F32):
        return nc.dram_tensor(name, list(shape), dt, kind="ExternalOutput").ap()

    def dscr(name, shape, dt):
        return nc.dram_tensor(name, list(shape), dt, kind="Internal").ap()

    def sb(name, shape, dt):
        return nc.alloc_sbuf_tensor("s_" + name, list(shape), dt).ap()

    xs = din("xs", [NT, TT, D])
    xsm = din("xsm", [64, D])
    crow_d = din("crow", [5, D])
    ck = din("ck", [2, PAST, 512])
    cv = din("cv", [2, PAST, 4, 128])
    st = din("st", [2, 4, 128, 128])
    w_ada = din("w_ada", [D, 6 * D])
    b_ada = din("b_ada", [1, 6 * D])
    w_in = din("w_in", [D, DIN])
    gqk = din("gqk", [1, 128])
    lamv = din("lamv", [1, 256])
    gsub = din("gsub", [1, 128])
    w_out = din("w_out", [D, D])
    wg = din("wg", [D, DFF])
    wu = din("wu", [D, DFF])
    wd = din("wd", [DFF, D])
    rbt_d = din("rbt", [32, 4])
    rope_d = din("rope", [NT, TT, 4, 256])
    rope_sd = din("rope_s", [64, 4, 256])
    dco_d = din("dco", [128, 32])
    ohr_d = din("ohr", [32, 384])
    oh15_d = din("oh15", [32, 128])
    maskadd_d = din("maskadd", [128, 256])
    kc_d = din("kc", [128, 640])

    y = dout("y", [NT // 2, TT, D])
    ysm = dout("ysm", [64, D])
    kp = dout("kp", [NT // 2, TT, 512])
    vp = dout("vp", [NT // 2, TT, 512])
    rp = dout("rp", [4, 128, 128])
    ksm = dout("ksm", [64, 512])
    vsm = dout("vsm", [64, 512])
    rs = dout("rs", [2, 4, 128, 128])

    Wi = dscr("Wi", [7, 128, 8, 512], BF16)
    Wo = dscr("Wo", [2, 128, 8, 512], BF16)
    Wgu = dscr("Wgu", [11, 128, 8, 512], BF16)
    Wd = dscr("Wd", [2, 128, NKF, 512], BF16)
    KT = dscr("KT", [4, NT, 128, 512], BF16)
    VS = dscr("VS", [4, NT, 128, 4, 128], BF16)
    KTs = dscr("KTs", [2, 4, 4, 128, 512], BF16)
    VSs = dscr("VSs", [2, 4, 4, 128, 4, 128], BF16)
    urd = dscr("urd", [4, 384], F32)

    kc = sb("kc", [128, 640], F32)
    sel0 = kc[0:64, 0:128]
    sel1 = kc[0:64, 128:256]
    trif = kc[:, 256:384]
    Jf = kc[:, 384:512]
    identf = kc[:, 512:640]
    identb = sb("identb", [128, 128], BF16)
    trib = sb("trib", [128, 128], BF16)
    onesb = sb("onesb", [128, 128], BF16)
    onesf = sb("onesf", [128, 128], F32)
    dco = sb("dco", [128, 32], F32)
    Mh = sb("Mh", [128, 4, 256], F32)
    G0 = sb("G0", [128, 4, 128], F32)
    G4 = sb("G4", [128, 4, 128], F32)
    MS = sb("MS", [64, 4, 32], F32)
    cols = sb("cols", [128, 64], F32)
    C15, CC0, CC4, CCF, NLAM, GSUB, EPSC = 0, 4, 8, 12, 16, 17, 18
    modc = sb("modc", [128, 4, 8, 3], F32)
    gate1b = sb("gate1b", [128, D], F32)
    gate2b = sb("gate2b", [128, D], F32)
    gqkb = sb("gqkb", [128, 128], F32)
    cT = sb("cT", [128, 8, 5], F32)
    scT = sb("scT", [128, 8, 3], BF16)
    scB = sb("scB", [128, 8, 128], BF16)
    scS = sb("scS", [128, 8, 64], BF16)
    browb = sb("browb", [1, 1, 512], F32)
    smallr = sb("smallr", [32, 512], F32)
    xo = sb("xo", [128, 4, D], F32)
    xf = sb("xf", [128, 1, D], F32)
    crow5 = xf[0:5, 0, :]
    gfull = sb("gfull", [128, 2, 512], F32)
    hT = sb("hT", [128, 8, TT], BF16)
    rt = sb("rt", [128, 2, 4, 256], F32)
    tokb = sb("tokb", [128, 4, 512], BF16)
    qAT = sb("qAT", [128, 2, 4, TT], BF16)
    kst = sb("kst", [128, 4, TT], BF16)
    vst = sb("vst", [128, 4, 512], BF16)
    kf = sb("kf", [128, 1, 512], F32)
    vf = sb("vf", [128, 1, 512], F32)
    qdT = sb("qdT", [128, 4, TT], BF16)
    keT = sb("keT", [128, 4, TT], BF16)
    ke = sb("ke", [128, 4, 512], BF16)
    vR = sb("vR", [128, 4, 512], BF16)
    sgT = sb("sgT", [128, 4, TT], BF16)
    oT = sb("oT", [128, 8, TT], BF16)
    ffT = sb("ffT", [128, NKF, TT], BF16)
    NW = 3
    wring = sb("wring", [128, NW, 4096], BF16)
    NKB = 2
    ktb = sb("ktb", [128, NKB, 512], BF16)
    vsb = sb("vsb", [128, NKB, 512], BF16)
    Pm = sb("Pm", [128, 4, 512], BF16)
    scb = Pm
    lg = sb("lg", [128, 2, 512], F32)
    et = sb("et", [128, 4, 512], F32)
    xn = sb("xn", [128, 4, D], BF16)
    lamt = sb("lamt", [1, 128], F32)
    sqb = sb("sqb", [128, 512], BF16)
    warmb = sb("warmb", [128, 512], BF16)
    S = sb("S", [128, 4, 128], F32)
    Sst = sb("Sst", [128, 4, 128], F32)
    tmpS = sb("tmpS", [128, 4, 128], F32)
    Sb = sb("Sb", [128, 4, 128], BF16)
    Uf = sb("Uf", [128, 4, 128], F32)
    stat = sb("stat", [128, 64], F32)

    psall = nc.alloc_psum_tensor("psall", [128, 8, 512], F32).ap()
    psallb = psall.bitcast(BF16)
    psb = [psall[:, i, :] for i in range(8)]
    psbb = [psallb[:, i, :] for i in range(8)]

    def PS(i):
        return ("ps", i)

    def mm(out, lhsT, rhs, start, stop, r, w):
        return P.op("pe", lambda e: e.matmul(out, lhsT=lhsT, rhs=rhs, start=start, stop=stop), reads=r, writes=w)

    def warm(n, bank):
        for _ in range(n):
            mm(psb[bank][:, :], onesb[:, :], warmb[:, :], True, True, ["onesb", "warmb"], [PS(bank)])

    def trn(out, in_, ident, r, w):
        return P.op("pe", lambda e: e.transpose(out, in_, ident), reads=r, writes=w)

    def act(out, in_, func, r, w, scale=1.0, bias=0.0, accum=None):
        if accum is None:
            return P.op("act", lambda e: e.activation(out=out, in_=in_, func=func, bias=bias, scale=scale), reads=r, writes=w)
        return P.op("act", lambda e: e.activation(out=out, in_=in_, func=func, bias=bias, scale=scale, accum_out=accum), reads=r, writes=w)

    def tt(out, in0, in1, op, r, w, eng="dve"):
        return P.op(eng, lambda e: e.tensor_tensor(out=out, in0=in0, in1=in1, op=op), reads=r, writes=w)

    def ts(out, in0, s1, s2, op0, op1, r, w, eng="dve"):
        if op1 is None:
            return P.op(eng, lambda e: e.tensor_scalar(out=out, in0=in0, scalar1=s1, scalar2=None, op0=op0), reads=r, writes=w)
        return P.op(eng, lambda e: e.tensor_scalar(out=out, in0=in0, scalar1=s1, scalar2=s2, op0=op0, op1=op1), reads=r, writes=w)

    def stt(out, in0, scalar, in1, op0, op1, r, w):
        return P.op("dve", lambda e: e.scalar_tensor_tensor(out=out, in0=in0, scalar=scalar, in1=in1, op0=op0, op1=op1), reads=r, writes=w)

    def cp(out, in_, r, w, eng="dve"):
        return P.op(eng, lambda e: e.tensor_copy(out=out, in_=in_), reads=r, writes=w)

    def ld(out, in_, r=(), w=()):
        return P.dma("sp", lambda e: e.dma_start(out=out, in_=in_), reads=r, writes=w)

    def stq(out, in_, r=(), w=()):
        return P.dma("pool", lambda e: e.dma_start(out=out, in_=in_), reads=r, writes=w)

    ev_rr = [0]

    def evac(out, in_, r, w):
        ev_rr[0] ^= 1
        if ev_rr[0]:
            return act(out, in_, AF.Copy, r, w)
        return cp(out, in_, r, w)

    def fr(ap, dims):
        return AP(ap.tensor, ap.offset, [list(ap.ap[0])] + [list(d) for d in dims])

    out_ops = []

    def prepass_in():
        for cb in (1, 2, 4, 5, 0, 3, 6):
            src = w_in[:, cb * 512:(cb + 1) * 512].rearrange("(k p) c -> p k c", p=128)
            stq(Wi[cb], src, w=[("Wi", cb)])

    def prepass_rest():
        for cb in range(2):
            src = w_out[:, cb * 512:(cb + 1) * 512].rearrange("(k p) c -> p k c", p=128)
            stq(Wo[cb], src, w=[("Wo", cb)])
        for gb in range(11):
            stq(Wgu[gb][:, :, 0:256], wg[:, gb * 256:(gb + 1) * 256].rearrange("(k p) c -> p k c", p=128), w=[("Wgu", gb, 0)])
            stq(Wgu[gb][:, :, 256:512], wu[:, gb * 256:(gb + 1) * 256].rearrange("(k p) c -> p k c", p=128), w=[("Wgu", gb, 1)])
        for half in range(2):
            for bi, (k0, k1) in enumerate(((0, 8), (8, 16), (16, 22))):
                src = wd[k0 * 128:k1 * 128, half * 512:(half + 1) * 512].rearrange("(k p) c -> p k c", p=128)
                stq(Wd[half][:, k0:k1, :], src, w=[("Wd", half, bi)])

    wcnt = [0]

    def wslot(nk, ncol):
        r = wcnt[0] % NW
        wcnt[0] += 1
        view = wring[:, r, 0:nk * ncol].rearrange("p (k c) -> p k c", k=nk)
        return view, ("wr", r)

    def load_w(src, nk, ncol, rkeys):
        view, key = wslot(nk, ncol)
        ld(view, src, r=rkeys, w=[key])
        return view, key

    pf = {}
    WSRC = {}
    for cb_ in range(7):
        WSRC[("Wi", cb_)] = (Wi[cb_], 8, 512, [("Wi", cb_)])
    for cb_ in range(2):
        WSRC[("Wo", cb_)] = (Wo[cb_], 8, 512, [("Wo", cb_)])
    for gb_ in range(11):
        WSRC[("Wgu", gb_)] = (Wgu[gb_], 8, 512, [("Wgu", gb_, 0), ("Wgu", gb_, 1)])
    for half_ in range(2):
        for bi_, (k0_, k1_) in enumerate(((0, 8), (8, 16), (16, 22))):
            WSRC[("Wd", half_, bi_)] = (Wd[half_][:, k0_:k1_, :], k1_ - k0_, 512, [("Wd", half_, bi_)])

    def prefetch(name):
        if name not in pf:
            pf[name] = load_w(*WSRC[name])

    def get_w(name):
        if name in pf:
            return pf.pop(name)
        return load_w(*WSRC[name])

    def setup():
        ld(kc[:], kc_d[:], w=["kc"])
        ld(dco[:], dco_d[:], w=["dco"])
        ld(crow5, crow_d[:], w=[("xf", 0)])
        ld(smallr[0:1, 0:256], lamv[:], w=["lamr"])
        ld(smallr[0:1, 256:384], gsub[:], w=["gsr"])
        ld(smallr[0:1, 384:512], gqk[:], w=["gqr"])
        rbt = sb("rbt", [32, 4], F32)
        oh15 = lg[0:32, 1, 0:128]
        ohr = lg[0:32, 0, 0:384]
        ld(rbt[:], rbt_d[:], w=["rbt"])
        ld(oh15, oh15_d[:], w=[("lg", 1)])
        ld(ohr, ohr_d[:], w=[("lg", 0)])
        maskadd = et[:, 3, 0:256]
        ld(maskadd, maskadd_d[:], w=["et3"])
        P.op("dve", lambda e: e.memset(onesb[:], 1.0), writes=["onesb"])
        P.op("dve", lambda e: e.memset(warmb[:], 1.0), writes=["warmb"])
        P.op("dve", lambda e: e.memset(onesf[:], 1.0), writes=["onesf"])
        P.op("dve", lambda e: e.memset(cols[:, EPSC:EPSC + 1], EPS), writes=["epsc"])
        P.op("dve", lambda e: e.memset(S[:], 0.0), writes=["S"])
        P.op("pool", lambda e: e.memset(qAT[:].rearrange("p m h t -> p (m h t)"), 0.0), writes=[("qAT", h) for h in range(4)])
        cp(identb[:], identf, ["kc"], ["identb"])
        cp(trib[:], trif, ["kc"], ["trib"])
        tt(lamt[0:1, 0:64], smallr[0:1, 0:64], smallr[0:1, 64:128], ALU.mult, ["lamr"], ["st_a"])
        tt(lamt[0:1, 64:128], smallr[0:1, 128:192], smallr[0:1, 192:256], ALU.mult, ["lamr"], ["st_a"])
        lam2 = sb("lam2", [1, 8], F32)
        P.op("dve", lambda e: e.tensor_reduce(out=lam2[0:1, 0:2], in_=lamt[0:1, 0:128].rearrange("p (a j) -> p a j", a=2), axis=AX.X, op=ALU.add),
             reads=["st_a"], writes=["lam2a"])
        act(lam2[0:1, 2:4], lam2[0:1, 0:2], AF.Exp, ["lam2a"], ["lam2b"])
        tt(lam2[0:1, 4:5], lam2[0:1, 3:4], lam2[0:1, 2:3], ALU.subtract, ["lam2b"], ["lam2c"])
        ts(lam2[0:1, 5:6], lam2[0:1, 4:5], -LAM_INIT, None, ALU.add, None, ["lam2c"], ["lam2d"])
        P.op("dve", lambda e: e.memset(lam2[0:1, 6:7], 1.0 - LAM_INIT), writes=["lam2e"])
        pm = psb[7]
        mm(pm[:, 0:1], onesf[0:1, 0:128], lam2[0:1, 5:6], True, True, ["onesf", "lam2d"], [PS(7)])
        mm(pm[:, 1:2], smallr[0:1, 256:384], lam2[0:1, 6:7], True, True, ["gsr", "lam2e"], [PS(7)])
        mm(pm[:, 2:6], oh15, rbt[:, :], True, True, [("lg", 1), "rbt"], [PS(7)])
        mm(pm[:, 128:256], onesf[0:1, 0:128], smallr[0:1, 384:512], True, True, ["onesf", "gqr"], [PS(7)])
        cp(cols[:, NLAM:NLAM + 2], pm[:, 0:2], [PS(7)], ["nlam", "gsubc"])
        cp(cols[:, C15:C15 + 4], pm[:, 2:6], [PS(7)], ["c15"])
        cp(gqkb[:], pm[:, 128:256], [PS(7)], ["gqkb"])
        for i in range(2):
            for g in range(8):
                cp(gfull[:, i, g * 64:(g + 1) * 64], gqkb[:, i * 64:(i + 1) * 64], ["gqkb"], ["gfull"])
        ts(cols[:, CC0:CC0 + 4], cols[:, C15:C15 + 4], dco[:, 21:22], dco[:, 22:23], ALU.mult, ALU.add, ["c15", "dco"], ["cc0"])
        ts(cols[:, CC4:CC4 + 4], cols[:, C15:C15 + 4], dco[:, 24:25], dco[:, 25:26], ALU.mult, ALU.add, ["c15", "dco"], ["cc4"])
        ts(cols[:, CCF:CCF + 4], cols[:, C15:C15 + 4], dco[:, 26:27], dco[:, 27:28], ALU.mult, ALU.add, ["c15", "dco"], ["ccf"])
        pu = psb[6]
        mm(pu[0:4, 0:384], rbt[:, :], ohr, True, True, ["rbt", ("lg", 0)], [PS(6)])
        urs = et[0:4, 2, 0:384]
        cp(urs, pu[0:4, 0:384], [PS(6)], ["et2"])
        stq(urd[:], urs, r=["et2"], w=["urd"])
        hk = et[:, 0:2, :].rearrange("p a (b c) -> p (a b) c", c=256)
        ld(hk, AP(urd.tensor, 0, [[1, 128], [384, 4], [1, 256]]), r=["urd"], w=["et0", "et1"])
        for h in range(4):
            pj = psb[h % 2]
            mm(pj[:, 0:256], Jf, hk[:, h, :], True, True, ["kc", "et0", "et1"], [PS(h % 2)])
            tt(Mh[:, h, :], pj[:, 0:256], maskadd, ALU.add, [PS(h % 2), "et3"], [("Mh", h)])
            ts(G0[:, h, :], Mh[:, h, 128:256], dco[:, 20:21], cols[:, CC0 + h:CC0 + h + 1], ALU.mult, ALU.add, [("Mh", h), "dco", "cc0"], [("G0", h)])
            ts(G4[:, h, :], Mh[:, h, 128:256], dco[:, 23:24], cols[:, CC4 + h:CC4 + h + 1], ALU.mult, ALU.add, [("Mh", h), "dco", "cc4"], [("G4", h)])
        ld(MS[0:32, :, :], Mh[0:32, :, 0:32], r=[("Mh", h) for h in range(4)], w=["MS0"])
        ld(MS[32:64, :, :], Mh[0:32, :, 0:32], r=[("Mh", h) for h in range(4)], w=["MS1"])

        act(crow5[0:3, :], crow5[0:3, :], AF.Silu, [("xf", 0)], [("xf", 0)])
        pt = psb[5]
        for k in range(8):
            trn(pt[:, k * 5:(k + 1) * 5], crow5[0:5, k * 128:(k + 1) * 128], identf[0:5, 0:5], [("xf", 0), "kc"], [PS(5)])
        cp(cT[:].rearrange("p k v -> p (k v)"), pt[:, 0:40], [PS(5)], ["cT"])
        cp(scT[:], cT[:, :, 0:3], ["cT"], ["scT"])
        cp(scB[:], fr(cT[:, 0, 0:1], [[5, 8], [0, 128]]), ["cT"], ["scB"])
        cp(scS[:, :, 0:32], fr(cT[:, 0, 1:2], [[5, 8], [0, 32]]), ["cT"], ["scS0"])
        cp(scS[:, :, 32:64], fr(cT[:, 0, 2:3], [[5, 8], [0, 32]]), ["cT"], ["scS1"])
        psm = psb[4]
        psmv = psm[:, 0:96].rearrange("p (a c v) -> p a c v", a=4, c=8)
        for kind, base in ((0, 0), (1, 2), (2, 6), (3, 8)):
            for half in range(2):
                cbk = base + half
                view, key = wslot(8, 512)
                stq(view, w_ada[:, cbk * 512:(cbk + 1) * 512].rearrange("(k p) c -> p k c", p=128), w=[key])
                bslot = 0
                ld(browb[0:1, bslot, :], b_ada[0:1, cbk * 512:(cbk + 1) * 512], w=[("brow", bslot)])
                for e in range(4):
                    c = half * 4 + e
                    for k in range(8):
                        mm(psmv[:, kind, c, :], view[:, k, e * 128:(e + 1) * 128], scT[:, k, :], k == 0, False, [key, "scT"], [PS(4)])
                    mm(psmv[:, kind, c, :], browb[0:1, bslot, e * 128:(e + 1) * 128], onesf[0:1, 0:3], False, True, [("brow", bslot), "onesf"], [PS(4)])
        cp(modc[:, 0], psmv[:, 0], [PS(4)], ["modc0"])
        cp(modc[:, 2], psmv[:, 2], [PS(4)], ["modc2"])
        for c in range(8):
            ts(modc[:, 1, c, :], psmv[:, 1, c, :], 1.0, cT[:, c, 3:4], ALU.add, ALU.mult, [PS(4), "cT"], ["modc1"])
            ts(modc[:, 3, c, :], psmv[:, 3, c, :], 1.0, cT[:, c, 4:5], ALU.add, ALU.mult, [PS(4), "cT"], ["modc3"])

    def gates(sample):
        rows = 64 if sample else 128
        for gi, (gt, base) in enumerate(((gate1b, 4), (gate2b, 10))):
            for half in range(2):
                cbk = base + half
                view, key = wslot(8, 512)
                stq(view, w_ada[:, cbk * 512:(cbk + 1) * 512].rearrange("(k p) c -> p k c", p=128), w=[key])
                bslot = 0
                ld(browb[0:1, bslot, :], b_ada[0:1, cbk * 512:(cbk + 1) * 512], w=[("brow", bslot)])
                pg = psb[(gi * 2 + half) % 4]
                pk = PS((gi * 2 + half) % 4)
                for k in range(8):
                    lhs = scS[:, k, :] if sample else scB[:, k, :]
                    mm(pg[0:rows, :], lhs, view[:, k, :], k == 0, False, [key, "scB", "scS0", "scS1"], [pk])
                mm(pg[0:rows, :], onesf[0:1, 0:rows], browb[0:1, bslot, :], False, True, [("brow", bslot), "onesf"], [pk])
                act(gt[0:rows, half * 512:(half + 1) * 512], pg[0:rows, :], AF.Copy, [pk], [("gate", gi, half)])

    acc_rr = [0]

    def next_acc():
        i = acc_rr[0] % 4
        acc_rr[0] += 1
        return psb[i], PS(i)

    tb_rr = [0]

    def next_tb():
        i = tb_rr[0] % 2
        tb_rr[0] += 1
        return psbb[4 + i], PS(4 + i)

    def rstd_small(dst, src, scale, rows, rk, key):
        act(dst, src, AF.Ln, list(rk) + ["epsc"], [key], scale=scale, bias=cols[0:rows, EPSC:EPSC + 1])
        act(dst, dst, AF.Exp, [key], [key], scale=-0.5)

    def norm_part(xv, xk, rows, subs, sc0, junk_pm=False):
        n = len(subs)
        if junk_pm:
            junk = Pm[0:rows, 0:2, :].rearrange("p a c -> p (a c)")
            jk = [("Pm", 0), ("Pm", 1)]
        else:
            junk = lg[0:rows, :, :].rearrange("p a c -> p (a c)")
            jk = [("lg", 0), ("lg", 1)]
        for i, s in enumerate(subs):
            act(junk, xv[i], AF.Square, [xk[i]], jk + [("ssq", sc0 + i)], accum=stat[0:rows, sc0 + i:sc0 + i + 1])
        rstd_small(stat[0:rows, 8 + sc0:8 + sc0 + n], stat[0:rows, sc0:sc0 + n], 1.0 / D, rows,
                   [("ssq", sc0 + i) for i in range(n)], ("rstd", sc0))
        for i, s in enumerate(subs):
            ts(xn[0:rows, s, :], xv[i], stat[0:rows, 8 + sc0 + i:9 + sc0 + i], None, ALU.mult, None, [xk[i], ("rstd", sc0)], [("xn", s)])

    def transp_part(rows, nsub, kG, kS, sample):
        T = rows * nsub
        mk = ["modc0", "modc1", "modc2", "modc3"]
        for cpair in range(4):
            tbb, tk = next_tb()
            for half in range(2):
                c = 2 * cpair + half
                tb = tbb[:, half * 512:(half + 1) * 512]
                for s in range(nsub):
                    trn(tb[:, s * rows:(s + 1) * rows], xn[0:rows, s, c * 128:(c + 1) * 128], identb[0:rows, 0:rows], [("xn", s), "identb"], [tk])
            for half in range(2):
                c = 2 * cpair + half
                tb = tbb[:, half * 512:(half + 1) * 512]
                if not sample:
                    act(hT[:, c, 0:T], tb[:, 0:T], AF.Identity, [tk] + mk, [("hT", c)],
                        scale=modc[:, kG, c, 0:1], bias=modc[:, kS, c, 0:1])
                else:
                    for q in range(2):
                        act(hT[:, c, q * 32:(q + 1) * 32], tb[:, q * 32:(q + 1) * 32], AF.Identity, [tk] + mk, [("hT", c)],
                            scale=modc[:, kG, c, 1 + q:2 + q], bias=modc[:, kS, c, 1 + q:2 + q])

    def phase_norm(xviews, xkeys, rows, nsub, kG, kS, sample):
        if not sample:
            warm(WARM_NORM, 7)
        norm_part(xviews, xkeys, rows, list(range(nsub)), 0)
        transp_part(rows, nsub, kG, kS, sample)

    def hT_keys(sample):
        return [("hT", c) for c in range(8)]

    def qknorm(acc, ak, rows, goff, dest, dk, par=0):
        e0, e1 = 2 * par, 2 * par + 1
        k0, k1 = f"et{e0}", f"et{e1}"
        sc = 16 + 8 * par
        rk = ("r8", par)
        sq = et[0:rows, e0, :]
        act(sq, acc[0:rows, :], AF.Square, [ak], [k0])
        P.op("dve", lambda e: e.tensor_reduce(out=stat[0:rows, sc:sc + 8], in_=sq.rearrange("p (g j) -> p g j", g=8), axis=AX.X, op=ALU.add),
             reads=[k0], writes=[rk])
        rstd_small(stat[0:rows, sc:sc + 8], stat[0:rows, sc:sc + 8], 1.0 / 64, rows, [rk], rk)
        t1 = et[0:rows, e1, :]
        tt(t1.rearrange("p (g j) -> p g j", g=8), acc[0:rows, :].rearrange("p (g j) -> p g j", g=8),
           fr(stat[0:rows, sc:sc + 1], [[1, 8], [0, 64]]), ALU.mult, [ak, rk], [k1])
        tt(dest, t1, gfull[0:rows, goff // 64, :], ALU.mult, [k1, "gfull"], [dk])

    def rope(acc, ak, rows, tabC, tabS, tkey, dest, dk, par=0):
        e0, e1 = 2 * par, 2 * par + 1
        k0, k1 = f"et{e0}", f"et{e1}"
        xe = acc[0:rows, 0:512:2]
        xo_ = acc[0:rows, 1:512:2]
        t1 = et[0:rows, e0, 0:256]
        t2 = et[0:rows, e0, 256:512]
        t3 = et[0:rows, e1, 0:256]
        t4 = et[0:rows, e1, 256:512]
        tt(t1, xe, tabC, ALU.mult, [ak, tkey], [k0])
        tt(t2, xo_, tabS, ALU.mult, [ak, tkey], [k0])
        tt(dest[:, 0:512:2], t1, t2, ALU.subtract, [k0], [dk])
        tt(t3, xe, tabS, ALU.mult, [ak, tkey], [k1])
        tt(t4, xo_, tabC, ALU.mult, [ak, tkey], [k1])
        tt(dest[:, 1:512:2], t3, t4, ALU.add, [k1], [dk])

    def transp_heads(src, skeys, rows, nsub, dest, dname, split=False):
        T = rows * nsub
        for hp in range(2):
            tbb, tk = next_tb()
            for half in range(2):
                h = 2 * hp + half
                tb = tbb[:, half * 512:(half + 1) * 512]
                for s in range(nsub):
                    trn(tb[:, s * rows:(s + 1) * rows], src[0:rows, s, h * 128:(h + 1) * 128], identb[0:rows, 0:rows], list(skeys(s)) + ["identb"], [tk])
            for half in range(2):
                h = 2 * hp + half
                if split:
                    act(dest[0:64, 0, h, 0:T], tbb[0:64, half * 512:half * 512 + T], AF.Copy, [tk], [(dname, h)])
                    act(dest[64:128, 1, h, 0:T], tbb[64:128, half * 512:half * 512 + T], AF.Copy, [tk], [(dname, h)])
                else:
                    act(dest[:, h, 0:T], tbb[:, half * 512:half * 512 + T], AF.Copy, [tk], [(dname, h)])

    def in_proj(rows, nsub, own, sample, tabsrc, tile_i, hook_after_first=None, next_w=None):
        T = rows * nsub
        hk_ = hT_keys(sample)
        cbs = (1, 2, 0, 4, 5, 3, 6) if own else (1, 2, 4, 5)
        pending = [None]

        def flush():
            if pending[0] is not None:
                pending[0]()
                pending[0] = None
        for ci, cb in enumerate(cbs):
            W, wkey = get_w(("Wi", cb))
            if ci + 1 < len(cbs):
                prefetch(("Wi", cbs[ci + 1]))
            elif next_w is not None:
                prefetch(next_w)
            if cb == 6:
                for e in range(4):
                    acc, ak = next_acc()
                    for k in range(8):
                        mm(acc[:, 0:T], W[:, k, e * 128:(e + 1) * 128], hT[:, k, 0:T], k == 0, k == 7, [wkey] + hk_, [ak])
                    act(sgT[:, e, 0:T], acc[:, 0:T], AF.Silu, [ak], [("sgT", e)])
                flush()
                continue
            for s in range(nsub):
                acc, ak = next_acc()
                for k in range(8):
                    mm(acc[0:rows, :], hT[:, k, s * rows:(s + 1) * rows], W[:, k, :], k == 0, k == 7, [wkey] + hk_, [ak])
                chk(31)
                if cb == 0:
                    qknorm(acc, ak, rows, 0, tokb[0:rows, s, :], ("tokb", s), s % 2)
                elif cb == 1:
                    if own:
                        kfv = kf[0:rows, 0, :]
                        qknorm(acc, ak, rows, 64, kfv, "kf", s % 2)
                        if sample:
                            out_ops.append(stq(ksm[:, :], kfv, r=["kf"]))
                        else:
                            out_ops.append(stq(kp[tile_i, s * 128:(s + 1) * 128, :], kfv, r=["kf"]))
                        act(tokb[0:rows, s, :], kfv, AF.Copy, ["kf"], [("tokb", s)])
                    else:
                        qknorm(acc, ak, rows, 64, tokb[0:rows, s, :], ("tokb", s), s % 2)
                elif cb == 2:
                    if own:
                        vfv = vf[0:rows, 0, :]
                        act(vfv, acc[0:rows, :], AF.Copy, [ak], ["vf"])
                        if sample:
                            out_ops.append(stq(vsm[:, :], vfv, r=["vf"]))
                        else:
                            out_ops.append(stq(vp[tile_i, s * 128:(s + 1) * 128, :], vfv, r=["vf"]))
                        cp(vst[0:rows, s, :], vfv, ["vf"], [("vst", s)])
                    else:
                        evac(vst[0:rows, s, :], acc[0:rows, :], [ak], [("vst", s)])
                elif cb in (3, 4):
                    rs_ = s % 2
                    ld(rt[0:rows, rs_], tabsrc(s), w=[("rt", rs_)])
                    if cb == 3:
                        rope(acc, ak, rows, rt[0:rows, rs_, 0, :], rt[0:rows, rs_, 1, :], ("rt", rs_), tokb[0:rows, s, :], ("tokb", s), s % 2)
                    else:
                        rope(acc, ak, rows, rt[0:rows, rs_, 2, :], rt[0:rows, rs_, 3, :], ("rt", rs_), ke[0:rows, s, :], ("ke", s), s % 2)
                elif cb == 5:
                    evac(vR[0:rows, s, :], acc[0:rows, :], [ak], [("vR", s)])
            if hook_after_first is not None and cb == cbs[0]:
                hook_after_first()
            flush()
            if cb == 0:
                pending[0] = lambda: transp_heads(tokb, lambda s: [("tokb", s)], rows, nsub, qAT, "qAT", split=True)
            elif cb == 1:
                pending[0] = lambda: transp_heads(tokb, lambda s: [("tokb", s)], rows, nsub, kst, "kst")
            elif cb == 3:
                pending[0] = lambda: transp_heads(tokb, lambda s: [("tokb", s)], rows, nsub, qdT, "qdT")
            elif cb == 4 and own:
                pending[0] = lambda: transp_heads(ke, lambda s: [("ke", s)], rows, nsub, keT, "keT")
        flush()

    def store_kv(unit):
        stq(KT[:, unit].rearrange("h p c -> p h c"), kst[:, :, :], r=[("kst", h) for h in range(4)], w=[("KT", unit)])
        for s in range(4):
            dst = VS[:, unit, :, s, :].rearrange("h p d -> p h d")
            stq(dst, vst[:, s, :].rearrange("p (h d) -> p h d", h=4), r=[("vst", s)], w=[("VS", unit, s)])

    def u_raw(rows, nsub, pbase, pu, pk):
        for h in range(4):
            for s in range(nsub):
                mm(pu[:, h * 128:(h + 1) * 128], ke[pbase:pbase + rows, s, h * 128:(h + 1) * 128], vR[pbase:pbase + rows, s, h * 128:(h + 1) * 128],
                   s == 0, s == nsub - 1, [("ke", s), ("vR", s)], [pk])

    def retention_prompt():
        def scores(h):
            for j in range(4):
                pj, pk = next_acc()
                n = 512 - 128 * j
                mm(pj[:, 0:n], keT[:, h, 128 * j:128 * j + 128], qdT[:, h, 128 * j:512], True, True, [("keT", h), ("qdT", h)], [pk])
                tt(scb[:, j, 0:128], pj[:, 0:128], trib[:, :], ALU.mult, [pk, "trib"], [("Pm", j)])
                if n > 128:
                    cp(scb[:, j, 128:n], pj[:, 128:n], [pk], [("Pm", j)])

        def av(h, po, pok):
            mm(po[:, :], Sb[:, h, :], qdT[:, h, :], True, False, ["Sb", ("qdT", h)], [pok])
            for j in range(4):
                n = 512 - 128 * j
                mm(po[:, 128 * j:512], vR[:, j, h * 128:(h + 1) * 128], scb[:, j, 0:n], False, j == 3,
                   [("vR", j), ("Pm", j)], [pok])

        scores(0)
        for h in range(4):
            po, pok = psb[4 + h % 2], PS(4 + h % 2)
            av(h, po, pok)
            if h + 1 < 4:
                scores(h + 1)
            ret_epilogue(po, pok, h, 512)

    def ret_epilogue(po, pok, h, T):
        sq = et[:, 2, 0:T]
        act(sqb[:, 0:T], po[:, 0:T], AF.Square, [pok], ["sqb"])
        pss, psk = psb[7], PS(7)
        mm(pss[:, 0:T], onesb[:, :], sqb[:, 0:T], True, True, ["onesb", "sqb"], [psk])
        rs_ = et[:, 3, 0:T]
        act(rs_, pss[:, 0:T], AF.Ln, [psk, "epsc"], ["et3"], scale=1.0 / 128, bias=cols[:, EPSC:EPSC + 1])
        act(rs_, rs_, AF.Exp, ["et3"], ["et3"], scale=-0.5)
        tt(sq, po[:, 0:T], rs_, ALU.mult, [pok, "et3", "et2"], ["et2"])
        tt(oT[:, 4 + h, 0:T], sq, sgT[:, h, 0:T], ALU.mult, ["et2", ("sgT", h)], [("oT", 4 + h)])

    kv_rr = [0]

    def attention(qc0, N, units, h, first_full=True):
        pO = [psb[4], psb[5]]
        pOk = [PS(4), PS(5)]
        pS = [psb[6], psb[7]]
        pSk = [PS(6), PS(7)]
        tiles = []
        for u in units:
            for (kt, c0, kind, arg) in u["tiles"]:
                tiles.append((u, kt, c0, kind, arg))
        n_tiles = len(tiles)
        info = {}

        def unit_ops(u):
            if id(u) in info:
                return info[id(u)]
            if u.get("sbuf"):
                ktv, vsv = u["ktv"], u["vsv"]
                r = (lambda m, kt: ktv, lambda kt: vsv, u["kkeys"], u["vkeys"], u["pbase"], u["nkeys"])
            else:
                slot = kv_rr[0] % NKB
                kv_rr[0] += 1
                ld(ktb[:, slot, :], u["kt_src"], r=u["kt_keys"], w=[("ktb", slot)])
                ld(vsb[:, slot, :].rearrange("p (k d) -> p k d", k=4), u["vs_src"], r=u["vs_keys"], w=[("vsb", slot)])
                r = (lambda m, kt: ktb[:, slot, kt * 128:(kt + 1) * 128],
                     lambda kt: vsb[:, slot, kt * 128:(kt + 1) * 128], [("ktb", slot)], [("vsb", slot)], 0, 128)
            info[id(u)] = r
            return r

        def emit_qk(t):
            u, kt, c0, kind, arg = tiles[t]
            lk, lv, kkeys, vkeys, pb, nkeys = unit_ops(u)
            pr = slice(pb, pb + nkeys)
            for m in range(2):
                bi = 2 * (t % 2) + m
                mm(psb[bi][pr, c0:N], lk(m, kt), qAT[:, m, h, qc0 + c0:qc0 + N], True, True, kkeys + [("qAT", h)], [PS(bi)])

        def emit_exp(t):
            u, kt, c0, kind, arg = tiles[t]
            lk, lv, kkeys, vkeys, pb, nkeys = unit_ops(u)
            pr = slice(pb, pb + nkeys)
            n = N - c0
            for m in range(2):
                bi = 2 * (t % 2) + m
                pq, pqk = psb[bi], PS(bi)
                pm = Pm[pr, bi, :]
                pmk = ("Pm", bi)
                if kind == "const":
                    if m == 0:
                        b0 = 2 * (t % 2)
                        act(Pm[pr, b0:b0 + 2, c0:N], psall[pr, b0:b0 + 2, c0:N], AF.Exp, [PS(b0), PS(b0 + 1), "c15", "cc0", "cc4", "ccf"],
                            [("Pm", b0), ("Pm", b0 + 1)], scale=0.125, bias=arg[pr, :])
                else:
                    btile, w_, ccol = arg
                    w_ = min(w_, n)
                    lgv = lg[pr, m, 0:w_]
                    stt(lgv, pq[pr, c0:c0 + w_], 0.125, btile[:, 0:w_], ALU.mult, ALU.add, [pqk] + u.get("bkeys", []), [("lg", m)])
                    act(pm[:, c0:c0 + w_], lgv, AF.Exp, [("lg", m)], [pmk])
                    if w_ < n:
                        act(pm[:, c0 + w_:N], pq[pr, c0 + w_:N], AF.Exp, [pqk, "c15", "cc0", "cc4", "ccf"], [pmk], scale=0.125, bias=ccol[pr, :])

        def emit_av(t):
            u, kt, c0, kind, arg = tiles[t]
            lk, lv, kkeys, vkeys, pb, nkeys = unit_ops(u)
            pr = slice(pb, pb + nkeys)
            first, last = (t == 0), (t == n_tiles - 1)
            for m in range(2):
                bi = 2 * (t % 2) + m
                mm(pO[m][:, c0:N], lv(kt), Pm[pr, bi, c0:N], first, last, vkeys + [("Pm", bi)], [pOk[m]])
            for m in range(2):
                bi = 2 * (t % 2) + m
                mm(pS[m][:, c0:N], onesb[pr, :], Pm[pr, bi, c0:N], first, last, ["onesb", ("Pm", bi)], [pSk[m]])

        emit_qk(0)
        for t in range(n_tiles):
            emit_exp(t)
            if t + 1 < n_tiles:
                emit_qk(t + 1)
            emit_av(t)
        T = N
        rinv = lg[:, :, 0:T]
        act(rinv, psall[:, 6:8, 0:T], AF.Ln, [pSk[0], pSk[1]], [("lg", 0), ("lg", 1)])
        act(rinv, rinv, AF.Exp, [("lg", 0), ("lg", 1)], [("lg", 0), ("lg", 1)], scale=-1.0)
        r0 = lg[:, 0, 0:T]
        r1 = lg[:, 1, 0:T]
        a_ = et[:, 1, 0:T]
        b_ = et[:, 2, 0:T]
        tt(a_, pO[0][:, 0:T], r0, ALU.mult, [pOk[0], ("lg", 0)], ["et1"])
        tt(b_, pO[1][:, 0:T], r1, ALU.mult, [pOk[1], ("lg", 1)], ["et2"])
        stt(a_, b_, cols[:, NLAM:NLAM + 1], a_, ALU.mult, ALU.add, ["et1", "et2", "nlam"], ["et1"])
        act(sqb[:, 0:T], a_, AF.Square, ["et1", "et2"], ["sqb"])
        if N == 512:
            warm(WARM_EPI, 3)
        pss, psk = psb[0], PS(0)
        mm(pss[:, 0:T], onesb[:, :], sqb[:, 0:T], True, True, ["onesb", "sqb"], [psk])
        rs_ = et[:, 3, 0:T]
        act(rs_, pss[:, 0:T], AF.Ln, [psk, "epsc"], ["et3"], scale=1.0 / 128, bias=cols[:, EPSC:EPSC + 1])
        act(rs_, rs_, AF.Exp, ["et3"], ["et3"], scale=-0.5)
        stt(oT[:, h, qc0:qc0 + T], a_, cols[:, GSUB:GSUB + 1], rs_, ALU.mult, ALU.mult, ["et1", "et3", "gsubc"], [("oT", h)])

    def prompt_units(step, h):
        units = []
        c15c = cols[:, C15 + h:C15 + h + 1]
        cc0c = cols[:, CC0 + h:CC0 + h + 1]
        cc4c = cols[:, CC4 + h:CC4 + h + 1]
        ccfc = cols[:, CCF + h:CCF + h + 1]
        for u in range(2 * step + 2):
            d = dict(kt_src=KT[h, u], vs_src=VS[h, u], kt_keys=[("KT", u)], vs_keys=[("VS", u, s) for s in range(4)])
            if u == 2 * step + 1:
                d["tiles"] = [(kt, 128 * kt, "bias", (Mh[:, h, :], 256, c15c)) for kt in range(4)]
                d["bkeys"] = [("Mh", h)]
            elif u == 2 * step:
                d["tiles"] = [(kt, 0, "const", ccfc) for kt in range(3)] + [(3, 0, "bias", (G4[:, h, :], 128, cc4c))]
                d["bkeys"] = [("G4", h)]
            elif u == 2 * step - 2:
                d["tiles"] = [(kt, 0, "const", c15c) for kt in range(3)] + [(3, 0, "bias", (G0[:, h, :], 128, cc0c))]
                d["bkeys"] = [("G0", h)]
            else:
                d["tiles"] = [(kt, 0, "const", c15c) for kt in range(4)]
            units.append(d)
        return units

    def out_proj(rows, nsub, xviews, xkeys):
        okeys = [("oT", i) for i in range(8)]
        for cb2 in range(2):
            W, wkey = get_w(("Wo", cb2))
            for s in range(nsub):
                acc, ak = next_acc()
                for k in range(8):
                    mm(acc[0:rows, :], oT[:, k, s * rows:(s + 1) * rows], W[:, k, :], k == 0, k == 7, [wkey] + okeys, [ak])
                t = et[0:rows, (cb2 * nsub + s) % 2, :]
                tkey = f"et{(cb2 * nsub + s) % 2}"
                tt(t, acc[0:rows, :], gate1b[0:rows, cb2 * 512:(cb2 + 1) * 512], ALU.mult, [ak, ("gate", 0, cb2)], [tkey])
                xv = xviews[s][:, cb2 * 512:(cb2 + 1) * 512]
                tt(xv, t, xv, ALU.add, [tkey, xkeys[s]], [xkeys[s]])

    def ffn(rows, nsub, xviews, xkeys, sample, store, gb_hooks=None, mid_hook=None, next_w=None):
        T = rows * nsub
        hk_ = hT_keys(sample)
        for gb in range(11):
            W, wkey = get_w(("Wgu", gb))
            prefetch(("Wgu", gb + 1) if gb + 1 < 11 else ("Wd", 0, 0))
            if gb_hooks is not None and gb in gb_hooks:
                gb_hooks[gb]()
            for fc in range(2):
                pg, pgk = next_acc()
                pu, puk = next_acc()
                for k in range(8):
                    mm(pg[:, 0:T], W[:, k, fc * 128:(fc + 1) * 128], hT[:, k, 0:T], k == 0, k == 7, [wkey] + hk_, [pgk])
                for k in range(8):
                    mm(pu[:, 0:T], W[:, k, 256 + fc * 128:256 + (fc + 1) * 128], hT[:, k, 0:T], k == 0, k == 7, [wkey] + hk_, [puk])
                sgi = (2 * gb + fc) % 2
                sg = lg[:, sgi, 0:T]
                act(sg, pg[:, 0:T], AF.Silu, [pgk], [("lg", sgi)])
                tt(ffT[:, 2 * gb + fc, 0:T], pu[:, 0:T], sg, ALU.mult, [puk, ("lg", sgi)], [("ffT", 2 * gb + fc)])
        fkeys = [("ffT", i) for i in range(NKF)]
        if not sample:
            chk(10)
        if mid_hook is not None:
            mid_hook()
        for half in range(2):
            accs = [next_acc() for s in range(nsub)]
            for bi, (k0, k1) in enumerate(((0, 8), (8, 16), (16, 22))):
                W, wkey = get_w(("Wd", half, bi))
                nxt = ("Wd", half, bi + 1) if bi < 2 else (("Wd", 1, 0) if half == 0 else next_w)
                if nxt is not None:
                    prefetch(nxt)
                for s in range(nsub):
                    acc, ak = accs[s]
                    for kk in range(k0, k1):
                        mm(acc[0:rows, :], ffT[:, kk, s * rows:(s + 1) * rows], W[:, kk - k0, :], kk == 0, kk == NKF - 1, [wkey] + fkeys, [ak])
            for s in range(nsub):
                acc, ak = accs[s]
                t = et[0:rows, s % 2, :]
                tkey = f"et{s % 2}"
                tt(t, acc[0:rows, :], gate2b[0:rows, half * 512:(half + 1) * 512], ALU.mult, [ak, ("gate", 1, half)], [tkey])
                xv = xviews[s][:, half * 512:(half + 1) * 512]
                tt(xv, t, xv, ALU.add, [tkey, xkeys[s]], [xkeys[s]])
                if half == 1:
                    store(s)

    setup()
    prepass_in()
    gates(sample=False)

    def sample_cache_chunk(q, u):
        for h in range(4):
            src = cv[q, u * 512:(u + 1) * 512, h, :].rearrange("(kt p) d -> p kt d", p=128)
            stq(VSs[q, h, u], src, w=[("VSs", q, h, u)])
        kcb = vst
        stq(kcb[:, :, :], ck[q, u * 512:(u + 1) * 512, :].rearrange("(kt p) c -> p kt c", p=128), w=[("vst", s) for s in range(4)])
        for hp in range(2):
            tbb, tk = next_tb()
            for half in range(2):
                h = 2 * hp + half
                for kt in range(4):
                    trn(tbb[:, half * 512 + kt * 128:half * 512 + (kt + 1) * 128], kcb[:, kt, h * 128:(h + 1) * 128], identb[:, :], [("vst", kt), "identb"], [tk])
            for half in range(2):
                h = 2 * hp + half
                act(kst[:, h, :], tbb[:, half * 512:(half + 1) * 512], AF.Copy, [tk], [("kst", h)])
        stq(KTs[q, :, u].rearrange("h p c -> p h c"), kst[:, :, :], r=[("kst", h) for h in range(4)], w=[("KTs", q, u)])

    sample_chunks = [(q, u) for q in range(2) for u in range(4)]

    try:
        def foreign_sub(slot, s):
            ld(xf[:, 0, :], xs[slot, s * 128:(s + 1) * 128, :], w=[("xf", 0)])
            norm_part([xf[:, 0, :]], [("xf", 0)], 128, [s], 24 + s, junk_pm=True)

        def foreign_part1(slot):
            for s in range(4):
                foreign_sub(slot, s)

        def foreign_part2():
            transp_part(128, 4, 1, 0, False)

        if n_steps > 0:
            foreign_part1(0)
            foreign_part2()
        for step in range(n_steps):
            slotF, slotO = 2 * step, 2 * step + 1
            xviews = [xo[:, s, :] for s in range(4)]
            xkeys = [("xo", s) for s in range(4)]
            def own_norm(xviews=xviews, xkeys=xkeys, slotO=slotO):
                for s in range(4):
                    ld(xviews[s], xs[slotO, s * 128:(s + 1) * 128, :], w=[xkeys[s]])
                norm_part(xviews, xkeys, 128, [0, 1, 2, 3], 0)
            chk(1)
            in_proj(128, 4, False, False, lambda s, slot=slotF: rope_d[slot, s * 128:(s + 1) * 128], step, hook_after_first=own_norm, next_w=("Wi", 1))
            chk(2)
            store_kv(slotF)
            pu, pk = psb[7], PS(7)
            u_raw(128, 4, 0, pu, pk)
            cp(Uf[:].rearrange("p h e -> p (h e)"), pu[:, :], [pk], ["Uf"])
            for h in range(4):
                ts(tmpS[:, h, :], S[:, h, :], dco[:, h:h + 1], None, ALU.mult, None, ["S", "dco"], [("tmpS", h)])
                stt(Sst[:, h, :], Uf[:, h, :], dco[:, 4 + h:5 + h], tmpS[:, h, :], ALU.mult, ALU.add, ["Uf", "dco", ("tmpS", h)], [("Sst", h)])
            cp(Sb[:].rearrange("p h e -> p (h e)"), Sst[:].rearrange("p h e -> p (h e)"), [("Sst", h) for h in range(4)], ["Sb"])
            chk(3)
            transp_part(128, 4, 1, 0, False)
            chk(4)
            in_proj(128, 4, True, False, lambda s, slot=slotO: rope_d[slot, s * 128:(s + 1) * 128], step)
            chk(5)
            store_kv(slotO)
            if step == 0:
                prepass_rest()
            if do_sample and sample_chunks and (step >= 1 or n_steps == 1):
                for _ in range(8 if n_steps == 1 else (2 if len(sample_chunks) > 8 - step else 1)):
                    if sample_chunks:
                        sample_cache_chunk(*sample_chunks.pop(0))
            retention_prompt()
            pu, pk = psb[7], PS(7)
            u_raw(128, 4, 0, pu, pk)
            for h in range(4):
                ts(tmpS[:, h, :], S[:, h, :], dco[:, 8 + h:9 + h], None, ALU.mult, None, ["S", "dco"], [("tmpS", h)])
                stt(tmpS[:, h, :], Uf[:, h, :], dco[:, 16 + h:17 + h], tmpS[:, h, :], ALU.mult, ALU.add, ["Uf", "dco", ("tmpS", h)], [("tmpS", h)])
                stt(S[:, h, :], pu[:, h * 128:(h + 1) * 128], dco[:, 12 + h:13 + h], tmpS[:, h, :], ALU.mult, ALU.add, [pk, "dco", ("tmpS", h)], ["S"])
            chk(6)
            prefetch(("Wo", 0))
            prefetch(("Wo", 1))
            prefetch(("Wgu", 0))
            for h in range(4):
                attention(0, 512, prompt_units(step, h), h)
            chk(7)
            out_proj(128, 4, xviews, xkeys)
            chk(8)
            phase_norm(xviews, xkeys, 128, 4, 3, 2, False)
            chk(9)

            def store(s, step=step, xviews=xviews, xkeys=xkeys):
                out_ops.append(stq(y[step, s * 128:(s + 1) * 128, :], xviews[s], r=[xkeys[s]]))
            if step + 1 < n_steps:
                nslot = 2 * (step + 1)
                ffn(128, 4, xviews, xkeys, False, store,
                    gb_hooks={1 + 2 * s_: (lambda nslot=nslot, s_=s_: foreign_sub(nslot, s_)) for s_ in range(4)}, mid_hook=foreign_part2,
                    next_w=("Wi", 1))
            else:
                ffn(128, 4, xviews, xkeys, False, store)
        out_ops.append(stq(rp[:].rearrange("h d e -> d h e"), S[:, :, :], r=["S"]))

        if do_sample:
            while sample_chunks:
                sample_cache_chunk(*sample_chunks.pop(0))
            gates(sample=True)
            xviews = [xo[0:64, 0, :]]
            xkeys = [("xo", 0)]
            ld(xviews[0], xsm[:, :], w=[xkeys[0]])
            phase_norm(xviews, xkeys, 64, 1, 1, 0, True)
            in_proj(64, 1, True, True, lambda s: rope_sd[:, :, :], 0)
            SH = [Sst, tmpS]
            SHk = [[("Sst", h) for h in range(4)], [("tmpS", h) for h in range(4)]]
            for q in range(2):
                ld(SH[q][:, :, :], st[q].rearrange("h d e -> d h e"), w=SHk[q])
            for q in range(2):
                shb = Sb
                cp(shb[:].rearrange("p h e -> p (h e)"), SH[q][:].rearrange("p h e -> p (h e)"), SHk[q], ["Sb"])
                pr = slice(32 * q, 32 * q + 32)
                for h in range(4):
                    pj, pk = next_acc()
                    mm(pj[pr, 0:32], keT[:, h, 32 * q:32 * q + 32], qdT[:, h, 32 * q:32 * q + 32], True, True, [("keT", h), ("qdT", h)], [pk])
                    tt(scb[pr, 0, 0:32], pj[pr, 0:32], trib[pr, 32 * q:32 * q + 32], ALU.mult, [pk, "trib"], [("Pm", 0)])
                    po, pok = psb[6], PS(6)
                    mm(po[:, 0:32], shb[:, h, :], qdT[:, h, 32 * q:32 * q + 32], True, False, ["Sb", ("qdT", h)], [pok])
                    mm(po[:, 0:32], vR[pr, 0, h * 128:(h + 1) * 128], scb[pr, 0, 0:32], False, True, [("vR", 0), ("Pm", 0)], [pok])
                    sq = et[:, 2, 0:32]
                    act(sq, po[:, 0:32], AF.Square, [pok], ["et2"])
                    pss, psk = psb[7], PS(7)
                    mm(pss[:, 0:32], onesf[:, :], sq, True, True, ["onesf", "et2"], [psk])
                    rs_ = et[:, 3, 0:32]
                    act(rs_, pss[:, 0:32], AF.Ln, [psk, "epsc"], ["et3"], scale=1.0 / 128, bias=cols[:, EPSC:EPSC + 1])
                    act(rs_, rs_, AF.Exp, ["et3"], ["et3"], scale=-0.5)
                    tt(sq, po[:, 0:32], rs_, ALU.mult, [pok, "et3", "et2"], ["et2"])
                    tt(oT[:, 4 + h, 32 * q:32 * q + 32], sq, sgT[:, h, 32 * q:32 * q + 32], ALU.mult, ["et2", ("sgT", h)], [("oT", 4 + h)])
                pu, pk = psb[7], PS(7)
                u_raw(32, 1, 32 * q, pu, pk)
                for h in range(4):
                    tt(Uf[:, h, :], pu[:, h * 128:(h + 1) * 128], SH[q][:, h, :], ALU.add, [pk] + SHk[q], ["Uf"])
                    ts(Uf[:, h, :], Uf[:, h, :], dco[:, 28 + h:29 + h], None, ALU.mult, None, ["Uf", "dco"], ["Uf"])
                out_ops.append(stq(rs[q].rearrange("h d e -> d h e"), Uf[:, :, :], r=["Uf"]))
            for q in range(2):
                for h in range(4):
                    c15c = cols[:, C15 + h:C15 + h + 1]
                    units = []
                    for u in range(4):
                        d = dict(kt_src=KTs[q, h, u], vs_src=VSs[q, h, u], kt_keys=[("KTs", q, u)], vs_keys=[("VSs", q, h, u)])
                        if u < 3:
                            d["tiles"] = [(kt, 0, "const", c15c) for kt in range(4)]
                        else:
                            d["tiles"] = [(kt, 0, "const", c15c) for kt in range(3)] + [(3, 0, "bias", (Mh[:, h, 128:160], 32, None))]
                            d["bkeys"] = [("Mh", h)]
                        units.append(d)
                    pr = slice(32 * q, 32 * q + 32)
                    units.append(dict(sbuf=True, ktv=kst[:, h, 32 * q:32 * q + 32], vsv=vst[pr, 0, h * 128:(h + 1) * 128],
                                      kkeys=[("kst", h)], vkeys=[("vst", 0)], pbase=32 * q, nkeys=32,
                                      tiles=[(0, 0, "bias", (MS[pr, h, :], 32, None))], bkeys=["MS0", "MS1"]))
                    attention(32 * q, 32, units, h)
            out_proj(64, 1, xviews, xkeys)
            phase_norm(xviews, xkeys, 64, 1, 3, 2, True)

            def store_s(s):
                out_ops.append(stq(ysm[:, :], xviews[0], r=[xkeys[0]]))
            ffn(64, 1, xviews, xkeys, True, store_s)


    except _StopBuild:
        pass
    P.emit(final_wait_ops=out_ops)
    return nc, P


_PROG_CACHE = {}


def kernel(x_prompt, x_sample, c_prompt, c_sample, cache_k, cache_v, state_ret, w_ada, b_ada,
           g_norm1, g_norm2, w_in, g_q, g_k, lam_q1, lam_k1, lam_q2, lam_k2, g_subln, w_out,
           w_ff_gate, w_ff_up, w_ff_down, rel_bias):
    f = lambda a: np.ascontiguousarray(np.asarray(a, dtype=np.float32))
    x_prompt, x_sample, c_prompt, c_sample = f(x_prompt), f(x_sample), f(c_prompt), f(c_sample)
    cache_k, cache_v, state_ret = f(cache_k), f(cache_v), f(state_ret)
    C = _static_consts()
    if "nc" not in _PROG_CACHE:
        _PROG_CACHE["nc"] = build_program()
    nc, P = _PROG_CACHE["nc"]
    shared = dict(
        w_ada=f(w_ada)[0], b_ada=f(b_ada), w_in=f(w_in)[0],
        gqk=np.concatenate([f(g_q), f(g_k)], axis=1),
        lamv=np.concatenate([f(lam_q1), f(lam_k1), f(lam_q2), f(lam_k2)], axis=1),
        gsub=f(g_subln), w_out=f(w_out)[0], wg=f(w_ff_gate)[0], wu=f(w_ff_up)[0], wd=f(w_ff_down)[0],
        rbt=f(rel_bias), rope_s=C["rope_s"], ohr=C["ohr"], oh15=C["oh15"], maskadd=C["maskadd"], kc=C["kc"],
    )
    in_maps = []
    for c in range(8):
        b, p = c // 2, c % 2
        order = []
        for i in range(NT // 2):
            order += [2 * i + 1 - p, 2 * i + p]
        xb = x_prompt[b].reshape(NT, TT, D)[order]
        m = dict(shared)
        m.update(
            xs=np.ascontiguousarray(xb),
            xsm=np.ascontiguousarray(x_sample[2 * c:2 * c + 2].reshape(64, D)),
            crow=np.ascontiguousarray(np.concatenate([c_prompt[b:b + 1], c_sample[2 * c:2 * c + 2], f(g_norm1), f(g_norm2)], axis=0)),
            ck=np.ascontiguousarray(cache_k[0, 2 * c:2 * c + 2].reshape(2, PAST, 512)),
            cv=np.ascontiguousarray(cache_v[0, 2 * c:2 * c + 2]),
            st=np.ascontiguousarray(state_ret[0, 2 * c:2 * c + 2]),
            rope=C["rope"][p], dco=C["dco"][p],
        )
        in_maps.append(m)
    res = run_bass_kernel_spmd(nc, in_maps, core_ids=list(range(8)))
    R = res.results
    y_prompt = np.zeros((4, SEQ, D), np.float32)
    k_prompt = np.zeros((1, 4, SEQ, 4, 2, 64), np.float32)
    v_prompt = np.zeros((1, 4, SEQ, 4, 128), np.float32)
    ret_prompt = np.zeros((1, 4, 4, 128, 128), np.float32)
    y_sample = np.zeros((16, LS, D), np.float32)
    k_sample = np.zeros((1, 16, LS, 4, 2, 64), np.float32)
    v_sample = np.zeros((1, 16, LS, 4, 128), np.float32)
    ret_sample = np.zeros((1, 16, 4, 128, 128), np.float32)
    for c in range(8):
        b, p = c // 2, c % 2
        r = R[c]
        for i in range(NT // 2):
            t = 2 * i + p
            y_prompt[b, t * TT:(t + 1) * TT] = r["y"][i]
            k_prompt[0, b, t * TT:(t + 1) * TT] = r["kp"][i].reshape(TT, 4, 2, 64)
            v_prompt[0, b, t * TT:(t + 1) * TT] = r["vp"][i].reshape(TT, 4, 128)
        if p == 0:
            ret_prompt[0, b] = r["rp"]
        y_sample[2 * c:2 * c + 2] = r["ysm"].reshape(2, LS, D)
        k_sample[0, 2 * c:2 * c + 2] = r["ksm"].reshape(2, LS, 4, 2, 64)
        v_sample[0, 2 * c:2 * c + 2] = r["vsm"].reshape(2, LS, 4, 128)
        ret_sample[0, 2 * c:2 * c + 2] = r["rs"]
    return (y_prompt, y_sample, k_prompt, v_prompt, ret_prompt, k_sample, v_sample, ret_sample)
```

```python
import math
import numpy as np
import concourse.bass as bass
import concourse.mybir as mybir
from concourse.bass_utils import run_bass_kernel_spmd

F32 = mybir.dt.float32
BF16 = mybir.dt.bfloat16
AF = mybir.ActivationFunctionType
ALU = mybir.AluOpType
AX = mybir.AxisListType
AP = bass.AP

D = 1024
DIN = 3584
DFF = 2816
NKF = DFF // 128
SEQ = 8192
TT = 512
NT = SEQ // TT
PAST = 2048
LS = 32
EPS = 1e-6
LAM_INIT = 0.8 - 0.6 * math.exp(-0.3 * 0)
NEG = -30000.0
WARM_EPI = 30
WARM_NORM = 44
ENGS = ("pe", "act", "dve", "pool", "sp")


class Op:
    __slots__ = ("eng", "fn", "deps", "is_dma", "signal", "val", "sem")

    def __init__(self, eng, fn, deps, is_dma):
        self.eng = eng
        self.fn = fn
        self.deps = deps
        self.is_dma = is_dma
        self.signal = is_dma
        self.val = None
        self.sem = None


class Prog:
    def __init__(self, nc, n_dma_sems=24):
        self.nc = nc
        self.ops = {e: [] for e in ENGS}
        self.all_ops = []
        self.bufs = {}
        self.n_dma_sems = n_dma_sems
        self.dma_rr = {"sp": 0, "pool": 0, "act": 0}
        self.dma_last = {}

    @staticmethod
    def _flat(keys):
        out = []
        for k in keys:
            if isinstance(k, list):
                out.extend(k)
            else:
                out.append(k)
        return out

    def _deps_for(self, reads, writes):
        deps = []
        for k in reads:
            ent = self.bufs.get(k)
            if ent is not None and ent[0] is not None:
                deps.append(ent[0])
        for k in writes:
            ent = self.bufs.get(k)
            if ent is not None:
                if ent[0] is not None:
                    deps.append(ent[0])
                deps.extend(ent[1])
        return deps

    def _commit(self, op, reads, writes):
        for k in reads:
            ent = self.bufs.setdefault(k, [None, []])
            if not op.is_dma:
                ent[1] = [r for r in ent[1] if r.is_dma or r.eng != op.eng]
            ent[1].append(op)
        for k in writes:
            self.bufs[k] = [op, []]

    def op(self, eng, fn, reads=(), writes=()):
        reads, writes = self._flat(reads), self._flat(writes)
        o = Op(eng, fn, self._deps_for(reads, writes), False)
        self.all_ops.append(o)
        self.ops[eng].append(o)
        self._commit(o, reads, writes)
        return o

    def dma(self, queue, fn, reads=(), writes=()):
        reads, writes = self._flat(reads), self._flat(writes)
        deps = self._deps_for(reads, writes)
        j = self.dma_rr[queue]
        nq = self.n_dma_sems if queue == "sp" else 4
        self.dma_rr[queue] = (j + 1) % nq
        prev = self.dma_last.get((queue, j))
        if prev is not None:
            deps.append(prev)
        o = Op(queue, fn, deps, True)
        o.sem = (queue, j)
        self.all_ops.append(o)
        self.ops[queue].append(o)
        self.dma_last[(queue, j)] = o
        self._commit(o, reads, writes)
        return o

    def emit(self, final_wait_ops=()):
        nc = self.nc
        for o in self.all_ops:
            for d in o.deps:
                if d.is_dma:
                    continue
                if d.eng == o.eng and d.eng == "pe" and not o.is_dma:
                    continue
                d.signal = True
        for o in final_wait_ops:
            if not o.is_dma:
                o.signal = True
        for e in ("pe", "act", "dve", "pool"):
            comp = [o for o in self.ops[e] if not o.is_dma]
            if comp:
                comp[-1].signal = True
        sem_h = {}
        for e in ("pe", "act", "dve", "pool"):
            sem_h[("eng", e)] = nc.alloc_semaphore(f"s_{e}")
        for q in ("sp", "pool", "act"):
            used = set(o.sem[1] for o in self.ops[q] if o.is_dma)
            for j in sorted(used):
                sem_h[(q, j)] = nc.alloc_semaphore(f"d_{q}{j}")
        cnt = {}
        for e in ENGS:
            for o in self.ops[e]:
                if o.is_dma:
                    k = o.sem
                    cnt[k] = cnt.get(k, 0) + 16
                    o.val = cnt[k]
                elif o.signal:
                    k = ("eng", e)
                    cnt[k] = cnt.get(k, 0) + 1
                    o.val = cnt[k]
                    o.sem = k
        self.stats = dict(n_ops={e: len(self.ops[e]) for e in ENGS}, max_sem=max(cnt.values()),
                          n_sig={e: cnt.get(("eng", e), 0) for e in ENGS}, n_wait={})
        prog = self

        def run_engine(e, h):
            seen = {}
            nwait = 0
            for o in prog.ops[e]:
                need = {}
                for d in o.deps:
                    if (not d.is_dma) and d.eng == e and e == "pe" and not o.is_dma:
                        continue
                    k = d.sem
                    if d.val > seen.get(k, 0) and d.val > need.get(k, 0):
                        need[k] = d.val
                for k, v in need.items():
                    h.wait_ge(sem_h[k], v)
                    seen[k] = v
                    nwait += 1
                ins = o.fn(h)
                if o.is_dma:
                    ins.then_inc(sem_h[o.sem], 16)
                elif o.signal:
                    ins.then_inc(sem_h[o.sem], 1)
            if e == "sp":
                for k, v in cnt.items():
                    h.wait_ge(sem_h[k], v)
            prog.stats["n_wait"][e] = nwait

        with nc.Block() as block:
            @block.tensor
            def _(h):
                run_engine("pe", h)

            @block.scalar
            def _(h):
                run_engine("act", h)

            @block.vector
            def _(h):
                run_engine("dve", h)

            @block.gpsimd
            def _(h):
                run_engine("pool", h)

            @block.sync
            def _(h):
                run_engine("sp", h)


def _t5_bucket_np(rel):
    import jax
    import jax.numpy as jnp
    cpu = jax.devices("cpu")[0]
    with jax.default_device(cpu):
        rel = jnp.asarray(rel, dtype=jnp.int32)
        nb = 16
        ret = jnp.where(rel > 0, nb, 0)
        n = jnp.abs(rel)
        max_exact = nb // 2
        large = max_exact + (jnp.log(jnp.maximum(n, 1).astype(jnp.float32) / max_exact)
                             / math.log(128 / max_exact) * (nb - max_exact)).astype(jnp.int32)
        large = jnp.minimum(large, nb - 1)
        out = ret + jnp.where(n < max_exact, n, large)
        return np.asarray(out)


_CONST_CACHE = {}


def _gammas():
    return (1.0 - 2.0 ** (-5.0 - np.arange(4, dtype=np.float64)))


def _rope_tables(pos, L):
    n = pos.shape[0]
    inv = (np.float32(1.0) / (np.float32(10000.0) ** np.linspace(0.0, 1.0, 64, dtype=np.float32))).astype(np.float32)
    ang = pos.astype(np.float32)[:, None] * inv[None, :]
    cos = np.cos(ang).astype(np.float64)
    sin = np.sin(ang).astype(np.float64)
    l = (np.arange(n) % L).astype(np.float64)
    lg = np.log(_gammas())
    qd = np.exp((l[:, None] + 1.0) * lg[None, :])
    kd = np.exp(-(l[:, None] + 1.0) * lg[None, :]) / math.sqrt(128.0)
    out = np.zeros((n, 4, 4, 64), np.float64)
    out[:, 0] = cos[:, None, :] * qd[:, :, None]
    out[:, 1] = sin[:, None, :] * qd[:, :, None]
    out[:, 2] = cos[:, None, :] * kd[:, :, None]
    out[:, 3] = sin[:, None, :] * kd[:, :, None]
    return out.reshape(n, 4, 256).astype(np.float32)


def _static_consts():
    if "c" in _CONST_CACHE:
        return _CONST_CACHE["c"]
    g = _gammas()
    rel = 127 - np.arange(384)
    bk = _t5_bucket_np(rel)
    ohr = np.zeros((32, 384), np.float32)
    ohr[bk, np.arange(384)] = 1.0
    oh15 = np.zeros((32, 128), np.float32)
    b_far = int(_t5_bucket_np(np.array([-200]))[0])
    oh15[b_far, :] = 1.0
    maskadd = np.zeros((128, 256), np.float32)
    k = np.arange(128)[:, None]
    c = np.arange(256)[None, :]
    maskadd[(k // 64) > (c // 64)] = NEG
    kc = np.zeros((128, 448), np.float32)
    kc[0, 0:128] = 1.0
    kc[32, 128:256] = 1.0
    m = np.arange(128)[:, None]
    l = np.arange(128)[None, :]
    kc[:, 256:384] = (l >= m).astype(np.float32)
    kc2 = np.zeros((128, 256), np.float32)
    kc2[np.arange(128), 127 - np.arange(128)] = 1.0
    kc2[np.arange(128), 128 + np.arange(128)] = 1.0
    kc = np.concatenate([kc[:, :384], kc2], axis=1)
    dco = np.zeros((2, 128, 32), np.float32)
    for p in range(2):
        a = np.ones(4) if p == 0 else g ** 512
        b = np.zeros(4) if p == 0 else g ** 512
        e = g ** 1024 if p == 0 else g ** 512
        f = g ** 512 if p == 0 else g ** 1024
        dco[p, :, 0:4] = a
        dco[p, :, 4:8] = b
        dco[p, :, 8:12] = g ** 1024
        dco[p, :, 12:16] = e
        dco[p, :, 16:20] = f
        sel = [1, 0, 0, 0, 0, NEG, 0, NEG] if p == 0 else [0, 1, 0, 1, 0, 0, 1, 0]
        dco[p, :, 20:28] = np.array(sel, np.float32)
        dco[p, :, 28:32] = g ** 32
    rope = np.zeros((2, NT, TT, 4, 256), np.float32)
    for p in range(2):
        for i in range(NT // 2):
            for j, tile in enumerate((2 * i + 1 - p, 2 * i + p)):
                pos = tile * TT + np.arange(TT)
                rope[p, 2 * i + j] = _rope_tables(pos, TT)
    rope_s = _rope_tables(PAST + (np.arange(64) % LS), LS)
    res = dict(ohr=ohr, oh15=oh15, maskadd=maskadd, kc=kc, dco=dco, rope=rope, rope_s=rope_s)
    _CONST_CACHE["c"] = res
    return res


class _StopBuild(Exception):
    pass


def build_program(n_steps=NT // 2, do_sample=True, cut=0):
    def chk(n):
        if cut == n:
            raise _StopBuild()
    nc = bass.Bass("TRN2", target_bir_lowering=False)
    P = Prog(nc)

    def din(name, shape, dt=F32):
        return nc.dram_tensor(name, list(shape), dt, kind="ExternalInput").ap()

    def dout(name, shape, dt=F32):
        return nc.dram_tensor(name, list(shape), dt, kind="ExternalOutput").ap()

    def dscr(name, shape, dt):
        return nc.dram_tensor(name, list(shape), dt, kind="Internal").ap()

    def sb(name, shape, dt):
        return nc.alloc_sbuf_tensor("s_" + name, list(shape), dt).ap()

    xs = din("xs", [NT, TT, D])
    xsm = din("xsm", [64, D])
    crow_d = din("crow", [5, D])
    ck = din("ck", [2, PAST, 512])
    cv = din("cv", [2, PAST, 4, 128])
    st = din("st", [2, 4, 128, 128])
    w_ada = din("w_ada", [D, 6 * D])
    b_ada = din("b_ada", [1, 6 * D])
    w_in = din("w_in", [D, DIN])
    gqk = din("gqk", [1, 128])
    lamv = din("lamv", [1, 256])
    gsub = din("gsub", [1, 128])
    w_out = din("w_out", [D, D])
    wg = din("wg", [D, DFF])
    wu = din("wu", [D, DFF])
    wd = din("wd", [DFF, D])
    rbt_d = din("rbt", [32, 4])
    rope_d = din("rope", [NT, TT, 4, 256])
    rope_sd = din("rope_s", [64, 4, 256])
    dco_d = din("dco", [128, 32])
    ohr_d = din("ohr", [32, 384])
    oh15_d = din("oh15", [32, 128])
    maskadd_d = din("maskadd", [128, 256])
    kc_d = din("kc", [128, 640])

    y = dout("y", [NT // 2, TT, D])
    ysm = dout("ysm", [64, D])
    kp = dout("kp", [NT // 2, TT, 512])
    vp = dout("vp", [NT // 2, TT, 512])
    rp = dout("rp", [4, 128, 128])
    ksm = dout("ksm", [64, 512])
    vsm = dout("vsm", [64, 512])
    rs = dout("rs", [2, 4, 128, 128])

    Wi = dscr("Wi", [7, 128, 8, 512], BF16)
    Wo = dscr("Wo", [2, 128, 8, 512], BF16)
    Wgu = dscr("Wgu", [11, 128, 8, 512], BF16)
    Wd = dscr("Wd", [2, 128, NKF, 512], BF16)
    KT = dscr("KT", [4, NT, 128, 512], BF16)
    VS = dscr("VS", [4, NT, 128, 4, 128], BF16)
    KTs = dscr("KTs", [2, 4, 4, 128, 512], BF16)
    VSs = dscr("VSs", [2, 4, 4, 128, 4, 128], BF16)
    urd = dscr("urd", [4, 384], F32)

    kc = sb("kc", [128, 640], F32)
    sel0 = kc[0:64, 0:128]
    sel1 = kc[0:64, 128:256]
    trif = kc[:, 256:384]
    Jf = kc[:, 384:512]
    identf = kc[:, 512:640]
    identb = sb("identb", [128, 128], BF16)
    trib = sb("trib", [128, 128], BF16)
    onesb = sb("onesb", [128, 128], BF16)
    onesf = sb("onesf", [128, 128], F32)
    dco = sb("dco", [128, 32], F32)
    Mh = sb("Mh", [128, 4, 256], F32)
    G0 = sb("G0", [128, 4, 128], F32)
    G4 = sb("G4", [128, 4, 128], F32)
    MS = sb("MS", [64, 4, 32], F32)
    cols = sb("cols", [128, 64], F32)
    C15, CC0, CC4, CCF, NLAM, GSUB, EPSC = 0, 4, 8, 12, 16, 17, 18
    modc = sb("modc", [128, 4, 8, 3], F32)
    gate1b = sb("gate1b", [128, D], F32)
    gate2b = sb("gate2b", [128, D], F32)
    gqkb = sb("gqkb", [128, 128], F32)
    cT = sb("cT", [128, 8, 5], F32)
    scT = sb("scT", [128, 8, 3], BF16)
    scB = sb("scB", [128, 8, 128], BF16)
    scS = sb("scS", [128, 8, 64], BF16)
    browb = sb("browb", [1, 1, 512], F32)
    smallr = sb("smallr", [32, 512], F32)
    xo = sb("xo", [128, 4, D], F32)
    xf = sb("xf", [128, 1, D], F32)
    crow5 = xf[0:5, 0, :]
    gfull = sb("gfull", [128, 2, 512], F32)
    hT = sb("hT", [128, 8, TT], BF16)
    rt = sb("rt", [128, 2, 4, 256], F32)
    tokb = sb("tokb", [128, 4, 512], BF16)
    qAT = sb("qAT", [128, 2, 4, TT], BF16)
    kst = sb("kst", [128, 4, TT], BF16)
    vst = sb("vst", [128, 4, 512], BF16)
    kf = sb("kf", [128, 1, 512], F32)
    vf = sb("vf", [128, 1, 512], F32)
    qdT = sb("qdT", [128, 4, TT], BF16)
    keT = sb("keT", [128, 4, TT], BF16)
    ke = sb("ke", [128, 4, 512], BF16)
    vR = sb("vR", [128, 4, 512], BF16)
    sgT = sb("sgT", [128, 4, TT], BF16)
    oT = sb("oT", [128, 8, TT], BF16)
    ffT = sb("ffT", [128, NKF, TT], BF16)
    NW = 3
    wring = sb("wring", [128, NW, 4096], BF16)
    NKB = 2
    ktb = sb("ktb", [128, NKB, 512], BF16)
    vsb = sb("vsb", [128, NKB, 512], BF16)
    Pm = sb("Pm", [128, 4, 512], BF16)
    scb = Pm
    lg = sb("lg", [128, 2, 512], F32)
    et = sb("et", [128, 4, 512], F32)
    xn = sb("xn", [128, 4, D], BF16)
    lamt = sb("lamt", [1, 128], F32)
    sqb = sb("sqb", [128, 512], BF16)
    warmb = sb("warmb", [128, 512], BF16)
    S = sb("S", [128, 4, 128], F32)
    Sst = sb("Sst", [128, 4, 128], F32)
    tmpS = sb("tmpS", [128, 4, 128], F32)
    Sb = sb("Sb", [128, 4, 128], BF16)
    Uf = sb("Uf", [128, 4, 128], F32)
    stat = sb("stat", [128, 64], F32)

    psall = nc.alloc_psum_tensor("psall", [128, 8, 512], F32).ap()
    psallb = psall.bitcast(BF16)
    psb = [psall[:, i, :] for i in range(8)]
    psbb = [psallb[:, i, :] for i in range(8)]

    def PS(i):
        return ("ps", i)

    def mm(out, lhsT, rhs, start, stop, r, w):
        return P.op("pe", lambda e: e.matmul(out, lhsT=lhsT, rhs=rhs, start=start, stop=stop), reads=r, writes=w)

    def warm(n, bank):
        for _ in range(n):
            mm(psb[bank][:, :], onesb[:, :], warmb[:, :], True, True, ["onesb", "warmb"], [PS(bank)])

    def trn(out, in_, ident, r, w):
        return P.op("pe", lambda e: e.transpose(out, in_, ident), reads=r, writes=w)

    def act(out, in_, func, r, w, scale=1.0, bias=0.0, accum=None):
        if accum is None:
            return P.op("act", lambda e: e.activation(out=out, in_=in_, func=func, bias=bias, scale=scale), reads=r, writes=w)
        return P.op("act", lambda e: e.activation(out=out, in_=in_, func=func, bias=bias, scale=scale, accum_out=accum), reads=r, writes=w)

    def tt(out, in0, in1, op, r, w, eng="dve"):
        return P.op(eng, lambda e: e.tensor_tensor(out=out, in0=in0, in1=in1, op=op), reads=r, writes=w)

    def ts(out, in0, s1, s2, op0, op1, r, w, eng="dve"):
        if op1 is None:
            return P.op(eng, lambda e: e.tensor_scalar(out=out, in0=in0, scalar1=s1, scalar2=None, op0=op0), reads=r, writes=w)
        return P.op(eng, lambda e: e.tensor_scalar(out=out, in0=in0, scalar1=s1, scalar2=s2, op0=op0, op1=op1), reads=r, writes=w)

    def stt(out, in0, scalar, in1, op0, op1, r, w):
        return P.op("dve", lambda e: e.scalar_tensor_tensor(out=out, in0=in0, scalar=scalar, in1=in1, op0=op0, op1=op1), reads=r, writes=w)

    def cp(out, in_, r, w, eng="dve"):
        return P.op(eng, lambda e: e.tensor_copy(out=out, in_=in_), reads=r, writes=w)

    def ld(out, in_, r=(), w=()):
        return P.dma("sp", lambda e: e.dma_start(out=out, in_=in_), reads=r, writes=w)

    def stq(out, in_, r=(), w=()):
        return P.dma("pool", lambda e: e.dma_start(out=out, in_=in_), reads=r, writes=w)

    ev_rr = [0]

    def evac(out, in_, r, w):
        ev_rr[0] ^= 1
        if ev_rr[0]:
            return act(out, in_, AF.Copy, r, w)
        return cp(out, in_, r, w)

    def fr(ap, dims):
        return AP(ap.tensor, ap.offset, [list(ap.ap[0])] + [list(d) for d in dims])

    out_ops = []

    def prepass_in():
        for cb in (1, 2, 4, 5, 0, 3, 6):
            src = w_in[:, cb * 512:(cb + 1) * 512].rearrange("(k p) c -> p k c", p=128)
            stq(Wi[cb], src, w=[("Wi", cb)])

    def prepass_rest():
        for cb in range(2):
            src = w_out[:, cb * 512:(cb + 1) * 512].rearrange("(k p) c -> p k c", p=128)
            stq(Wo[cb], src, w=[("Wo", cb)])
        for gb in range(11):
            stq(Wgu[gb][:, :, 0:256], wg[:, gb * 256:(gb + 1) * 256].rearrange("(k p) c -> p k c", p=128), w=[("Wgu", gb, 0)])
            stq(Wgu[gb][:, :, 256:512], wu[:, gb * 256:(gb + 1) * 256].rearrange("(k p) c -> p k c", p=128), w=[("Wgu", gb, 1)])
        for half in range(2):
            for bi, (k0, k1) in enumerate(((0, 8), (8, 16), (16, 22))):
                src = wd[k0 * 128:k1 * 128, half * 512:(half + 1) * 512].rearrange("(k p) c -> p k c", p=128)
                stq(Wd[half][:, k0:k1, :], src, w=[("Wd", half, bi)])

    wcnt = [0]

    def wslot(nk, ncol):
        r = wcnt[0] % NW
        wcnt[0] += 1
        view = wring[:, r, 0:nk * ncol].rearrange("p (k c) -> p k c", k=nk)
        return view, ("wr", r)

    def load_w(src, nk, ncol, rkeys):
        view, key = wslot(nk, ncol)
        ld(view, src, r=rkeys, w=[key])
        return view, key

    pf = {}
    WSRC = {}
    for cb_ in range(7):
        WSRC[("Wi", cb_)] = (Wi[cb_], 8, 512, [("Wi", cb_)])
    for cb_ in range(2):
        WSRC[("Wo", cb_)] = (Wo[cb_], 8, 512, [("Wo", cb_)])
    for gb_ in range(11):
        WSRC[("Wgu", gb_)] = (Wgu[gb_], 8, 512, [("Wgu", gb_, 0), ("Wgu", gb_, 1)])
    for half_ in range(2):
        for bi_, (k0_, k1_) in enumerate(((0, 8), (8, 16), (16, 22))):
            WSRC[("Wd", half_, bi_)] = (Wd[half_][:, k0_:k1_, :], k1_ - k0_, 512, [("Wd", half_, bi_)])

    def prefetch(name):
        if name not in pf:
            pf[name] = load_w(*WSRC[name])

    def get_w(name):
        if name in pf:
            return pf.pop(name)
        return load_w(*WSRC[name])

    def setup():
        ld(kc[:], kc_d[:], w=["kc"])
        ld(dco[:], dco_d[:], w=["dco"])
        ld(crow5, crow_d[:], w=[("xf", 0)])
        ld(smallr[0:1, 0:256], lamv[:], w=["lamr"])
        ld(smallr[0:1, 256:384], gsub[:], w=["gsr"])
        ld(smallr[0:1, 384:512], gqk[:], w=["gqr"])
        rbt = sb("rbt", [32, 4], F32)
        oh15 = lg[0:32, 1, 0:128]
        ohr = lg[0:32, 0, 0:384]
        ld(rbt[:], rbt_d[:], w=["rbt"])
        ld(oh15, oh15_d[:], w=[("lg", 1)])
        ld(ohr, ohr_d[:], w=[("lg", 0)])
        maskadd = et[:, 3, 0:256]
        ld(maskadd, maskadd_d[:], w=["et3"])
        P.op("dve", lambda e: e.memset(onesb[:], 1.0), writes=["onesb"])
        P.op("dve", lambda e: e.memset(warmb[:], 1.0), writes=["warmb"])
        P.op("dve", lambda e: e.memset(onesf[:], 1.0), writes=["onesf"])
        P.op("dve", lambda e: e.memset(cols[:, EPSC:EPSC + 1], EPS), writes=["epsc"])
        P.op("dve", lambda e: e.memset(S[:], 0.0), writes=["S"])
        P.op("pool", lambda e: e.memset(qAT[:].rearrange("p m h t -> p (m h t)"), 0.0), writes=[("qAT", h) for h in range(4)])
        cp(identb[:], identf, ["kc"], ["identb"])
        cp(trib[:], trif, ["kc"], ["trib"])
        tt(lamt[0:1, 0:64], smallr[0:1, 0:64], smallr[0:1, 64:128], ALU.mult, ["lamr"], ["st_a"])
        tt(lamt[0:1, 64:128], smallr[0:1, 128:192], smallr[0:1, 192:256], ALU.mult, ["lamr"], ["st_a"])
        lam2 = sb("lam2", [1, 8], F32)
        P.op("dve", lambda e: e.tensor_reduce(out=lam2[0:1, 0:2], in_=lamt[0:1, 0:128].rearrange("p (a j) -> p a j", a=2), axis=AX.X, op=ALU.add),
             reads=["st_a"], writes=["lam2a"])
        act(lam2[0:1, 2:4], lam2[0:1, 0:2], AF.Exp, ["lam2a"], ["lam2b"])
        tt(lam2[0:1, 4:5], lam2[0:1, 3:4], lam2[0:1, 2:3], ALU.subtract, ["lam2b"], ["lam2c"])
        ts(lam2[0:1, 5:6], lam2[0:1, 4:5], -LAM_INIT, None, ALU.add, None, ["lam2c"], ["lam2d"])
        P.op("dve", lambda e: e.memset(lam2[0:1, 6:7], 1.0 - LAM_INIT), writes=["lam2e"])
        pm = psb[7]
        mm(pm[:, 0:1], onesf[0:1, 0:128], lam2[0:1, 5:6], True, True, ["onesf", "lam2d"], [PS(7)])
        mm(pm[:, 1:2], smallr[0:1, 256:384], lam2[0:1, 6:7], True, True, ["gsr", "lam2e"], [PS(7)])
        mm(pm[:, 2:6], oh15, rbt[:, :], True, True, [("lg", 1), "rbt"], [PS(7)])
        mm(pm[:, 128:256], onesf[0:1, 0:128], smallr[0:1, 384:512], True, True, ["onesf", "gqr"], [PS(7)])
        cp(cols[:, NLAM:NLAM + 2], pm[:, 0:2], [PS(7)], ["nlam", "gsubc"])
        cp(cols[:, C15:C15 + 4], pm[:, 2:6], [PS(7)], ["c15"])
        cp(gqkb[:], pm[:, 128:256], [PS(7)], ["gqkb"])
        for i in range(2):
            for g in range(8):
                cp(gfull[:, i, g * 64:(g + 1) * 64], gqkb[:, i * 64:(i + 1) * 64], ["gqkb"], ["gfull"])
        ts(cols[:, CC0:CC0 + 4], cols[:, C15:C15 + 4], dco[:, 21:22], dco[:, 22:23], ALU.mult, ALU.add, ["c15", "dco"], ["cc0"])
        ts(cols[:, CC4:CC4 + 4], cols[:, C15:C15 + 4], dco[:, 24:25], dco[:, 25:26], ALU.mult, ALU.add, ["c15", "dco"], ["cc4"])
        ts(cols[:, CCF:CCF + 4], cols[:, C15:C15 + 4], dco[:, 26:27], dco[:, 27:28], ALU.mult, ALU.add, ["c15", "dco"], ["ccf"])
        pu = psb[6]
        mm(pu[0:4, 0:384], rbt[:, :], ohr, True, True, ["rbt", ("lg", 0)], [PS(6)])
        urs = et[0:4, 2, 0:384]
        cp(urs, pu[0:4, 0:384], [PS(6)], ["et2"])
        stq(urd[:], urs, r=["et2"], w=["urd"])
        hk = et[:, 0:2, :].rearrange("p a (b c) -> p (a b) c", c=256)
        ld(hk, AP(urd.tensor, 0, [[1, 128], [384, 4], [1, 256]]), r=["urd"], w=["et0", "et1"])
        for h in range(4):
            pj = psb[h % 2]
            mm(pj[:, 0:256], Jf, hk[:, h, :], True, True, ["kc", "et0", "et1"], [PS(h % 2)])
            tt(Mh[:, h, :], pj[:, 0:256], maskadd, ALU.add, [PS(h % 2), "et3"], [("Mh", h)])
            ts(G0[:, h, :], Mh[:, h, 128:256], dco[:, 20:21], cols[:, CC0 + h:CC0 + h + 1], ALU.mult, ALU.add, [("Mh", h), "dco", "cc0"], [("G0", h)])
            ts(G4[:, h, :], Mh[:, h, 128:256], dco[:, 23:24], cols[:, CC4 + h:CC4 + h + 1], ALU.mult, ALU.add, [("Mh", h), "dco", "cc4"], [("G4", h)])
        ld(MS[0:32, :, :], Mh[0:32, :, 0:32], r=[("Mh", h) for h in range(4)], w=["MS0"])
        ld(MS[32:64, :, :], Mh[0:32, :, 0:32], r=[("Mh", h) for h in range(4)], w=["MS1"])

        act(crow5[0:3, :], crow5[0:3, :], AF.Silu, [("xf", 0)], [("xf", 0)])
        pt = psb[5]
        for k in range(8):
            trn(pt[:, k * 5:(k + 1) * 5], crow5[0:5, k * 128:(k + 1) * 128], identf[0:5, 0:5], [("xf", 0), "kc"], [PS(5)])
        cp(cT[:].rearrange("p k v -> p (k v)"), pt[:, 0:40], [PS(5)], ["cT"])
        cp(scT[:], cT[:, :, 0:3], ["cT"], ["scT"])
        cp(scB[:], fr(cT[:, 0, 0:1], [[5, 8], [0, 128]]), ["cT"], ["scB"])
        cp(scS[:, :, 0:32], fr(cT[:, 0, 1:2], [[5, 8], [0, 32]]), ["cT"], ["scS0"])
        cp(scS[:, :, 32:64], fr(cT[:, 0, 2:3], [[5, 8], [0, 32]]), ["cT"], ["scS1"])
        psm = psb[4]
        psmv = psm[:, 0:96].rearrange("p (a c v) -> p a c v", a=4, c=8)
        for kind, base in ((0, 0), (1, 2), (2, 6), (3, 8)):
            for half in range(2):
                cbk = base + half
                view, key = wslot(8, 512)
                stq(view, w_ada[:, cbk * 512:(cbk + 1) * 512].rearrange("(k p) c -> p k c", p=128), w=[key])
                bslot = 0
                ld(browb[0:1, bslot, :], b_ada[0:1, cbk * 512:(cbk + 1) * 512], w=[("brow", bslot)])
                for e in range(4):
                    c = half * 4 + e
                    for k in range(8):
                        mm(psmv[:, kind, c, :], view[:, k, e * 128:(e + 1) * 128], scT[:, k, :], k == 0, False, [key, "scT"], [PS(4)])
                    mm(psmv[:, kind, c, :], browb[0:1, bslot, e * 128:(e + 1) * 128], onesf[0:1, 0:3], False, True, [("brow", bslot), "onesf"], [PS(4)])
        cp(modc[:, 0], psmv[:, 0], [PS(4)], ["modc0"])
        cp(modc[:, 2], psmv[:, 2], [PS(4)], ["modc2"])
        for c in range(8):
            ts(modc[:, 1, c, :], psmv[:, 1, c, :], 1.0, cT[:, c, 3:4], ALU.add, ALU.mult, [PS(4), "cT"], ["modc1"])
            ts(modc[:, 3, c, :], psmv[:, 3, c, :], 1.0, cT[:, c, 4:5], ALU.add, ALU.mult, [PS(4), "cT"], ["modc3"])

    def gates(sample):
        rows = 64 if sample else 128
        for gi, (gt, base) in enumerate(((gate1b, 4), (gate2b, 10))):
            for half in range(2):
                cbk = base + half
                view, key = wslot(8, 512)
                stq(view, w_ada[:, cbk * 512:(cbk + 1) * 512].rearrange("(k p) c -> p k c", p=128), w=[key])
                bslot = 0
                ld(browb[0:1, bslot, :], b_ada[0:1, cbk * 512:(cbk + 1) * 512], w=[("brow", bslot)])
                pg = psb[(gi * 2 + half) % 4]
                pk = PS((gi * 2 + half) % 4)
                for k in range(8):
                    lhs = scS[:, k, :] if sample else scB[:, k, :]
                    mm(pg[0:rows, :], lhs, view[:, k, :], k == 0, False, [key, "scB", "scS0", "scS1"], [pk])
                mm(pg[0:rows, :], onesf[0:1, 0:rows], browb[0:1, bslot, :], False, True, [("brow", bslot), "onesf"], [pk])
                act(gt[0:rows, half * 512:(half + 1) * 512], pg[0:rows, :], AF.Copy, [pk], [("gate", gi, half)])

    acc_rr = [0]

    def next_acc():
        i = acc_rr[0] % 4
        acc_rr[0] += 1
        return psb[i], PS(i)

    tb_rr = [0]

    def next_tb():
        i = tb_rr[0] % 2
        tb_rr[0] += 1
        return psbb[4 + i], PS(4 + i)

    def rstd_small(dst, src, scale, rows, rk, key):
        act(dst, src, AF.Ln, list(rk) + ["epsc"], [key], scale=scale, bias=cols[0:rows, EPSC:EPSC + 1])
        act(dst, dst, AF.Exp, [key], [key], scale=-0.5)

    def norm_part(xv, xk, rows, subs, sc0, junk_pm=False):
        n = len(subs)
        if junk_pm:
            junk = Pm[0:rows, 0:2, :].rearrange("p a c -> p (a c)")
            jk = [("Pm", 0), ("Pm", 1)]
        else:
            junk = lg[0:rows, :, :].rearrange("p a c -> p (a c)")
            jk = [("lg", 0), ("lg", 1)]
        for i, s in enumerate(subs):
            act(junk, xv[i], AF.Square, [xk[i]], jk + [("ssq", sc0 + i)], accum=stat[0:rows, sc0 + i:sc0 + i + 1])
        rstd_small(stat[0:rows, 8 + sc0:8 + sc0 + n], stat[0:rows, sc0:sc0 + n], 1.0 / D, rows,
                   [("ssq", sc0 + i) for i in range(n)], ("rstd", sc0))
        for i, s in enumerate(subs):
            ts(xn[0:rows, s, :], xv[i], stat[0:rows, 8 + sc0 + i:9 + sc0 + i], None, ALU.mult, None, [xk[i], ("rstd", sc0)], [("xn", s)])

    def transp_part(rows, nsub, kG, kS, sample):
        T = rows * nsub
        mk = ["modc0", "modc1", "modc2", "modc3"]
        for cpair in range(4):
            tbb, tk = next_tb()
            for half in range(2):
                c = 2 * cpair + half
                tb = tbb[:, half * 512:(half + 1) * 512]
                for s in range(nsub):
                    trn(tb[:, s * rows:(s + 1) * rows], xn[0:rows, s, c * 128:(c + 1) * 128], identb[0:rows, 0:rows], [("xn", s), "identb"], [tk])
            for half in range(2):
                c = 2 * cpair + half
                tb = tbb[:, half * 512:(half + 1) * 512]
                if not sample:
                    act(hT[:, c, 0:T], tb[:, 0:T], AF.Identity, [tk] + mk, [("hT", c)],
                        scale=modc[:, kG, c, 0:1], bias=modc[:, kS, c, 0:1])
                else:
                    for q in range(2):
                        act(hT[:, c, q * 32:(q + 1) * 32], tb[:, q * 32:(q + 1) * 32], AF.Identity, [tk] + mk, [("hT", c)],
                            scale=modc[:, kG, c, 1 + q:2 + q], bias=modc[:, kS, c, 1 + q:2 + q])

    def phase_norm(xviews, xkeys, rows, nsub, kG, kS, sample):
        if not sample:
            warm(WARM_NORM, 7)
        norm_part(xviews, xkeys, rows, list(range(nsub)), 0)
        transp_part(rows, nsub, kG, kS, sample)

    def hT_keys(sample):
        return [("hT", c) for c in range(8)]

    def qknorm(acc, ak, rows, goff, dest, dk, par=0):
        e0, e1 = 2 * par, 2 * par + 1
        k0, k1 = f"et{e0}", f"et{e1}"
        sc = 16 + 8 * par
        rk = ("r8", par)
        sq = et[0:rows, e0, :]
        act(sq, acc[0:rows, :], AF.Square, [ak], [k0])
        P.op("dve", lambda e: e.tensor_reduce(out=stat[0:rows, sc:sc + 8], in_=sq.rearrange("p (g j) -> p g j", g=8), axis=AX.X, op=ALU.add),
             reads=[k0], writes=[rk])
        rstd_small(stat[0:rows, sc:sc + 8], stat[0:rows, sc:sc + 8], 1.0 / 64, rows, [rk], rk)
        t1 = et[0:rows, e1, :]
        tt(t1.rearrange("p (g j) -> p g j", g=8), acc[0:rows, :].rearrange("p (g j) -> p g j", g=8),
           fr(stat[0:rows, sc:sc + 1], [[1, 8], [0, 64]]), ALU.mult, [ak, rk], [k1])
        tt(dest, t1, gfull[0:rows, goff // 64, :], ALU.mult, [k1, "gfull"], [dk])

    def rope(acc, ak, rows, tabC, tabS, tkey, dest, dk, par=0):
        e0, e1 = 2 * par, 2 * par + 1
        k0, k1 = f"et{e0}", f"et{e1}"
        xe = acc[0:rows, 0:512:2]
        xo_ = acc[0:rows, 1:512:2]
        t1 = et[0:rows, e0, 0:256]
        t2 = et[0:rows, e0, 256:512]
        t3 = et[0:rows, e1, 0:256]
        t4 = et[0:rows, e1, 256:512]
        tt(t1, xe, tabC, ALU.mult, [ak, tkey], [k0])
        tt(t2, xo_, tabS, ALU.mult, [ak, tkey], [k0])
        tt(dest[:, 0:512:2], t1, t2, ALU.subtract, [k0], [dk])
        tt(t3, xe, tabS, ALU.mult, [ak, tkey], [k1])
        tt(t4, xo_, tabC, ALU.mult, [ak, tkey], [k1])
        tt(dest[:, 1:512:2], t3, t4, ALU.add, [k1], [dk])

    def transp_heads(src, skeys, rows, nsub, dest, dname, split=False):
        T = rows * nsub
        for hp in range(2):
            tbb, tk = next_tb()
            for half in range(2):
                h = 2 * hp + half
                tb = tbb[:, half * 512:(half + 1) * 512]
                for s in range(nsub):
                    trn(tb[:, s * rows:(s + 1) * rows], src[0:rows, s, h * 128:(h + 1) * 128], identb[0:rows, 0:rows], list(skeys(s)) + ["identb"], [tk])
            for half in range(2):
                h = 2 * hp + half
                if split:
                    act(dest[0:64, 0, h, 0:T], tbb[0:64, half * 512:half * 512 + T], AF.Copy, [tk], [(dname, h)])
                    act(dest[64:128, 1, h, 0:T], tbb[64:128, half * 512:half * 512 + T], AF.Copy, [tk], [(dname, h)])
                else:
                    act(dest[:, h, 0:T], tbb[:, half * 512:half * 512 + T], AF.Copy, [tk], [(dname, h)])

    def in_proj(rows, nsub, own, sample, tabsrc, tile_i, hook_after_first=None, next_w=None):
        T = rows * nsub
        hk_ = hT_keys(sample)
        cbs = (1, 2, 0, 4, 5, 3, 6) if own else (1, 2, 4, 5)
        pending = [None]

        def flush():
            if pending[0] is not None:
                pending[0]()
                pending[0] = None
        for ci, cb in enumerate(cbs):
            W, wkey = get_w(("Wi", cb))
            if ci + 1 < len(cbs):
                prefetch(("Wi", cbs[ci + 1]))
            elif next_w is not None:
                prefetch(next_w)
            if cb == 6:
                for e in range(4):
                    acc, ak = next_acc()
                    for k in range(8):
                        mm(acc[:, 0:T], W[:, k, e * 128:(e + 1) * 128], hT[:, k, 0:T], k == 0, k == 7, [wkey] + hk_, [ak])
                    act(sgT[:, e, 0:T], acc[:, 0:T], AF.Silu, [ak], [("sgT", e)])
                flush()
                continue
            for s in range(nsub):
                acc, ak = next_acc()
                for k in range(8):
                    mm(acc[0:rows, :], hT[:, k, s * rows:(s + 1) * rows], W[:, k, :], k == 0, k == 7, [wkey] + hk_, [ak])
                chk(31)
                if cb == 0:
                    qknorm(acc, ak, rows, 0, tokb[0:rows, s, :], ("tokb", s), s % 2)
                elif cb == 1:
                    if own:
                        kfv = kf[0:rows, 0, :]
                        qknorm(acc, ak, rows, 64, kfv, "kf", s % 2)
                        if sample:
                            out_ops.append(stq(ksm[:, :], kfv, r=["kf"]))
                        else:
                            out_ops.append(stq(kp[tile_i, s * 128:(s + 1) * 128, :], kfv, r=["kf"]))
                        act(tokb[0:rows, s, :], kfv, AF.Copy, ["kf"], [("tokb", s)])
                    else:
                        qknorm(acc, ak, rows, 64, tokb[0:rows, s, :], ("tokb", s), s % 2)
                elif cb == 2:
                    if own:
                        vfv = vf[0:rows, 0, :]
                        act(vfv, acc[0:rows, :], AF.Copy, [ak], ["vf"])
                        if sample:
                            out_ops.append(stq(vsm[:, :], vfv, r=["vf"]))
                        else:
                            out_ops.append(stq(vp[tile_i, s * 128:(s + 1) * 128, :], vfv, r=["vf"]))
                        cp(vst[0:rows, s, :], vfv, ["vf"], [("vst", s)])
                    else:
                        evac(vst[0:rows, s, :], acc[0:rows, :], [ak], [("vst", s)])
                elif cb in (3, 4):
                    rs_ = s % 2
                    ld(rt[0:rows, rs_], tabsrc(s), w=[("rt", rs_)])
                    if cb == 3:
                        rope(acc, ak, rows, rt[0:rows, rs_, 0, :], rt[0:rows, rs_, 1, :], ("rt", rs_), tokb[0:rows, s, :], ("tokb", s), s % 2)
                    else:
                        rope(acc, ak, rows, rt[0:rows, rs_, 2, :], rt[0:rows, rs_, 3, :], ("rt", rs_), ke[0:rows, s, :], ("ke", s), s % 2)
                elif cb == 5:
                    evac(vR[0:rows, s, :], acc[0:rows, :], [ak], [("vR", s)])
            if hook_after_first is not None and cb == cbs[0]:
                hook_after_first()
            flush()
            if cb == 0:
                pending[0] = lambda: transp_heads(tokb, lambda s: [("tokb", s)], rows, nsub, qAT, "qAT", split=True)
            elif cb == 1:
                pending[0] = lambda: transp_heads(tokb, lambda s: [("tokb", s)], rows, nsub, kst, "kst")
            elif cb == 3:
                pending[0] = lambda: transp_heads(tokb, lambda s: [("tokb", s)], rows, nsub, qdT, "qdT")
            elif cb == 4 and own:
                pending[0] = lambda: transp_heads(ke, lambda s: [("ke", s)], rows, nsub, keT, "keT")
        flush()

    def store_kv(unit):
        stq(KT[:, unit].rearrange("h p c -> p h c"), kst[:, :, :], r=[("kst", h) for h in range(4)], w=[("KT", unit)])
        for s in range(4):
            dst = VS[:, unit, :, s, :].rearrange("h p d -> p h d")
            stq(dst, vst[:, s, :].rearrange("p (h d) -> p h d", h=4), r=[("vst", s)], w=[("VS", unit, s)])

    def u_raw(rows, nsub, pbase, pu, pk):
        for h in range(4):
            for s in range(nsub):
                mm(pu[:, h * 128:(h + 1) * 128], ke[pbase:pbase + rows, s, h * 128:(h + 1) * 128], vR[pbase:pbase + rows, s, h * 128:(h + 1) * 128],
                   s == 0, s == nsub - 1, [("ke", s), ("vR", s)], [pk])

    def retention_prompt():
        def scores(h):
            for j in range(4):
                pj, pk = next_acc()
                n = 512 - 128 * j
                mm(pj[:, 0:n], keT[:, h, 128 * j:128 * j + 128], qdT[:, h, 128 * j:512], True, True, [("keT", h), ("qdT", h)], [pk])
                tt(scb[:, j, 0:128], pj[:, 0:128], trib[:, :], ALU.mult, [pk, "trib"], [("Pm", j)])
                if n > 128:
                    cp(scb[:, j, 128:n], pj[:, 128:n], [pk], [("Pm", j)])

        def av(h, po, pok):
            mm(po[:, :], Sb[:, h, :], qdT[:, h, :], True, False, ["Sb", ("qdT", h)], [pok])
            for j in range(4):
                n = 512 - 128 * j
                mm(po[:, 128 * j:512], vR[:, j, h * 128:(h + 1) * 128], scb[:, j, 0:n], False, j == 3,
                   [("vR", j), ("Pm", j)], [pok])

        scores(0)
        for h in range(4):
            po, pok = psb[4 + h % 2], PS(4 + h % 2)
            av(h, po, pok)
            if h + 1 < 4:
                scores(h + 1)
            ret_epilogue(po, pok, h, 512)

    def ret_epilogue(po, pok, h, T):
        sq = et[:, 2, 0:T]
        act(sqb[:, 0:T], po[:, 0:T], AF.Square, [pok], ["sqb"])
        pss, psk = psb[7], PS(7)
        mm(pss[:, 0:T], onesb[:, :], sqb[:, 0:T], True, True, ["onesb", "sqb"], [psk])
        rs_ = et[:, 3, 0:T]
        act(rs_, pss[:, 0:T], AF.Ln, [psk, "epsc"], ["et3"], scale=1.0 / 128, bias=cols[:, EPSC:EPSC + 1])
        act(rs_, rs_, AF.Exp, ["et3"], ["et3"], scale=-0.5)
        tt(sq, po[:, 0:T], rs_, ALU.mult, [pok, "et3", "et2"], ["et2"])
        tt(oT[:, 4 + h, 0:T], sq, sgT[:, h, 0:T], ALU.mult, ["et2", ("sgT", h)], [("oT", 4 + h)])

    kv_rr = [0]

    def attention(qc0, N, units, h, first_full=True):
        pO = [psb[4], psb[5]]
        pOk = [PS(4), PS(5)]
        pS = [psb[6], psb[7]]
        pSk = [PS(6), PS(7)]
        tiles = []
        for u in units:
            for (kt, c0, kind, arg) in u["tiles"]:
                tiles.append((u, kt, c0, kind, arg))
        n_tiles = len(tiles)
        info = {}

        def unit_ops(u):
            if id(u) in info:
                return info[id(u)]
            if u.get("sbuf"):
                ktv, vsv = u["ktv"], u["vsv"]
                r = (lambda m, kt: ktv, lambda kt: vsv, u["kkeys"], u["vkeys"], u["pbase"], u["nkeys"])
            else:
                slot = kv_rr[0] % NKB
                kv_rr[0] += 1
                ld(ktb[:, slot, :], u["kt_src"], r=u["kt_keys"], w=[("ktb", slot)])
                ld(vsb[:, slot, :].rearrange("p (k d) -> p k d", k=4), u["vs_src"], r=u["vs_keys"], w=[("vsb", slot)])
                r = (lambda m, kt: ktb[:, slot, kt * 128:(kt + 1) * 128],
                     lambda kt: vsb[:, slot, kt * 128:(kt + 1) * 128], [("ktb", slot)], [("vsb", slot)], 0, 128)
            info[id(u)] = r
            return r

        def emit_qk(t):
            u, kt, c0, kind, arg = tiles[t]
            lk, lv, kkeys, vkeys, pb, nkeys = unit_ops(u)
            pr = slice(pb, pb + nkeys)
            for m in range(2):
                bi = 2 * (t % 2) + m
                mm(psb[bi][pr, c0:N], lk(m, kt), qAT[:, m, h, qc0 + c0:qc0 + N], True, True, kkeys + [("qAT", h)], [PS(bi)])

        def emit_exp(t):
            u, kt, c0, kind, arg = tiles[t]
            lk, lv, kkeys, vkeys, pb, nkeys = unit_ops(u)
            pr = slice(pb, pb + nkeys)
            n = N - c0
            for m in range(2):
                bi = 2 * (t % 2) + m
                pq, pqk = psb[bi], PS(bi)
                pm = Pm[pr, bi, :]
                pmk = ("Pm", bi)
                if kind == "const":
                    if m == 0:
                        b0 = 2 * (t % 2)
                        act(Pm[pr, b0:b0 + 2, c0:N], psall[pr, b0:b0 + 2, c0:N], AF.Exp, [PS(b0), PS(b0 + 1), "c15", "cc0", "cc4", "ccf"],
                            [("Pm", b0), ("Pm", b0 + 1)], scale=0.125, bias=arg[pr, :])
                else:
                    btile, w_, ccol = arg
                    w_ = min(w_, n)
                    lgv = lg[pr, m, 0:w_]
                    stt(lgv, pq[pr, c0:c0 + w_], 0.125, btile[:, 0:w_], ALU.mult, ALU.add, [pqk] + u.get("bkeys", []), [("lg", m)])
                    act(pm[:, c0:c0 + w_], lgv, AF.Exp, [("lg", m)], [pmk])
                    if w_ < n:
                        act(pm[:, c0 + w_:N], pq[pr, c0 + w_:N], AF.Exp, [pqk, "c15", "cc0", "cc4", "ccf"], [pmk], scale=0.125, bias=ccol[pr, :])

        def emit_av(t):
            u, kt, c0, kind, arg = tiles[t]
            lk, lv, kkeys, vkeys, pb, nkeys = unit_ops(u)
            pr = slice(pb, pb + nkeys)
            first, last = (t == 0), (t == n_tiles - 1)
            for m in range(2):
                bi = 2 * (t % 2) + m
                mm(pO[m][:, c0:N], lv(kt), Pm[pr, bi, c0:N], first, last, vkeys + [("Pm", bi)], [pOk[m]])
            for m in range(2):
                bi = 2 * (t % 2) + m
                mm(pS[m][:, c0:N], onesb[pr, :], Pm[pr, bi, c0:N], first, last, ["onesb", ("Pm", bi)], [pSk[m]])

        emit_qk(0)
        for t in range(n_tiles):
            emit_exp(t)
            if t + 1 < n_tiles:
                emit_qk(t + 1)
            emit_av(t)
        T = N
        rinv = lg[:, :, 0:T]
        act(rinv, psall[:, 6:8, 0:T], AF.Ln, [pSk[0], pSk[1]], [("lg", 0), ("lg", 1)])
        act(rinv, rinv, AF.Exp, [("lg", 0), ("lg", 1)], [("lg", 0), ("lg", 1)], scale=-1.0)
        r0 = lg[:, 0, 0:T]
        r1 = lg[:, 1, 0:T]
        a_ = et[:, 1, 0:T]
        b_ = et[:, 2, 0:T]
        tt(a_, pO[0][:, 0:T], r0, ALU.mult, [pOk[0], ("lg", 0)], ["et1"])
        tt(b_, pO[1][:, 0:T], r1, ALU.mult, [pOk[1], ("lg", 1)], ["et2"])
        stt(a_, b_, cols[:, NLAM:NLAM + 1], a_, ALU.mult, ALU.add, ["et1", "et2", "nlam"], ["et1"])
        act(sqb[:, 0:T], a_, AF.Square, ["et1", "et2"], ["sqb"])
        if N == 512:
            warm(WARM_EPI, 3)
        pss, psk = psb[0], PS(0)
        mm(pss[:, 0:T], onesb[:, :], sqb[:, 0:T], True, True, ["onesb", "sqb"], [psk])
        rs_ = et[:, 3, 0:T]
        act(rs_, pss[:, 0:T], AF.Ln, [psk, "epsc"], ["et3"], scale=1.0 / 128, bias=cols[:, EPSC:EPSC + 1])
        act(rs_, rs_, AF.Exp, ["et3"], ["et3"], scale=-0.5)
        stt(oT[:, h, qc0:qc0 + T], a_, cols[:, GSUB:GSUB + 1], rs_, ALU.mult, ALU.mult, ["et1", "et3", "gsubc"], [("oT", h)])

    def prompt_units(step, h):
        units = []
        c15c = cols[:, C15 + h:C15 + h + 1]
        cc0c = cols[:, CC0 + h:CC0 + h + 1]
        cc4c = cols[:, CC4 + h:CC4 + h + 1]
        ccfc = cols[:, CCF + h:CCF + h + 1]
        for u in range(2 * step + 2):
            d = dict(kt_src=KT[h, u], vs_src=VS[h, u], kt_keys=[("KT", u)], vs_keys=[("VS", u, s) for s in range(4)])
            if u == 2 * step + 1:
                d["tiles"] = [(kt, 128 * kt, "bias", (Mh[:, h, :], 256, c15c)) for kt in range(4)]
                d["bkeys"] = [("Mh", h)]
            elif u == 2 * step:
                d["tiles"] = [(kt, 0, "const", ccfc) for kt in range(3)] + [(3, 0, "bias", (G4[:, h, :], 128, cc4c))]
                d["bkeys"] = [("G4", h)]
            elif u == 2 * step - 2:
                d["tiles"] = [(kt, 0, "const", c15c) for kt in range(3)] + [(3, 0, "bias", (G0[:, h, :], 128, cc0c))]
                d["bkeys"] = [("G0", h)]
            else:
                d["tiles"] = [(kt, 0, "const", c15c) for kt in range(4)]
            units.append(d)
        return units

    def out_proj(rows, nsub, xviews, xkeys):
        okeys = [("oT", i) for i in range(8)]
        for cb2 in range(2):
            W, wkey = get_w(("Wo", cb2))
            for s in range(nsub):
                acc, ak = next_acc()
                for k in range(8):
                    mm(acc[0:rows, :], oT[:, k, s * rows:(s + 1) * rows], W[:, k, :], k == 0, k == 7, [wkey] + okeys, [ak])
                t = et[0:rows, (cb2 * nsub + s) % 2, :]
                tkey = f"et{(cb2 * nsub + s) % 2}"
                tt(t, acc[0:rows, :], gate1b[0:rows, cb2 * 512:(cb2 + 1) * 512], ALU.mult, [ak, ("gate", 0, cb2)], [tkey])
                xv = xviews[s][:, cb2 * 512:(cb2 + 1) * 512]
                tt(xv, t, xv, ALU.add, [tkey, xkeys[s]], [xkeys[s]])

    def ffn(rows, nsub, xviews, xkeys, sample, store, gb_hooks=None, mid_hook=None, next_w=None):
        T = rows * nsub
        hk_ = hT_keys(sample)
        for gb in range(11):
            W, wkey = get_w(("Wgu", gb))
            prefetch(("Wgu", gb + 1) if gb + 1 < 11 else ("Wd", 0, 0))
            if gb_hooks is not None and gb in gb_hooks:
                gb_hooks[gb]()
            for fc in range(2):
                pg, pgk = next_acc()
                pu, puk = next_acc()
                for k in range(8):
                    mm(pg[:, 0:T], W[:, k, fc * 128:(fc + 1) * 128], hT[:, k, 0:T], k == 0, k == 7, [wkey] + hk_, [pgk])
                for k in range(8):
                    mm(pu[:, 0:T], W[:, k, 256 + fc * 128:256 + (fc + 1) * 128], hT[:, k, 0:T], k == 0, k == 7, [wkey] + hk_, [puk])
                sgi = (2 * gb + fc) % 2
                sg = lg[:, sgi, 0:T]
                act(sg, pg[:, 0:T], AF.Silu, [pgk], [("lg", sgi)])
                tt(ffT[:, 2 * gb + fc, 0:T], pu[:, 0:T], sg, ALU.mult, [puk, ("lg", sgi)], [("ffT", 2 * gb + fc)])
        fkeys = [("ffT", i) for i in range(NKF)]
        if not sample:
            chk(10)
        if mid_hook is not None:
            mid_hook()
        for half in range(2):
            accs = [next_acc() for s in range(nsub)]
            for bi, (k0, k1) in enumerate(((0, 8), (8, 16), (16, 22))):
                W, wkey = get_w(("Wd", half, bi))
                nxt = ("Wd", half, bi + 1) if bi < 2 else (("Wd", 1, 0) if half == 0 else next_w)
                if nxt is not None:
                    prefetch(nxt)
                for s in range(nsub):
                    acc, ak = accs[s]
                    for kk in range(k0, k1):
                        mm(acc[0:rows, :], ffT[:, kk, s * rows:(s + 1) * rows], W[:, kk - k0, :], kk == 0, kk == NKF - 1, [wkey] + fkeys, [ak])
            for s in range(nsub):
                acc, ak = accs[s]
                t = et[0:rows, s % 2, :]
                tkey = f"et{s % 2}"
                tt(t, acc[0:rows, :], gate2b[0:rows, half * 512:(half + 1) * 512], ALU.mult, [ak, ("gate", 1, half)], [tkey])
                xv = xviews[s][:, half * 512:(half + 1) * 512]
                tt(xv, t, xv, ALU.add, [tkey, xkeys[s]], [xkeys[s]])
                if half == 1:
                    store(s)

    setup()
    prepass_in()
    gates(sample=False)

    def sample_cache_chunk(q, u):
        for h in range(4):
            src = cv[q, u * 512:(u + 1) * 512, h, :].rearrange("(kt p) d -> p kt d", p=128)
            stq(VSs[q, h, u], src, w=[("VSs", q, h, u)])
        kcb = vst
        stq(kcb[:, :, :], ck[q, u * 512:(u + 1) * 512, :].rearrange("(kt p) c -> p kt c", p=128), w=[("vst", s) for s in range(4)])
        for hp in range(2):
            tbb, tk = next_tb()
            for half in range(2):
                h = 2 * hp + half
                for kt in range(4):
                    trn(tbb[:, half * 512 + kt * 128:half * 512 + (kt + 1) * 128], kcb[:, kt, h * 128:(h + 1) * 128], identb[:, :], [("vst", kt), "identb"], [tk])
            for half in range(2):
                h = 2 * hp + half
                act(kst[:, h, :], tbb[:, half * 512:(half + 1) * 512], AF.Copy, [tk], [("kst", h)])
        stq(KTs[q, :, u].rearrange("h p c -> p h c"), kst[:, :, :], r=[("kst", h) for h in range(4)], w=[("KTs", q, u)])

    sample_chunks = [(q, u) for q in range(2) for u in range(4)]

    try:
        def foreign_sub(slot, s):
            ld(xf[:, 0, :], xs[slot, s * 128:(s + 1) * 128, :], w=[("xf", 0)])
            norm_part([xf[:, 0, :]], [("xf", 0)], 128, [s], 24 + s, junk_pm=True)

        def foreign_part1(slot):
            for s in range(4):
                foreign_sub(slot, s)

        def foreign_part2():
            transp_part(128, 4, 1, 0, False)

        if n_steps > 0:
            foreign_part1(0)
            foreign_part2()
        for step in range(n_steps):
            slotF, slotO = 2 * step, 2 * step + 1
            xviews = [xo[:, s, :] for s in range(4)]
            xkeys = [("xo", s) for s in range(4)]
            def own_norm(xviews=xviews, xkeys=xkeys, slotO=slotO):
                for s in range(4):
                    ld(xviews[s], xs[slotO, s * 128:(s + 1) * 128, :], w=[xkeys[s]])
                norm_part(xviews, xkeys, 128, [0, 1, 2, 3], 0)
            chk(1)
            in_proj(128, 4, False, False, lambda s, slot=slotF: rope_d[slot, s * 128:(s + 1) * 128], step, hook_after_first=own_norm, next_w=("Wi", 1))
            chk(2)
            store_kv(slotF)
            pu, pk = psb[7], PS(7)
            u_raw(128, 4, 0, pu, pk)
            cp(Uf[:].rearrange("p h e -> p (h e)"), pu[:, :], [pk], ["Uf"])
            for h in range(4):
                ts(tmpS[:, h, :], S[:, h, :], dco[:, h:h + 1], None, ALU.mult, None, ["S", "dco"], [("tmpS", h)])
                stt(Sst[:, h, :], Uf[:, h, :], dco[:, 4 + h:5 + h], tmpS[:, h, :], ALU.mult, ALU.add, ["Uf", "dco", ("tmpS", h)], [("Sst", h)])
            cp(Sb[:].rearrange("p h e -> p (h e)"), Sst[:].rearrange("p h e -> p (h e)"), [("Sst", h) for h in range(4)], ["Sb"])
            chk(3)
            transp_part(128, 4, 1, 0, False)
            chk(4)
            in_proj(128, 4, True, False, lambda s, slot=slotO: rope_d[slot, s * 128:(s + 1) * 128], step)
            chk(5)
            store_kv(slotO)
            if step == 0:
                prepass_rest()
            if do_sample and sample_chunks and (step >= 1 or n_steps == 1):
                for _ in range(8 if n_steps == 1 else (2 if len(sample_chunks) > 8 - step else 1)):
                    if sample_chunks:
                        sample_cache_chunk(*sample_chunks.pop(0))
            retention_prompt()
            pu, pk = psb[7], PS(7)
            u_raw(128, 4, 0, pu, pk)
            for h in range(4):
                ts(tmpS[:, h, :], S[:, h, :], dco[:, 8 + h:9 + h], None, ALU.mult, None, ["S", "dco"], [("tmpS", h)])
                stt(tmpS[:, h, :], Uf[:, h, :], dco[:, 16 + h:17 + h], tmpS[:, h, :], ALU.mult, ALU.add, ["Uf", "dco", ("tmpS", h)], [("tmpS", h)])
                stt(S[:, h, :], pu[:, h * 128:(h + 1) * 128], dco[:, 12 + h:13 + h], tmpS[:, h, :], ALU.mult, ALU.add, [pk, "dco", ("tmpS", h)], ["S"])
            chk(6)
            prefetch(("Wo", 0))
            prefetch(("Wo", 1))
            prefetch(("Wgu", 0))
            for h in range(4):
                attention(0, 512, prompt_units(step, h), h)
            chk(7)
            out_proj(128, 4, xviews, xkeys)
            chk(8)
            phase_norm(xviews, xkeys, 128, 4, 3, 2, False)
            chk(9)

            def store(s, step=step, xviews=xviews, xkeys=xkeys):
                out_ops.append(stq(y[step, s * 128:(s + 1) * 128, :], xviews[s], r=[xkeys[s]]))
            if step + 1 < n_steps:
                nslot = 2 * (step + 1)
                ffn(128, 4, xviews, xkeys, False, store,
                    gb_hooks={1 + 2 * s_: (lambda nslot=nslot, s_=s_: foreign_sub(nslot, s_)) for s_ in range(4)}, mid_hook=foreign_part2,
                    next_w=("Wi", 1))
            else:
                ffn(128, 4, xviews, xkeys, False, store)
        out_ops.append(stq(rp[:].rearrange("h d e -> d h e"), S[:, :, :], r=["S"]))

        if do_sample:
            while sample_chunks:
                sample_cache_chunk(*sample_chunks.pop(0))
            gates(sample=True)
            xviews = [xo[0:64, 0, :]]
            xkeys = [("xo", 0)]
            ld(xviews[0], xsm[:, :], w=[xkeys[0]])
            phase_norm(xviews, xkeys, 64, 1, 1, 0, True)
            in_proj(64, 1, True, True, lambda s: rope_sd[:, :, :], 0)
            SH = [Sst, tmpS]
            SHk = [[("Sst", h) for h in range(4)], [("tmpS", h) for h in range(4)]]
            for q in range(2):
                ld(SH[q][:, :, :], st[q].rearrange("h d e -> d h e"), w=SHk[q])
            for q in range(2):
                shb = Sb
                cp(shb[:].rearrange("p h e -> p (h e)"), SH[q][:].rearrange("p h e -> p (h e)"), SHk[q], ["Sb"])
                pr = slice(32 * q, 32 * q + 32)
                for h in range(4):
                    pj, pk = next_acc()
                    mm(pj[pr, 0:32], keT[:, h, 32 * q:32 * q + 32], qdT[:, h, 32 * q:32 * q + 32], True, True, [("keT", h), ("qdT", h)], [pk])
                    tt(scb[pr, 0, 0:32], pj[pr, 0:32], trib[pr, 32 * q:32 * q + 32], ALU.mult, [pk, "trib"], [("Pm", 0)])
                    po, pok = psb[6], PS(6)
                    mm(po[:, 0:32], shb[:, h, :], qdT[:, h, 32 * q:32 * q + 32], True, False, ["Sb", ("qdT", h)], [pok])
                    mm(po[:, 0:32], vR[pr, 0, h * 128:(h + 1) * 128], scb[pr, 0, 0:32], False, True, [("vR", 0), ("Pm", 0)], [pok])
                    sq = et[:, 2, 0:32]
                    act(sq, po[:, 0:32], AF.Square, [pok], ["et2"])
                    pss, psk = psb[7], PS(7)
                    mm(pss[:, 0:32], onesf[:, :], sq, True, True, ["onesf", "et2"], [psk])
                    rs_ = et[:, 3, 0:32]
                    act(rs_, pss[:, 0:32], AF.Ln, [psk, "epsc"], ["et3"], scale=1.0 / 128, bias=cols[:, EPSC:EPSC + 1])
                    act(rs_, rs_, AF.Exp, ["et3"], ["et3"], scale=-0.5)
                    tt(sq, po[:, 0:32], rs_, ALU.mult, [pok, "et3", "et2"], ["et2"])
                    tt(oT[:, 4 + h, 32 * q:32 * q + 32], sq, sgT[:, h, 32 * q:32 * q + 32], ALU.mult, ["et2", ("sgT", h)], [("oT", 4 + h)])
                pu, pk = psb[7], PS(7)
                u_raw(32, 1, 32 * q, pu, pk)
                for h in range(4):
                    tt(Uf[:, h, :], pu[:, h * 128:(h + 1) * 128], SH[q][:, h, :], ALU.add, [pk] + SHk[q], ["Uf"])
                    ts(Uf[:, h, :], Uf[:, h, :], dco[:, 28 + h:29 + h], None, ALU.mult, None, ["Uf", "dco"], ["Uf"])
                out_ops.append(stq(rs[q].rearrange("h d e -> d h e"), Uf[:, :, :], r=["Uf"]))
            for q in range(2):
                for h in range(4):
                    c15c = cols[:, C15 + h:C15 + h + 1]
                    units = []
                    for u in range(4):
                        d = dict(kt_src=KTs[q, h, u], vs_src=VSs[q, h, u], kt_keys=[("KTs", q, u)], vs_keys=[("VSs", q, h, u)])
                        if u < 3:
                            d["tiles"] = [(kt, 0, "const", c15c) for kt in range(4)]
                        else:
                            d["tiles"] = [(kt, 0, "const", c15c) for kt in range(3)] + [(3, 0, "bias", (Mh[:, h, 128:160], 32, None))]
                            d["bkeys"] = [("Mh", h)]
                        units.append(d)
                    pr = slice(32 * q, 32 * q + 32)
                    units.append(dict(sbuf=True, ktv=kst[:, h, 32 * q:32 * q + 32], vsv=vst[pr, 0, h * 128:(h + 1) * 128],
                                      kkeys=[("kst", h)], vkeys=[("vst", 0)], pbase=32 * q, nkeys=32,
                                      tiles=[(0, 0, "bias", (MS[pr, h, :], 32, None))], bkeys=["MS0", "MS1"]))
                    attention(32 * q, 32, units, h)
            out_proj(64, 1, xviews, xkeys)
            phase_norm(xviews, xkeys, 64, 1, 3, 2, True)

            def store_s(s):
                out_ops.append(stq(ysm[:, :], xviews[0], r=[xkeys[0]]))
            ffn(64, 1, xviews, xkeys, True, store_s)


    except _StopBuild:
        pass
    P.emit(final_wait_ops=out_ops)
    return nc, P


_PROG_CACHE = {}


def kernel(x_prompt, x_sample, c_prompt, c_sample, cache_k, cache_v, state_ret, w_ada, b_ada,
           g_norm1, g_norm2, w_in, g_q, g_k, lam_q1, lam_k1, lam_q2, lam_k2, g_subln, w_out,
           w_ff_gate, w_ff_up, w_ff_down, rel_bias):
    f = lambda a: np.ascontiguousarray(np.asarray(a, dtype=np.float32))
    x_prompt, x_sample, c_prompt, c_sample = f(x_prompt), f(x_sample), f(c_prompt), f(c_sample)
    cache_k, cache_v, state_ret = f(cache_k), f(cache_v), f(state_ret)
    C = _static_consts()
    if "nc" not in _PROG_CACHE:
        _PROG_CACHE["nc"] = build_program()
    nc, P = _PROG_CACHE["nc"]
    shared = dict(
        w_ada=f(w_ada)[0], b_ada=f(b_ada), w_in=f(w_in)[0],
        gqk=np.concatenate([f(g_q), f(g_k)], axis=1),
        lamv=np.concatenate([f(lam_q1), f(lam_k1), f(lam_q2), f(lam_k2)], axis=1),
        gsub=f(g_subln), w_out=f(w_out)[0], wg=f(w_ff_gate)[0], wu=f(w_ff_up)[0], wd=f(w_ff_down)[0],
        rbt=f(rel_bias), rope_s=C["rope_s"], ohr=C["ohr"], oh15=C["oh15"], maskadd=C["maskadd"], kc=C["kc"],
    )
    in_maps = []
    for c in range(8):
        b, p = c // 2, c % 2
        order = []
        for i in range(NT // 2):
            order += [2 * i + 1 - p, 2 * i + p]
        xb = x_prompt[b].reshape(NT, TT, D)[order]
        m = dict(shared)
        m.update(
            xs=np.ascontiguousarray(xb),
            xsm=np.ascontiguousarray(x_sample[2 * c:2 * c + 2].reshape(64, D)),
            crow=np.ascontiguousarray(np.concatenate([c_prompt[b:b + 1], c_sample[2 * c:2 * c + 2], f(g_norm1), f(g_norm2)], axis=0)),
            ck=np.ascontiguousarray(cache_k[0, 2 * c:2 * c + 2].reshape(2, PAST, 512)),
            cv=np.ascontiguousarray(cache_v[0, 2 * c:2 * c + 2]),
            st=np.ascontiguousarray(state_ret[0, 2 * c:2 * c + 2]),
            rope=C["rope"][p], dco=C["dco"][p],
        )
        in_maps.append(m)
    res = run_bass_kernel_spmd(nc, in_maps, core_ids=list(range(8)))
    R = res.results
    y_prompt = np.zeros((4, SEQ, D), np.float32)
    k_prompt = np.zeros((1, 4, SEQ, 4, 2, 64), np.float32)
    v_prompt = np.zeros((1, 4, SEQ, 4, 128), np.float32)
    ret_prompt = np.zeros((1, 4, 4, 128, 128), np.float32)
    y_sample = np.zeros((16, LS, D), np.float32)
    k_sample = np.zeros((1, 16, LS, 4, 2, 64), np.float32)
    v_sample = np.zeros((1, 16, LS, 4, 128), np.float32)
    ret_sample = np.zeros((1, 16, 4, 128, 128), np.float32)
    for c in range(8):
        b, p = c // 2, c % 2
        r = R[c]
        for i in range(NT // 2):
            t = 2 * i + p
            y_prompt[b, t * TT:(t + 1) * TT] = r["y"][i]
            k_prompt[0, b, t * TT:(t + 1) * TT] = r["kp"][i].reshape(TT, 4, 2, 64)
            v_prompt[0, b, t * TT:(t + 1) * TT] = r["vp"][i].reshape(TT, 4, 128)
        if p == 0:
            ret_prompt[0, b] = r["rp"]
        y_sample[2 * c:2 * c + 2] = r["ysm"].reshape(2, LS, D)
        k_sample[0, 2 * c:2 * c + 2] = r["ksm"].reshape(2, LS, 4, 2, 64)
        v_sample[0, 2 * c:2 * c + 2] = r["vsm"].reshape(2, LS, 4, 128)
        ret_sample[0, 2 * c:2 * c + 2] = r["rs"]
    return (y_prompt, y_sample, k_prompt, v_prompt, ret_prompt, k_sample, v_sample, ret_sample)
```

```python
import math
import numpy as np
import concourse.bass as bass
import concourse.mybir as mybir
from concourse.bass_utils import run_bass_kernel_spmd

F32 = mybir.dt.float32
BF16 = mybir.dt.bfloat16
AF = mybir.ActivationFunctionType
ALU = mybir.AluOpType
AX = mybir.AxisListType
AP = bass.AP

D = 1024
DIN = 3584
DFF = 2816
NKF = DFF // 128
SEQ = 8192
TT = 512
NT = SEQ // TT
PAST = 2048
LS = 32
EPS = 1e-6
LAM_INIT = 0.8 - 0.6 * math.exp(-0.3 * 0)
NEG = -30000.0
WARM_EPI = 24
WARM_NORM = 30
ENGS = ("pe", "act", "dve", "pool", "sp")


class Op:
    __slots__ = ("eng", "fn", "deps", "is_dma", "signal", "val", "sem", "waits", "know")

    def __init__(self, eng, fn, deps, is_dma):
        self.eng = eng
        self.fn = fn
        self.deps = deps
        self.is_dma = is_dma
        self.signal = is_dma
        self.val = None
        self.sem = None


class Prog:
    def __init__(self, nc, n_dma_sems=24):
        self.nc = nc
        self.ops = {e: [] for e in ENGS}
        self.all_ops = []
        self.bufs = {}
        self.n_dma_sems = n_dma_sems
        self.dma_rr = {"sp": 0, "pool": 0, "act": 0}
        self.dma_last = {}

    @staticmethod
    def _flat(keys):
        out = []
        for k in keys:
            if isinstance(k, list):
                out.extend(k)
            else:
                out.append(k)
        return out

    def _deps_for(self, reads, writes):
        deps = []
        for k in reads:
            ent = self.bufs.get(k)
            if ent is not None and ent[0] is not None:
                deps.append(ent[0])
        for k in writes:
            ent = self.bufs.get(k)
            if ent is not None:
                if ent[0] is not None:
                    deps.append(ent[0])
                deps.extend(ent[1])
        return deps

    def _commit(self, op, reads, writes):
        for k in reads:
            ent = self.bufs.setdefault(k, [None, []])
            if not op.is_dma:
                ent[1] = [r for r in ent[1] if r.is_dma or r.eng != op.eng]
            ent[1].append(op)
        for k in writes:
            self.bufs[k] = [op, []]

    def op(self, eng, fn, reads=(), writes=()):
        reads, writes = self._flat(reads), self._flat(writes)
        o = Op(eng, fn, self._deps_for(reads, writes), False)
        self.all_ops.append(o)
        self.ops[eng].append(o)
        self._commit(o, reads, writes)
        return o

    def dma(self, queue, fn, reads=(), writes=()):
        reads, writes = self._flat(reads), self._flat(writes)
        deps = self._deps_for(reads, writes)
        j = self.dma_rr[queue]
        nq = self.n_dma_sems if queue == "sp" else 4
        self.dma_rr[queue] = (j + 1) % nq
        prev = self.dma_last.get((queue, j))
        if prev is not None:
            deps.append(prev)
        o = Op(queue, fn, deps, True)
        o.sem = (queue, j)
        self.all_ops.append(o)
        self.ops[queue].append(o)
        self.dma_last[(queue, j)] = o
        self._commit(o, reads, writes)
        return o

    def emit(self, final_wait_ops=()):
        nc = self.nc
        for o in self.all_ops:
            for d in o.deps:
                if d.is_dma:
                    continue
                if d.eng == o.eng and d.eng == "pe" and not o.is_dma:
                    continue
                d.signal = True
        for o in final_wait_ops:
            if not o.is_dma:
                o.signal = True
        for e in ("pe", "act", "dve", "pool"):
            comp = [o for o in self.ops[e] if not o.is_dma]
            if comp:
                comp[-1].signal = True
        sem_h = {}
        for e in ("pe", "act", "dve", "pool"):
            sem_h[("eng", e)] = nc.alloc_semaphore(f"s_{e}")
        for q in ("sp", "pool", "act"):
            used = set(o.sem[1] for o in self.ops[q] if o.is_dma)
            for j in sorted(used):
                sem_h[(q, j)] = nc.alloc_semaphore(f"d_{q}{j}")
        cnt = {}
        for e in ENGS:
            for o in self.ops[e]:
                if o.is_dma:
                    k = o.sem
                    cnt[k] = cnt.get(k, 0) + 16
                    o.val = cnt[k]
                elif o.signal:
                    k = ("eng", e)
                    cnt[k] = cnt.get(k, 0) + 1
                    o.val = cnt[k]
                    o.sem = k
        self.stats = dict(n_ops={e: len(self.ops[e]) for e in ENGS}, max_sem=max(cnt.values()),
                          n_sig={e: cnt.get(("eng", e), 0) for e in ENGS}, n_wait={})
        prog = self
        eng_know = {e: {} for e in ENGS}
        order = {id(o): i for i, o in enumerate(self.all_ops)}
        for o in self.all_ops:
            e = o.eng
            ek = eng_know[e]
            waits = []
            deps = [d for d in o.deps if not ((not d.is_dma) and d.eng == e and e == "pe" and not o.is_dma)]
            deps.sort(key=lambda d: -order[id(d)])
            for d in deps:
                k, v = d.sem, d.val
                if ek.get(k, 0) >= v:
                    continue
                waits.append((k, v))
                for kk, vv in d.know.items():
                    if vv > ek.get(kk, 0):
                        ek[kk] = vv
                if v > ek.get(k, 0):
                    ek[k] = v
            o.waits = waits
            kn = dict(ek)
            if o.sem is not None and o.val is not None:
                if o.val > kn.get(o.sem, 0):
                    kn[o.sem] = o.val
            o.know = kn

        def run_engine(e, h):
            nwait = 0
            for o in prog.ops[e]:
                for k, v in o.waits:
                    h.wait_ge(sem_h[k], v)
                    nwait += 1
                ins = o.fn(h)
                if o.is_dma:
                    ins.then_inc(sem_h[o.sem], 16)
                elif o.signal:
                    ins.then_inc(sem_h[o.sem], 1)
            if e == "sp":
                for k, v in cnt.items():
                    h.wait_ge(sem_h[k], v)
            prog.stats["n_wait"][e] = nwait

        with nc.Block() as block:
            @block.tensor
            def _(h):
                run_engine("pe", h)

            @block.scalar
            def _(h):
                run_engine("act", h)

            @block.vector
            def _(h):
                run_engine("dve", h)

            @block.gpsimd
            def _(h):
                run_engine("pool", h)

            @block.sync
            def _(h):
                run_engine("sp", h)


def _t5_bucket_np(rel):
    import jax
    import jax.numpy as jnp
    cpu = jax.devices("cpu")[0]
    with jax.default_device(cpu):
        rel = jnp.asarray(rel, dtype=jnp.int32)
        nb = 16
        ret = jnp.where(rel > 0, nb, 0)
        n = jnp.abs(rel)
        max_exact = nb // 2
        large = max_exact + (jnp.log(jnp.maximum(n, 1).astype(jnp.float32) / max_exact)
                             / math.log(128 / max_exact) * (nb - max_exact)).astype(jnp.int32)
        large = jnp.minimum(large, nb - 1)
        out = ret + jnp.where(n < max_exact, n, large)
        return np.asarray(out)


_CONST_CACHE = {}


def _gammas():
    return (1.0 - 2.0 ** (-5.0 - np.arange(4, dtype=np.float64)))


def _rope_tables(pos, L):
    n = pos.shape[0]
    inv = (np.float32(1.0) / (np.float32(10000.0) ** np.linspace(0.0, 1.0, 64, dtype=np.float32))).astype(np.float32)
    ang = pos.astype(np.float32)[:, None] * inv[None, :]
    cos = np.cos(ang).astype(np.float64)
    sin = np.sin(ang).astype(np.float64)
    l = (np.arange(n) % L).astype(np.float64)
    lg = np.log(_gammas())
    qd = np.exp((l[:, None] + 1.0) * lg[None, :])
    kd = np.exp(-(l[:, None] + 1.0) * lg[None, :]) / math.sqrt(128.0)
    out = np.zeros((n, 4, 4, 64), np.float64)
    out[:, 0] = cos[:, None, :] * qd[:, :, None]
    out[:, 1] = sin[:, None, :] * qd[:, :, None]
    out[:, 2] = cos[:, None, :] * kd[:, :, None]
    out[:, 3] = sin[:, None, :] * kd[:, :, None]
    return out.reshape(n, 4, 256).astype(np.float32)


def _static_consts():
    if "c" in _CONST_CACHE:
        return _CONST_CACHE["c"]
    g = _gammas()
    rel = 127 - np.arange(384)
    bk = _t5_bucket_np(rel)
    ohr = np.zeros((32, 384), np.float32)
    ohr[bk, np.arange(384)] = 1.0
    oh15 = np.zeros((32, 128), np.float32)
    b_far = int(_t5_bucket_np(np.array([-200]))[0])
    oh15[b_far, :] = 1.0
    maskadd = np.zeros((128, 256), np.float32)
    k = np.arange(128)[:, None]
    c = np.arange(256)[None, :]
    maskadd[(k // 64) > (c // 64)] = NEG
    kc = np.zeros((128, 448), np.float32)
    kc[0, 0:128] = 1.0
    kc[32, 128:256] = 1.0
    m = np.arange(128)[:, None]
    l = np.arange(128)[None, :]
    kc[:, 256:384] = (l >= m).astype(np.float32)
    kc2 = np.zeros((128, 256), np.float32)
    kc2[np.arange(128), 127 - np.arange(128)] = 1.0
    kc2[np.arange(128), 128 + np.arange(128)] = 1.0
    kc = np.concatenate([kc[:, :384], kc2], axis=1)
    dco = np.zeros((2, 128, 32), np.float32)
    for p in range(2):
        a = np.ones(4) if p == 0 else g ** 512
        b = np.zeros(4) if p == 0 else g ** 512
        e = g ** 1024 if p == 0 else g ** 512
        f = g ** 512 if p == 0 else g ** 1024
        dco[p, :, 0:4] = a
        dco[p, :, 4:8] = b
        dco[p, :, 8:12] = g ** 1024
        dco[p, :, 12:16] = e
        dco[p, :, 16:20] = f
        sel = [1, 0, 0, 0, 0, NEG, 0, NEG] if p == 0 else [0, 1, 0, 1, 0, 0, 1, 0]
        dco[p, :, 20:28] = np.array(sel, np.float32)
        dco[p, :, 28:32] = g ** 32
    rope = np.zeros((2, NT, TT, 4, 256), np.float32)
    for p in range(2):
        for i in range(NT // 2):
            for j, tile in enumerate((2 * i + 1 - p, 2 * i + p)):
                pos = tile * TT + np.arange(TT)
                rope[p, 2 * i + j] = _rope_tables(pos, TT)
    rope_s = _rope_tables(PAST + (np.arange(64) % LS), LS)
    res = dict(ohr=ohr, oh15=oh15, maskadd=maskadd, kc=kc, dco=dco, rope=rope, rope_s=rope_s)
    _CONST_CACHE["c"] = res
    return res


class _StopBuild(Exception):
    pass


def build_program(n_steps=NT // 2, do_sample=True, cut=0):
    def chk(n):
        if cut == n:
            raise _StopBuild()
    nc = bass.Bass("TRN2", target_bir_lowering=False)
    P = Prog(nc)

    def din(name, shape, dt=F32):
        return nc.dram_tensor(name, list(shape), dt, kind="ExternalInput").ap()

    def dout(name, shape, dt=F32):
        return nc.dram_tensor(name, list(shape), dt, kind="ExternalOutput").ap()

    def dscr(name, shape, dt):
        return nc.dram_tensor(name, list(shape), dt, kind="Internal").ap()

    def sb(name, shape, dt):
        return nc.alloc_sbuf_tensor("s_" + name, list(shape), dt).ap()

    xs = din("xs", [NT, TT, D])
    xsm = din("xsm", [64, D])
    crow_d = din("crow", [5, D])
    ck = din("ck", [2, PAST, 512])
    cv = din("cv", [2, PAST, 4, 128])
    st = din("st", [2, 4, 128, 128])
    w_ada = din("w_ada", [D, 6 * D])
    b_ada = din("b_ada", [1, 6 * D])
    w_in = din("w_in", [D, DIN])
    gqk = din("gqk", [1, 128])
    lamv = din("lamv", [1, 256])
    gsub = din("gsub", [1, 128])
    w_out = din("w_out", [D, D])
    wg = din("wg", [D, DFF])
    wu = din("wu", [D, DFF])
    wd = din("wd", [DFF, D])
    rbt_d = din("rbt", [32, 4])
    rope_d = din("rope", [NT, TT, 4, 256])
    rope_sd = din("rope_s", [64, 4, 256])
    dco_d = din("dco", [128, 32])
    ohr_d = din("ohr", [32, 384])
    oh15_d = din("oh15", [32, 128])
    maskadd_d = din("maskadd", [128, 256])
    kc_d = din("kc", [128, 640])

    y = dout("y", [NT // 2, TT, D])
    ysm = dout("ysm", [64, D])
    kp = dout("kp", [NT // 2, TT, 512])
    vp = dout("vp", [NT // 2, TT, 512])
    rp = dout("rp", [4, 128, 128])
    ksm = dout("ksm", [64, 512])
    vsm = dout("vsm", [64, 512])
    rs = dout("rs", [2, 4, 128, 128])

    Wi = dscr("Wi", [7, 128, 8, 512], BF16)
    Wo = dscr("Wo", [2, 128, 8, 512], BF16)
    Wgu = dscr("Wgu", [11, 128, 8, 512], BF16)
    Wd = dscr("Wd", [2, 128, NKF, 512], BF16)
    KT = dscr("KT", [4, NT, 128, 512], BF16)
    VS = dscr("VS", [4, NT, 128, 4, 128], BF16)
    KTs = dscr("KTs", [2, 4, 4, 128, 512], BF16)
    VSs = dscr("VSs", [2, 4, 4, 128, 4, 128], BF16)
    urd = dscr("urd", [4, 384], F32)

    kc = sb("kc", [128, 640], F32)
    sel0 = kc[0:64, 0:128]
    sel1 = kc[0:64, 128:256]
    trif = kc[:, 256:384]
    Jf = kc[:, 384:512]
    identf = kc[:, 512:640]
    identb = sb("identb", [128, 128], BF16)
    trib = sb("trib", [128, 128], BF16)
    onesb = sb("onesb", [128, 128], BF16)
    onesf = sb("onesf", [128, 128], F32)
    dco = sb("dco", [128, 32], F32)
    Mh = sb("Mh", [128, 4, 256], F32)
    G0 = sb("G0", [128, 4, 128], F32)
    G4 = sb("G4", [128, 4, 128], F32)
    MS = sb("MS", [64, 4, 32], F32)
    cols = sb("cols", [128, 64], F32)
    C15, CC0, CC4, CCF, NLAM, GSUB, EPSC = 0, 4, 8, 12, 16, 17, 18
    modc = sb("modc", [128, 4, 8, 3], F32)
    gate1b = sb("gate1b", [128, D], F32)
    gate2b = sb("gate2b", [128, D], F32)
    gqkb = sb("gqkb", [128, 128], F32)
    cT = sb("cT", [128, 8, 5], F32)
    scT = sb("scT", [128, 8, 3], BF16)
    scB = sb("scB", [128, 8, 128], BF16)
    scS = sb("scS", [128, 8, 64], BF16)
    browb = sb("browb", [1, 1, 512], F32)
    smallr = sb("smallr", [32, 512], F32)
    xo = sb("xo", [128, 4, D], F32)
    xf = sb("xf", [128, 1, D], F32)
    crow5 = xf[0:5, 0, :]
    gfull = sb("gfull", [128, 2, 512], F32)
    hT = sb("hT", [128, 8, TT], BF16)
    rt = sb("rt", [128, 2, 4, 256], F32)
    tokb = sb("tokb", [128, 4, 512], BF16)
    qAT = sb("qAT", [128, 2, 4, TT], BF16)
    kst = sb("kst", [128, 4, TT], BF16)
    vst = sb("vst", [128, 4, 512], BF16)
    kf = sb("kf", [128, 1, 512], F32)
    vf = sb("vf", [128, 1, 512], F32)
    qdT = sb("qdT", [128, 4, TT], BF16)
    keT = sb("keT", [128, 4, TT], BF16)
    ke = sb("ke", [128, 4, 512], BF16)
    vR = sb("vR", [128, 4, 512], BF16)
    sgT = sb("sgT", [128, 4, TT], BF16)
    oT = sb("oT", [128, 8, TT], BF16)
    ffT = sb("ffT", [128, NKF, TT], BF16)
    NW = 3
    wring = sb("wring", [128, NW, 4096], BF16)
    NKB = 2
    ktb = sb("ktb", [128, NKB, 512], BF16)
    vsb = sb("vsb", [128, NKB, 512], BF16)
    Pm = sb("Pm", [128, 4, 512], BF16)
    scb = Pm
    lg = sb("lg", [128, 2, 512], F32)
    et = sb("et", [128, 4, 512], F32)
    xn = sb("xn", [128, 4, D], BF16)
    lamt = sb("lamt", [1, 128], F32)
    sqb = sb("sqb", [128, 512], BF16)
    warmb = sb("warmb", [128, 512], BF16)
    S = sb("S", [128, 4, 128], F32)
    Sst = sb("Sst", [128, 4, 128], F32)
    tmpS = sb("tmpS", [128, 4, 128], F32)
    Sb = sb("Sb", [128, 4, 128], BF16)
    Uf = sb("Uf", [128, 4, 128], F32)
    stat = sb("stat", [128, 64], F32)

    psall = nc.alloc_psum_tensor("psall", [128, 8, 512], F32).ap()
    psallb = psall.bitcast(BF16)
    psb = [psall[:, i, :] for i in range(8)]
    psbb = [psallb[:, i, :] for i in range(8)]

    def PS(i):
        return ("ps", i)

    def mm(out, lhsT, rhs, start, stop, r, w):
        return P.op("pe", lambda e: e.matmul(out, lhsT=lhsT, rhs=rhs, start=start, stop=stop), reads=r, writes=w)

    def warm(n, bank):
        for _ in range(n):
            mm(psb[bank][:, :], onesb[:, :], warmb[:, :], True, True, ["onesb", "warmb"], [PS(bank)])

    def trn(out, in_, ident, r, w):
        return P.op("pe", lambda e: e.transpose(out, in_, ident), reads=r, writes=w)

    def act(out, in_, func, r, w, scale=1.0, bias=0.0, accum=None):
        if accum is None:
            return P.op("act", lambda e: e.activation(out=out, in_=in_, func=func, bias=bias, scale=scale), reads=r, writes=w)
        return P.op("act", lambda e: e.activation(out=out, in_=in_, func=func, bias=bias, scale=scale, accum_out=accum), reads=r, writes=w)

    def tt(out, in0, in1, op, r, w, eng="dve"):
        return P.op(eng, lambda e: e.tensor_tensor(out=out, in0=in0, in1=in1, op=op), reads=r, writes=w)

    def ts(out, in0, s1, s2, op0, op1, r, w, eng="dve"):
        if op1 is None:
            return P.op(eng, lambda e: e.tensor_scalar(out=out, in0=in0, scalar1=s1, scalar2=None, op0=op0), reads=r, writes=w)
        return P.op(eng, lambda e: e.tensor_scalar(out=out, in0=in0, scalar1=s1, scalar2=s2, op0=op0, op1=op1), reads=r, writes=w)

    def stt(out, in0, scalar, in1, op0, op1, r, w):
        return P.op("dve", lambda e: e.scalar_tensor_tensor(out=out, in0=in0, scalar=scalar, in1=in1, op0=op0, op1=op1), reads=r, writes=w)

    def cp(out, in_, r, w, eng="dve"):
        return P.op(eng, lambda e: e.tensor_copy(out=out, in_=in_), reads=r, writes=w)

    def ld(out, in_, r=(), w=()):
        return P.dma("sp", lambda e: e.dma_start(out=out, in_=in_), reads=r, writes=w)

    def stq(out, in_, r=(), w=()):
        return P.dma("pool", lambda e: e.dma_start(out=out, in_=in_), reads=r, writes=w)

    ev_rr = [0]

    def evac(out, in_, r, w):
        ev_rr[0] ^= 1
        if ev_rr[0]:
            return act(out, in_, AF.Copy, r, w)
        return cp(out, in_, r, w)

    def fr(ap, dims):
        return AP(ap.tensor, ap.offset, [list(ap.ap[0])] + [list(d) for d in dims])

    out_ops = []

    def prepass_in():
        for cb in (1, 2, 4, 5, 0, 3, 6):
            src = w_in[:, cb * 512:(cb + 1) * 512].rearrange("(k p) c -> p k c", p=128)
            stq(Wi[cb], src, w=[("Wi", cb)])

    def prepass_rest():
        for cb in range(2):
            src = w_out[:, cb * 512:(cb + 1) * 512].rearrange("(k p) c -> p k c", p=128)
            stq(Wo[cb], src, w=[("Wo", cb)])
        for gb in range(11):
            stq(Wgu[gb][:, :, 0:256], wg[:, gb * 256:(gb + 1) * 256].rearrange("(k p) c -> p k c", p=128), w=[("Wgu", gb, 0)])
            stq(Wgu[gb][:, :, 256:512], wu[:, gb * 256:(gb + 1) * 256].rearrange("(k p) c -> p k c", p=128), w=[("Wgu", gb, 1)])
        for half in range(2):
            for bi, (k0, k1) in enumerate(((0, 8), (8, 16), (16, 22))):
                src = wd[k0 * 128:k1 * 128, half * 512:(half + 1) * 512].rearrange("(k p) c -> p k c", p=128)
                stq(Wd[half][:, k0:k1, :], src, w=[("Wd", half, bi)])

    wcnt = [0]

    def wslot(nk, ncol):
        r = wcnt[0] % NW
        wcnt[0] += 1
        view = wring[:, r, 0:nk * ncol].rearrange("p (k c) -> p k c", k=nk)
        return view, ("wr", r)

    def load_w(src, nk, ncol, rkeys):
        view, key = wslot(nk, ncol)
        ld(view, src, r=rkeys, w=[key])
        return view, key

    pf = {}
    WSRC = {}
    for cb_ in range(7):
        WSRC[("Wi", cb_)] = (Wi[cb_], 8, 512, [("Wi", cb_)])
    for cb_ in range(2):
        WSRC[("Wo", cb_)] = (Wo[cb_], 8, 512, [("Wo", cb_)])
    for gb_ in range(11):
        WSRC[("Wgu", gb_)] = (Wgu[gb_], 8, 512, [("Wgu", gb_, 0), ("Wgu", gb_, 1)])
    for half_ in range(2):
        for bi_, (k0_, k1_) in enumerate(((0, 8), (8, 16), (16, 22))):
            WSRC[("Wd", half_, bi_)] = (Wd[half_][:, k0_:k1_, :], k1_ - k0_, 512, [("Wd", half_, bi_)])

    def prefetch(name):
        if name not in pf:
            pf[name] = load_w(*WSRC[name])

    def get_w(name):
        if name in pf:
            return pf.pop(name)
        return load_w(*WSRC[name])

    def setup():
        ld(kc[:], kc_d[:], w=["kc"])
        ld(dco[:], dco_d[:], w=["dco"])
        ld(crow5, crow_d[:], w=[("xf", 0)])
        ld(smallr[0:1, 0:256], lamv[:], w=["lamr"])
        ld(smallr[0:1, 256:384], gsub[:], w=["gsr"])
        ld(smallr[0:1, 384:512], gqk[:], w=["gqr"])
        rbt = sb("rbt", [32, 4], F32)
        oh15 = lg[0:32, 1, 0:128]
        ohr = lg[0:32, 0, 0:384]
        ld(rbt[:], rbt_d[:], w=["rbt"])
        ld(oh15, oh15_d[:], w=[("lg", 1)])
        ld(ohr, ohr_d[:], w=[("lg", 0)])
        maskadd = et[:, 3, 0:256]
        ld(maskadd, maskadd_d[:], w=["et3"])
        P.op("dve", lambda e: e.memset(onesb[:], 1.0), writes=["onesb"])
        P.op("dve", lambda e: e.memset(warmb[:], 1.0), writes=["warmb"])
        P.op("dve", lambda e: e.memset(onesf[:], 1.0), writes=["onesf"])
        P.op("dve", lambda e: e.memset(cols[:, EPSC:EPSC + 1], EPS), writes=["epsc"])
        P.op("dve", lambda e: e.memset(S[:], 0.0), writes=["S"])
        P.op("pool", lambda e: e.memset(qAT[:].rearrange("p m h t -> p (m h t)"), 0.0), writes=[("qAT", h) for h in range(4)])
        cp(identb[:], identf, ["kc"], ["identb"])
        cp(trib[:], trif, ["kc"], ["trib"])
        tt(lamt[0:1, 0:64], smallr[0:1, 0:64], smallr[0:1, 64:128], ALU.mult, ["lamr"], ["st_a"])
        tt(lamt[0:1, 64:128], smallr[0:1, 128:192], smallr[0:1, 192:256], ALU.mult, ["lamr"], ["st_a"])
        lam2 = sb("lam2", [1, 8], F32)
        P.op("dve", lambda e: e.tensor_reduce(out=lam2[0:1, 0:2], in_=lamt[0:1, 0:128].rearrange("p (a j) -> p a j", a=2), axis=AX.X, op=ALU.add),
             reads=["st_a"], writes=["lam2a"])
        act(lam2[0:1, 2:4], lam2[0:1, 0:2], AF.Exp, ["lam2a"], ["lam2b"])
        tt(lam2[0:1, 4:5], lam2[0:1, 3:4], lam2[0:1, 2:3], ALU.subtract, ["lam2b"], ["lam2c"])
        ts(lam2[0:1, 5:6], lam2[0:1, 4:5], -LAM_INIT, None, ALU.add, None, ["lam2c"], ["lam2d"])
        P.op("dve", lambda e: e.memset(lam2[0:1, 6:7], 1.0 - LAM_INIT), writes=["lam2e"])
        pm = psb[7]
        mm(pm[:, 0:1], onesf[0:1, 0:128], lam2[0:1, 5:6], True, True, ["onesf", "lam2d"], [PS(7)])
        mm(pm[:, 1:2], smallr[0:1, 256:384], lam2[0:1, 6:7], True, True, ["gsr", "lam2e"], [PS(7)])
        mm(pm[:, 2:6], oh15, rbt[:, :], True, True, [("lg", 1), "rbt"], [PS(7)])
        mm(pm[:, 128:256], onesf[0:1, 0:128], smallr[0:1, 384:512], True, True, ["onesf", "gqr"], [PS(7)])
        cp(cols[:, NLAM:NLAM + 2], pm[:, 0:2], [PS(7)], ["nlam", "gsubc"])
        cp(cols[:, C15:C15 + 4], pm[:, 2:6], [PS(7)], ["c15"])
        cp(gqkb[:], pm[:, 128:256], [PS(7)], ["gqkb"])
        for i in range(2):
            for g in range(8):
                cp(gfull[:, i, g * 64:(g + 1) * 64], gqkb[:, i * 64:(i + 1) * 64], ["gqkb"], ["gfull"])
        ts(cols[:, CC0:CC0 + 4], cols[:, C15:C15 + 4], dco[:, 21:22], dco[:, 22:23], ALU.mult, ALU.add, ["c15", "dco"], ["cc0"])
        ts(cols[:, CC4:CC4 + 4], cols[:, C15:C15 + 4], dco[:, 24:25], dco[:, 25:26], ALU.mult, ALU.add, ["c15", "dco"], ["cc4"])
        ts(cols[:, CCF:CCF + 4], cols[:, C15:C15 + 4], dco[:, 26:27], dco[:, 27:28], ALU.mult, ALU.add, ["c15", "dco"], ["ccf"])
        pu = psb[6]
        mm(pu[0:4, 0:384], rbt[:, :], ohr, True, True, ["rbt", ("lg", 0)], [PS(6)])
        urs = et[0:4, 2, 0:384]
        cp(urs, pu[0:4, 0:384], [PS(6)], ["et2"])
        stq(urd[:], urs, r=["et2"], w=["urd"])
        hk = et[:, 0:2, :].rearrange("p a (b c) -> p (a b) c", c=256)
        ld(hk, AP(urd.tensor, 0, [[1, 128], [384, 4], [1, 256]]), r=["urd"], w=["et0", "et1"])
        for h in range(4):
            pj = psb[h % 2]
            mm(pj[:, 0:256], Jf, hk[:, h, :], True, True, ["kc", "et0", "et1"], [PS(h % 2)])
            tt(Mh[:, h, :], pj[:, 0:256], maskadd, ALU.add, [PS(h % 2), "et3"], [("Mh", h)])
            ts(G0[:, h, :], Mh[:, h, 128:256], dco[:, 20:21], cols[:, CC0 + h:CC0 + h + 1], ALU.mult, ALU.add, [("Mh", h), "dco", "cc0"], [("G0", h)])
            ts(G4[:, h, :], Mh[:, h, 128:256], dco[:, 23:24], cols[:, CC4 + h:CC4 + h + 1], ALU.mult, ALU.add, [("Mh", h), "dco", "cc4"], [("G4", h)])
        ld(MS[0:32, :, :], Mh[0:32, :, 0:32], r=[("Mh", h) for h in range(4)], w=["MS0"])
        ld(MS[32:64, :, :], Mh[0:32, :, 0:32], r=[("Mh", h) for h in range(4)], w=["MS1"])

        act(crow5[0:3, :], crow5[0:3, :], AF.Silu, [("xf", 0)], [("xf", 0)])
        pt = psb[5]
        for k in range(8):
            trn(pt[:, k * 5:(k + 1) * 5], crow5[0:5, k * 128:(k + 1) * 128], identf[0:5, 0:5], [("xf", 0), "kc"], [PS(5)])
        cp(cT[:].rearrange("p k v -> p (k v)"), pt[:, 0:40], [PS(5)], ["cT"])
        cp(scT[:], cT[:, :, 0:3], ["cT"], ["scT"])
        cp(scB[:], fr(cT[:, 0, 0:1], [[5, 8], [0, 128]]), ["cT"], ["scB"])
        cp(scS[:, :, 0:32], fr(cT[:, 0, 1:2], [[5, 8], [0, 32]]), ["cT"], ["scS0"])
        cp(scS[:, :, 32:64], fr(cT[:, 0, 2:3], [[5, 8], [0, 32]]), ["cT"], ["scS1"])
        psm = psb[4]
        psmv = psm[:, 0:96].rearrange("p (a c v) -> p a c v", a=4, c=8)
        for kind, base in ((0, 0), (1, 2), (2, 6), (3, 8)):
            for half in range(2):
                cbk = base + half
                view, key = wslot(8, 512)
                stq(view, w_ada[:, cbk * 512:(cbk + 1) * 512].rearrange("(k p) c -> p k c", p=128), w=[key])
                bslot = 0
                ld(browb[0:1, bslot, :], b_ada[0:1, cbk * 512:(cbk + 1) * 512], w=[("brow", bslot)])
                for e in range(4):
                    c = half * 4 + e
                    for k in range(8):
                        mm(psmv[:, kind, c, :], view[:, k, e * 128:(e + 1) * 128], scT[:, k, :], k == 0, False, [key, "scT"], [PS(4)])
                    mm(psmv[:, kind, c, :], browb[0:1, bslot, e * 128:(e + 1) * 128], onesf[0:1, 0:3], False, True, [("brow", bslot), "onesf"], [PS(4)])
        cp(modc[:, 0], psmv[:, 0], [PS(4)], ["modc0"])
        cp(modc[:, 2], psmv[:, 2], [PS(4)], ["modc2"])
        for c in range(8):
            ts(modc[:, 1, c, :], psmv[:, 1, c, :], 1.0, cT[:, c, 3:4], ALU.add, ALU.mult, [PS(4), "cT"], ["modc1"])
            ts(modc[:, 3, c, :], psmv[:, 3, c, :], 1.0, cT[:, c, 4:5], ALU.add, ALU.mult, [PS(4), "cT"], ["modc3"])

    def gates(sample):
        rows = 64 if sample else 128
        for gi, (gt, base) in enumerate(((gate1b, 4), (gate2b, 10))):
            for half in range(2):
                cbk = base + half
                view, key = wslot(8, 512)
                stq(view, w_ada[:, cbk * 512:(cbk + 1) * 512].rearrange("(k p) c -> p k c", p=128), w=[key])
                bslot = 0
                ld(browb[0:1, bslot, :], b_ada[0:1, cbk * 512:(cbk + 1) * 512], w=[("brow", bslot)])
                pg = psb[(gi * 2 + half) % 4]
                pk = PS((gi * 2 + half) % 4)
                for k in range(8):
                    lhs = scS[:, k, :] if sample else scB[:, k, :]
                    mm(pg[0:rows, :], lhs, view[:, k, :], k == 0, False, [key, "scB", "scS0", "scS1"], [pk])
                mm(pg[0:rows, :], onesf[0:1, 0:rows], browb[0:1, bslot, :], False, True, [("brow", bslot), "onesf"], [pk])
                act(gt[0:rows, half * 512:(half + 1) * 512], pg[0:rows, :], AF.Copy, [pk], [("gate", gi, half)])

    acc_rr = [0]

    def next_acc():
        i = acc_rr[0] % 4
        acc_rr[0] += 1
        return psb[i], PS(i)

    tb_rr = [0]

    def next_tb():
        i = tb_rr[0] % 2
        tb_rr[0] += 1
        return psbb[4 + i], PS(4 + i)

    def rstd_small(dst, src, scale, rows, rk, key):
        act(dst, src, AF.Ln, list(rk) + ["epsc"], [key], scale=scale, bias=cols[0:rows, EPSC:EPSC + 1])
        act(dst, dst, AF.Exp, [key], [key], scale=-0.5)

    def norm_part(xv, xk, rows, subs, sc0, junk_pm=False):
        n = len(subs)
        if junk_pm:
            junk = Pm[0:rows, 0:2, :].rearrange("p a c -> p (a c)")
            jk = [("Pm", 0), ("Pm", 1)]
        else:
            junk = lg[0:rows, :, :].rearrange("p a c -> p (a c)")
            jk = [("lg", 0), ("lg", 1)]
        for i, s in enumerate(subs):
            act(junk, xv[i], AF.Square, [xk[i]], jk + [("ssq", sc0 + i)], accum=stat[0:rows, sc0 + i:sc0 + i + 1])
        rstd_small(stat[0:rows, 8 + sc0:8 + sc0 + n], stat[0:rows, sc0:sc0 + n], 1.0 / D, rows,
                   [("ssq", sc0 + i) for i in range(n)], ("rstd", sc0))
        for i, s in enumerate(subs):
            ts(xn[0:rows, s, :], xv[i], stat[0:rows, 8 + sc0 + i:9 + sc0 + i], None, ALU.mult, None, [xk[i], ("rstd", sc0)], [("xn", s)])

    def transp_part(rows, nsub, kG, kS, sample):
        T = rows * nsub
        mk = ["modc0", "modc1", "modc2", "modc3"]
        for cpair in range(4):
            tbb, tk = next_tb()
            for half in range(2):
                c = 2 * cpair + half
                tb = tbb[:, half * 512:(half + 1) * 512]
                for s in range(nsub):
                    trn(tb[:, s * rows:(s + 1) * rows], xn[0:rows, s, c * 128:(c + 1) * 128], identb[0:rows, 0:rows], [("xn", s), "identb"], [tk])
            for half in range(2):
                c = 2 * cpair + half
                tb = tbb[:, half * 512:(half + 1) * 512]
                if not sample:
                    act(hT[:, c, 0:T], tb[:, 0:T], AF.Identity, [tk] + mk, [("hT", c)],
                        scale=modc[:, kG, c, 0:1], bias=modc[:, kS, c, 0:1])
                else:
                    for q in range(2):
                        act(hT[:, c, q * 32:(q + 1) * 32], tb[:, q * 32:(q + 1) * 32], AF.Identity, [tk] + mk, [("hT", c)],
                            scale=modc[:, kG, c, 1 + q:2 + q], bias=modc[:, kS, c, 1 + q:2 + q])

    def phase_norm(xviews, xkeys, rows, nsub, kG, kS, sample):
        if not sample:
            warm(WARM_NORM, 7)
        norm_part(xviews, xkeys, rows, list(range(nsub)), 0)
        transp_part(rows, nsub, kG, kS, sample)

    def hT_keys(sample):
        return [("hT", c) for c in range(8)]

    def qknorm(acc, ak, rows, goff, dest, dk, par=0):
        e0, e1 = 2 * par, 2 * par + 1
        k0, k1 = f"et{e0}", f"et{e1}"
        sc = 16 + 8 * par
        rk = ("r8", par)
        sq = et[0:rows, e0, :]
        act(sq, acc[0:rows, :], AF.Square, [ak], [k0])
        P.op("dve", lambda e: e.tensor_reduce(out=stat[0:rows, sc:sc + 8], in_=sq.rearrange("p (g j) -> p g j", g=8), axis=AX.X, op=ALU.add),
             reads=[k0], writes=[rk])
        rstd_small(stat[0:rows, sc:sc + 8], stat[0:rows, sc:sc + 8], 1.0 / 64, rows, [rk], rk)
        t1 = et[0:rows, e1, :]
        tt(t1.rearrange("p (g j) -> p g j", g=8), acc[0:rows, :].rearrange("p (g j) -> p g j", g=8),
           fr(stat[0:rows, sc:sc + 1], [[1, 8], [0, 64]]), ALU.mult, [ak, rk], [k1])
        tt(dest, t1, gfull[0:rows, goff // 64, :], ALU.mult, [k1, "gfull"], [dk])

    def rope(acc, ak, rows, tabC, tabS, tkey, dest, dk, par=0):
        e0, e1 = 2 * par, 2 * par + 1
        k0, k1 = f"et{e0}", f"et{e1}"
        xe = acc[0:rows, 0:512:2]
        xo_ = acc[0:rows, 1:512:2]
        t1 = et[0:rows, e0, 0:256]
        t2 = et[0:rows, e0, 256:512]
        t3 = et[0:rows, e1, 0:256]
        t4 = et[0:rows, e1, 256:512]
        tt(t1, xe, tabC, ALU.mult, [ak, tkey], [k0])
        tt(t2, xo_, tabS, ALU.mult, [ak, tkey], [k0])
        tt(dest[:, 0:512:2], t1, t2, ALU.subtract, [k0], [dk])
        tt(t3, xe, tabS, ALU.mult, [ak, tkey], [k1])
        tt(t4, xo_, tabC, ALU.mult, [ak, tkey], [k1])
        tt(dest[:, 1:512:2], t3, t4, ALU.add, [k1], [dk])

    def transp_heads(src, skeys, rows, nsub, dest, dname, split=False):
        T = rows * nsub
        for hp in range(2):
            tbb, tk = next_tb()
            for half in range(2):
                h = 2 * hp + half
                tb = tbb[:, half * 512:(half + 1) * 512]
                for s in range(nsub):
                    trn(tb[:, s * rows:(s + 1) * rows], src[0:rows, s, h * 128:(h + 1) * 128], identb[0:rows, 0:rows], list(skeys(s)) + ["identb"], [tk])
            for half in range(2):
                h = 2 * hp + half
                if split:
                    act(dest[0:64, 0, h, 0:T], tbb[0:64, half * 512:half * 512 + T], AF.Copy, [tk], [(dname, h)])
                    act(dest[64:128, 1, h, 0:T], tbb[64:128, half * 512:half * 512 + T], AF.Copy, [tk], [(dname, h)])
                else:
                    act(dest[:, h, 0:T], tbb[:, half * 512:half * 512 + T], AF.Copy, [tk], [(dname, h)])

    def in_proj(rows, nsub, own, sample, tabsrc, tile_i, hook_after_first=None, next_w=None):
        T = rows * nsub
        hk_ = hT_keys(sample)
        cbs = (1, 2, 0, 4, 5, 3, 6) if own else (1, 2, 4, 5)
        pending = [None]

        def flush():
            if pending[0] is not None:
                pending[0]()
                pending[0] = None
        for ci, cb in enumerate(cbs):
            W, wkey = get_w(("Wi", cb))
            if ci + 1 < len(cbs):
                prefetch(("Wi", cbs[ci + 1]))
            elif next_w is not None:
                prefetch(next_w)
            if cb == 6:
                for e in range(4):
                    acc, ak = next_acc()
                    for k in range(8):
                        mm(acc[:, 0:T], W[:, k, e * 128:(e + 1) * 128], hT[:, k, 0:T], k == 0, k == 7, [wkey] + hk_, [ak])
                    act(sgT[:, e, 0:T], acc[:, 0:T], AF.Silu, [ak], [("sgT", e)])
                flush()
                continue
            for s in range(nsub):
                acc, ak = next_acc()
                for k in range(8):
                    mm(acc[0:rows, :], hT[:, k, s * rows:(s + 1) * rows], W[:, k, :], k == 0, k == 7, [wkey] + hk_, [ak])
                chk(31)
                if cb == 0:
                    qknorm(acc, ak, rows, 0, tokb[0:rows, s, :], ("tokb", s), s % 2)
                elif cb == 1:
                    if own:
                        kfv = kf[0:rows, 0, :]
                        qknorm(acc, ak, rows, 64, kfv, "kf", s % 2)
                        if sample:
                            out_ops.append(stq(ksm[:, :], kfv, r=["kf"]))
                        else:
                            out_ops.append(stq(kp[tile_i, s * 128:(s + 1) * 128, :], kfv, r=["kf"]))
                        act(tokb[0:rows, s, :], kfv, AF.Copy, ["kf"], [("tokb", s)])
                    else:
                        qknorm(acc, ak, rows, 64, tokb[0:rows, s, :], ("tokb", s), s % 2)
                elif cb == 2:
                    if own:
                        vfv = vf[0:rows, 0, :]
                        act(vfv, acc[0:rows, :], AF.Copy, [ak], ["vf"])
                        if sample:
                            out_ops.append(stq(vsm[:, :], vfv, r=["vf"]))
                        else:
                            out_ops.append(stq(vp[tile_i, s * 128:(s + 1) * 128, :], vfv, r=["vf"]))
                        cp(vst[0:rows, s, :], vfv, ["vf"], [("vst", s)])
                    else:
                        evac(vst[0:rows, s, :], acc[0:rows, :], [ak], [("vst", s)])
                elif cb in (3, 4):
                    rs_ = s % 2
                    ld(rt[0:rows, rs_], tabsrc(s), w=[("rt", rs_)])
                    if cb == 3:
                        rope(acc, ak, rows, rt[0:rows, rs_, 0, :], rt[0:rows, rs_, 1, :], ("rt", rs_), tokb[0:rows, s, :], ("tokb", s), s % 2)
                    else:
                        rope(acc, ak, rows, rt[0:rows, rs_, 2, :], rt[0:rows, rs_, 3, :], ("rt", rs_), ke[0:rows, s, :], ("ke", s), s % 2)
                elif cb == 5:
                    evac(vR[0:rows, s, :], acc[0:rows, :], [ak], [("vR", s)])
            if hook_after_first is not None and cb == cbs[0]:
                hook_after_first()
            flush()
            if cb == 0:
                pending[0] = lambda: transp_heads(tokb, lambda s: [("tokb", s)], rows, nsub, qAT, "qAT", split=True)
            elif cb == 1:
                pending[0] = lambda: transp_heads(tokb, lambda s: [("tokb", s)], rows, nsub, kst, "kst")
            elif cb == 3:
                pending[0] = lambda: transp_heads(tokb, lambda s: [("tokb", s)], rows, nsub, qdT, "qdT")
            elif cb == 4 and own:
                pending[0] = lambda: transp_heads(ke, lambda s: [("ke", s)], rows, nsub, keT, "keT")
        flush()

    def store_kv(unit):
        stq(KT[:, unit].rearrange("h p c -> p h c"), kst[:, :, :], r=[("kst", h) for h in range(4)], w=[("KT", unit)])
        for s in range(4):
            dst = VS[:, unit, :, s, :].rearrange("h p d -> p h d")
            stq(dst, vst[:, s, :].rearrange("p (h d) -> p h d", h=4), r=[("vst", s)], w=[("VS", unit, s)])

    def u_raw(rows, nsub, pbase, pu, pk):
        for h in range(4):
            for s in range(nsub):
                mm(pu[:, h * 128:(h + 1) * 128], ke[pbase:pbase + rows, s, h * 128:(h + 1) * 128], vR[pbase:pbase + rows, s, h * 128:(h + 1) * 128],
                   s == 0, s == nsub - 1, [("ke", s), ("vR", s)], [pk])

    def retention_prompt():
        def scores(h):
            for j in range(4):
                pj, pk = next_acc()
                n = 512 - 128 * j
                mm(pj[:, 0:n], keT[:, h, 128 * j:128 * j + 128], qdT[:, h, 128 * j:512], True, True, [("keT", h), ("qdT", h)], [pk])
                tt(scb[:, j, 0:128], pj[:, 0:128], trib[:, :], ALU.mult, [pk, "trib"], [("Pm", j)])
                if n > 128:
                    cp(scb[:, j, 128:n], pj[:, 128:n], [pk], [("Pm", j)])

        def av(h, po, pok):
            mm(po[:, :], Sb[:, h, :], qdT[:, h, :], True, False, ["Sb", ("qdT", h)], [pok])
            for j in range(4):
                n = 512 - 128 * j
                mm(po[:, 128 * j:512], vR[:, j, h * 128:(h + 1) * 128], scb[:, j, 0:n], False, j == 3,
                   [("vR", j), ("Pm", j)], [pok])

        scores(0)
        for h in range(4):
            po, pok = psb[4 + h % 2], PS(4 + h % 2)
            av(h, po, pok)
            if h + 1 < 4:
                scores(h + 1)
            ret_epilogue(po, pok, h, 512)

    def ret_epilogue(po, pok, h, T):
        sq = et[:, 2, 0:T]
        act(sqb[:, 0:T], po[:, 0:T], AF.Square, [pok], ["sqb"])
        pss, psk = psb[7], PS(7)
        mm(pss[:, 0:T], onesb[:, :], sqb[:, 0:T], True, True, ["onesb", "sqb"], [psk])
        rs_ = et[:, 3, 0:T]
        act(rs_, pss[:, 0:T], AF.Ln, [psk, "epsc"], ["et3"], scale=1.0 / 128, bias=cols[:, EPSC:EPSC + 1])
        act(rs_, rs_, AF.Exp, ["et3"], ["et3"], scale=-0.5)
        tt(sq, po[:, 0:T], rs_, ALU.mult, [pok, "et3", "et2"], ["et2"])
        tt(oT[:, 4 + h, 0:T], sq, sgT[:, h, 0:T], ALU.mult, ["et2", ("sgT", h)], [("oT", 4 + h)])

    kv_rr = [0]

    def attention(qc0, N, units, h, first_full=True):
        pO = [psb[4], psb[5]]
        pOk = [PS(4), PS(5)]
        pS = [psb[6], psb[7]]
        pSk = [PS(6), PS(7)]
        tiles = []
        for u in units:
            for (kt, c0, kind, arg) in u["tiles"]:
                tiles.append((u, kt, c0, kind, arg))
        n_tiles = len(tiles)
        info = {}

        def unit_ops(u):
            if id(u) in info:
                return info[id(u)]
            if u.get("sbuf"):
                ktv, vsv = u["ktv"], u["vsv"]
                r = (lambda m, kt: ktv, lambda kt: vsv, u["kkeys"], u["vkeys"], u["pbase"], u["nkeys"])
            else:
                slot = kv_rr[0] % NKB
                kv_rr[0] += 1
                ld(ktb[:, slot, :], u["kt_src"], r=u["kt_keys"], w=[("ktb", slot)])
                ld(vsb[:, slot, :].rearrange("p (k d) -> p k d", k=4), u["vs_src"], r=u["vs_keys"], w=[("vsb", slot)])
                r = (lambda m, kt: ktb[:, slot, kt * 128:(kt + 1) * 128],
                     lambda kt: vsb[:, slot, kt * 128:(kt + 1) * 128], [("ktb", slot)], [("vsb", slot)], 0, 128)
            info[id(u)] = r
            return r

        def emit_qk(t):
            u, kt, c0, kind, arg = tiles[t]
            lk, lv, kkeys, vkeys, pb, nkeys = unit_ops(u)
            pr = slice(pb, pb + nkeys)
            for m in range(2):
                bi = 2 * (t % 2) + m
                mm(psb[bi][pr, c0:N], lk(m, kt), qAT[:, m, h, qc0 + c0:qc0 + N], True, True, kkeys + [("qAT", h)], [PS(bi)])

        def emit_exp(t):
            u, kt, c0, kind, arg = tiles[t]
            lk, lv, kkeys, vkeys, pb, nkeys = unit_ops(u)
            pr = slice(pb, pb + nkeys)
            n = N - c0
            for m in range(2):
                bi = 2 * (t % 2) + m
                pq, pqk = psb[bi], PS(bi)
                pm = Pm[pr, bi, :]
                pmk = ("Pm", bi)
                if kind == "const":
                    if m == 0:
                        b0 = 2 * (t % 2)
                        act(Pm[pr, b0:b0 + 2, c0:N], psall[pr, b0:b0 + 2, c0:N], AF.Exp, [PS(b0), PS(b0 + 1), "c15", "cc0", "cc4", "ccf"],
                            [("Pm", b0), ("Pm", b0 + 1)], scale=0.125, bias=arg[pr, :])
                else:
                    btile, w_, ccol = arg
                    w_ = min(w_, n)
                    lgv = lg[pr, m, 0:w_]
                    stt(lgv, pq[pr, c0:c0 + w_], 0.125, btile[:, 0:w_], ALU.mult, ALU.add, [pqk] + u.get("bkeys", []), [("lg", m)])
                    act(pm[:, c0:c0 + w_], lgv, AF.Exp, [("lg", m)], [pmk])
                    if w_ < n:
                        act(pm[:, c0 + w_:N], pq[pr, c0 + w_:N], AF.Exp, [pqk, "c15", "cc0", "cc4", "ccf"], [pmk], scale=0.125, bias=ccol[pr, :])

        def emit_av(t):
            u, kt, c0, kind, arg = tiles[t]
            lk, lv, kkeys, vkeys, pb, nkeys = unit_ops(u)
            pr = slice(pb, pb + nkeys)
            first, last = (t == 0), (t == n_tiles - 1)
            for m in range(2):
                bi = 2 * (t % 2) + m
                mm(pO[m][:, c0:N], lv(kt), Pm[pr, bi, c0:N], first, last, vkeys + [("Pm", bi)], [pOk[m]])
            for m in range(2):
                bi = 2 * (t % 2) + m
                mm(pS[m][:, c0:N], onesb[pr, :], Pm[pr, bi, c0:N], first, last, ["onesb", ("Pm", bi)], [pSk[m]])

        emit_qk(0)
        for t in range(n_tiles):
            emit_exp(t)
            if t + 1 < n_tiles:
                emit_qk(t + 1)
            emit_av(t)
        T = N
        rinv = lg[:, :, 0:T]
        act(rinv, psall[:, 6:8, 0:T], AF.Ln, [pSk[0], pSk[1]], [("lg", 0), ("lg", 1)])
        act(rinv, rinv, AF.Exp, [("lg", 0), ("lg", 1)], [("lg", 0), ("lg", 1)], scale=-1.0)
        r0 = lg[:, 0, 0:T]
        r1 = lg[:, 1, 0:T]
        a_ = et[:, 1, 0:T]
        b_ = et[:, 2, 0:T]
        tt(a_, pO[0][:, 0:T], r0, ALU.mult, [pOk[0], ("lg", 0)], ["et1"])
        tt(b_, pO[1][:, 0:T], r1, ALU.mult, [pOk[1], ("lg", 1)], ["et2"])
        stt(a_, b_, cols[:, NLAM:NLAM + 1], a_, ALU.mult, ALU.add, ["et1", "et2", "nlam"], ["et1"])
        act(sqb[:, 0:T], a_, AF.Square, ["et1", "et2"], ["sqb"])
        if N == 512:
            warm(WARM_EPI, 3)
        pss, psk = psb[0], PS(0)
        mm(pss[:, 0:T], onesb[:, :], sqb[:, 0:T], True, True, ["onesb", "sqb"], [psk])
        rs_ = et[:, 3, 0:T]
        act(rs_, pss[:, 0:T], AF.Ln, [psk, "epsc"], ["et3"], scale=1.0 / 128, bias=cols[:, EPSC:EPSC + 1])
        act(rs_, rs_, AF.Exp, ["et3"], ["et3"], scale=-0.5)
        stt(oT[:, h, qc0:qc0 + T], a_, cols[:, GSUB:GSUB + 1], rs_, ALU.mult, ALU.mult, ["et1", "et3", "gsubc"], [("oT", h)])

    def prompt_units(step, h):
        units = []
        c15c = cols[:, C15 + h:C15 + h + 1]
        cc0c = cols[:, CC0 + h:CC0 + h + 1]
        cc4c = cols[:, CC4 + h:CC4 + h + 1]
        ccfc = cols[:, CCF + h:CCF + h + 1]
        for u in range(2 * step + 2):
            d = dict(kt_src=KT[h, u], vs_src=VS[h, u], kt_keys=[("KT", u)], vs_keys=[("VS", u, s) for s in range(4)])
            if u == 2 * step + 1:
                d["tiles"] = [(kt, 128 * kt, "bias", (Mh[:, h, :], 256, c15c)) for kt in range(4)]
                d["bkeys"] = [("Mh", h)]
            elif u == 2 * step:
                d["tiles"] = [(kt, 0, "const", ccfc) for kt in range(3)] + [(3, 0, "bias", (G4[:, h, :], 128, cc4c))]
                d["bkeys"] = [("G4", h)]
            elif u == 2 * step - 2:
                d["tiles"] = [(kt, 0, "const", c15c) for kt in range(3)] + [(3, 0, "bias", (G0[:, h, :], 128, cc0c))]
                d["bkeys"] = [("G0", h)]
            else:
                d["tiles"] = [(kt, 0, "const", c15c) for kt in range(4)]
            units.append(d)
        return units

    def out_proj(rows, nsub, xviews, xkeys):
        okeys = [("oT", i) for i in range(8)]
        for cb2 in range(2):
            W, wkey = get_w(("Wo", cb2))
            for s in range(nsub):
                acc, ak = next_acc()
                for k in range(8):
                    mm(acc[0:rows, :], oT[:, k, s * rows:(s + 1) * rows], W[:, k, :], k == 0, k == 7, [wkey] + okeys, [ak])
                t = et[0:rows, (cb2 * nsub + s) % 2, :]
                tkey = f"et{(cb2 * nsub + s) % 2}"
                tt(t, acc[0:rows, :], gate1b[0:rows, cb2 * 512:(cb2 + 1) * 512], ALU.mult, [ak, ("gate", 0, cb2)], [tkey])
                xv = xviews[s][:, cb2 * 512:(cb2 + 1) * 512]
                tt(xv, t, xv, ALU.add, [tkey, xkeys[s]], [xkeys[s]])

    def ffn(rows, nsub, xviews, xkeys, sample, store, gb_hooks=None, mid_hook=None, next_w=None):
        T = rows * nsub
        hk_ = hT_keys(sample)
        for gb in range(11):
            W, wkey = get_w(("Wgu", gb))
            prefetch(("Wgu", gb + 1) if gb + 1 < 11 else ("Wd", 0, 0))
            if gb_hooks is not None and gb in gb_hooks:
                gb_hooks[gb]()
            for fc in range(2):
                pg, pgk = next_acc()
                pu, puk = next_acc()
                for k in range(8):
                    mm(pg[:, 0:T], W[:, k, fc * 128:(fc + 1) * 128], hT[:, k, 0:T], k == 0, k == 7, [wkey] + hk_, [pgk])
                for k in range(8):
                    mm(pu[:, 0:T], W[:, k, 256 + fc * 128:256 + (fc + 1) * 128], hT[:, k, 0:T], k == 0, k == 7, [wkey] + hk_, [puk])
                sgi = (2 * gb + fc) % 2
                sg = lg[:, sgi, 0:T]
                act(sg, pg[:, 0:T], AF.Silu, [pgk], [("lg", sgi)])
                tt(ffT[:, 2 * gb + fc, 0:T], pu[:, 0:T], sg, ALU.mult, [puk, ("lg", sgi)], [("ffT", 2 * gb + fc)])
        fkeys = [("ffT", i) for i in range(NKF)]
        if not sample:
            chk(10)
        if mid_hook is not None:
            mid_hook()
        for half in range(2):
            accs = [next_acc() for s in range(nsub)]
            for bi, (k0, k1) in enumerate(((0, 8), (8, 16), (16, 22))):
                W, wkey = get_w(("Wd", half, bi))
                nxt = ("Wd", half, bi + 1) if bi < 2 else (("Wd", 1, 0) if half == 0 else next_w)
                if nxt is not None:
                    prefetch(nxt)
                for s in range(nsub):
                    acc, ak = accs[s]
                    for kk in range(k0, k1):
                        mm(acc[0:rows, :], ffT[:, kk, s * rows:(s + 1) * rows], W[:, kk - k0, :], kk == 0, kk == NKF - 1, [wkey] + fkeys, [ak])
            for s in range(nsub):
                acc, ak = accs[s]
                t = et[0:rows, s % 2, :]
                tkey = f"et{s % 2}"
                tt(t, acc[0:rows, :], gate2b[0:rows, half * 512:(half + 1) * 512], ALU.mult, [ak, ("gate", 1, half)], [tkey])
                xv = xviews[s][:, half * 512:(half + 1) * 512]
                tt(xv, t, xv, ALU.add, [tkey, xkeys[s]], [xkeys[s]])
                if half == 1:
                    store(s)

    setup()
    prepass_in()
    gates(sample=False)

    def sample_cache_chunk(q, u):
        for h in range(4):
            src = cv[q, u * 512:(u + 1) * 512, h, :].rearrange("(kt p) d -> p kt d", p=128)
            stq(VSs[q, h, u], src, w=[("VSs", q, h, u)])
        kcb = vst
        stq(kcb[:, :, :], ck[q, u * 512:(u + 1) * 512, :].rearrange("(kt p) c -> p kt c", p=128), w=[("vst", s) for s in range(4)])
        for hp in range(2):
            tbb, tk = next_tb()
            for half in range(2):
                h = 2 * hp + half
                for kt in range(4):
                    trn(tbb[:, half * 512 + kt * 128:half * 512 + (kt + 1) * 128], kcb[:, kt, h * 128:(h + 1) * 128], identb[:, :], [("vst", kt), "identb"], [tk])
            for half in range(2):
                h = 2 * hp + half
                act(kst[:, h, :], tbb[:, half * 512:(half + 1) * 512], AF.Copy, [tk], [("kst", h)])
        stq(KTs[q, :, u].rearrange("h p c -> p h c"), kst[:, :, :], r=[("kst", h) for h in range(4)], w=[("KTs", q, u)])

    sample_chunks = [(q, u) for q in range(2) for u in range(4)]

    try:
        def foreign_sub(slot, s):
            ld(xf[:, 0, :], xs[slot, s * 128:(s + 1) * 128, :], w=[("xf", 0)])
            norm_part([xf[:, 0, :]], [("xf", 0)], 128, [s], 24 + s, junk_pm=True)

        def foreign_part1(slot):
            for s in range(4):
                foreign_sub(slot, s)

        def foreign_part2():
            transp_part(128, 4, 1, 0, False)

        if n_steps > 0:
            foreign_part1(0)
            foreign_part2()
        for step in range(n_steps):
            slotF, slotO = 2 * step, 2 * step + 1
            xviews = [xo[:, s, :] for s in range(4)]
            xkeys = [("xo", s) for s in range(4)]
            def own_norm(xviews=xviews, xkeys=xkeys, slotO=slotO):
                for s in range(4):
                    ld(xviews[s], xs[slotO, s * 128:(s + 1) * 128, :], w=[xkeys[s]])
                norm_part(xviews, xkeys, 128, [0, 1, 2, 3], 0)
            chk(1)
            in_proj(128, 4, False, False, lambda s, slot=slotF: rope_d[slot, s * 128:(s + 1) * 128], step, hook_after_first=own_norm, next_w=("Wi", 1))
            chk(2)
            store_kv(slotF)
            pu, pk = psb[7], PS(7)
            u_raw(128, 4, 0, pu, pk)
            cp(Uf[:].rearrange("p h e -> p (h e)"), pu[:, :], [pk], ["Uf"])
            for h in range(4):
                ts(tmpS[:, h, :], S[:, h, :], dco[:, h:h + 1], None, ALU.mult, None, ["S", "dco"], [("tmpS", h)])
                stt(Sst[:, h, :], Uf[:, h, :], dco[:, 4 + h:5 + h], tmpS[:, h, :], ALU.mult, ALU.add, ["Uf", "dco", ("tmpS", h)], [("Sst", h)])
            cp(Sb[:].rearrange("p h e -> p (h e)"), Sst[:].rearrange("p h e -> p (h e)"), [("Sst", h) for h in range(4)], ["Sb"])
            chk(3)
            transp_part(128, 4, 1, 0, False)
            chk(4)
            in_proj(128, 4, True, False, lambda s, slot=slotO: rope_d[slot, s * 128:(s + 1) * 128], step)
            chk(5)
            store_kv(slotO)
            if step == 0:
                prepass_rest()
            if do_sample and sample_chunks and (step >= 1 or n_steps == 1):
                for _ in range(8 if n_steps == 1 else (2 if len(sample_chunks) > 8 - step else 1)):
                    if sample_chunks:
                        sample_cache_chunk(*sample_chunks.pop(0))
            retention_prompt()
            pu, pk = psb[7], PS(7)
            u_raw(128, 4, 0, pu, pk)
            for h in range(4):
                ts(tmpS[:, h, :], S[:, h, :], dco[:, 8 + h:9 + h], None, ALU.mult, None, ["S", "dco"], [("tmpS", h)])
                stt(tmpS[:, h, :], Uf[:, h, :], dco[:, 16 + h:17 + h], tmpS[:, h, :], ALU.mult, ALU.add, ["Uf", "dco", ("tmpS", h)], [("tmpS", h)])
                stt(S[:, h, :], pu[:, h * 128:(h + 1) * 128], dco[:, 12 + h:13 + h], tmpS[:, h, :], ALU.mult, ALU.add, [pk, "dco", ("tmpS", h)], ["S"])
            chk(6)
            prefetch(("Wo", 0))
            prefetch(("Wo", 1))
            prefetch(("Wgu", 0))
            for h in range(4):
                attention(0, 512, prompt_units(step, h), h)
            chk(7)
            out_proj(128, 4, xviews, xkeys)
            chk(8)
            phase_norm(xviews, xkeys, 128, 4, 3, 2, False)
            chk(9)

            def store(s, step=step, xviews=xviews, xkeys=xkeys):
                out_ops.append(stq(y[step, s * 128:(s + 1) * 128, :], xviews[s], r=[xkeys[s]]))
            if step + 1 < n_steps:
                nslot = 2 * (step + 1)
                ffn(128, 4, xviews, xkeys, False, store,
                    gb_hooks={1 + 2 * s_: (lambda nslot=nslot, s_=s_: foreign_sub(nslot, s_)) for s_ in range(4)}, mid_hook=foreign_part2,
                    next_w=("Wi", 1))
            else:
                ffn(128, 4, xviews, xkeys, False, store)
        out_ops.append(stq(rp[:].rearrange("h d e -> d h e"), S[:, :, :], r=["S"]))

        if do_sample:
            while sample_chunks:
                sample_cache_chunk(*sample_chunks.pop(0))
            gates(sample=True)
            xviews = [xo[0:64, 0, :]]
            xkeys = [("xo", 0)]
            ld(xviews[0], xsm[:, :], w=[xkeys[0]])
            phase_norm(xviews, xkeys, 64, 1, 1, 0, True)
            in_proj(64, 1, True, True, lambda s: rope_sd[:, :, :], 0)
            SH = [Sst, tmpS]
            SHk = [[("Sst", h) for h in range(4)], [("tmpS", h) for h in range(4)]]
            for q in range(2):
                ld(SH[q][:, :, :], st[q].rearrange("h d e -> d h e"), w=SHk[q])
            for q in range(2):
                shb = Sb
                cp(shb[:].rearrange("p h e -> p (h e)"), SH[q][:].rearrange("p h e -> p (h e)"), SHk[q], ["Sb"])
                pr = slice(32 * q, 32 * q + 32)
                for h in range(4):
                    pj, pk = next_acc()
                    mm(pj[pr, 0:32], keT[:, h, 32 * q:32 * q + 32], qdT[:, h, 32 * q:32 * q + 32], True, True, [("keT", h), ("qdT", h)], [pk])
                    tt(scb[pr, 0, 0:32], pj[pr, 0:32], trib[pr, 32 * q:32 * q + 32], ALU.mult, [pk, "trib"], [("Pm", 0)])
                    po, pok = psb[6], PS(6)
                    mm(po[:, 0:32], shb[:, h, :], qdT[:, h, 32 * q:32 * q + 32], True, False, ["Sb", ("qdT", h)], [pok])
                    mm(po[:, 0:32], vR[pr, 0, h * 128:(h + 1) * 128], scb[pr, 0, 0:32], False, True, [("vR", 0), ("Pm", 0)], [pok])
                    sq = et[:, 2, 0:32]
                    act(sq, po[:, 0:32], AF.Square, [pok], ["et2"])
                    pss, psk = psb[7], PS(7)
                    mm(pss[:, 0:32], onesf[:, :], sq, True, True, ["onesf", "et2"], [psk])
                    rs_ = et[:, 3, 0:32]
                    act(rs_, pss[:, 0:32], AF.Ln, [psk, "epsc"], ["et3"], scale=1.0 / 128, bias=cols[:, EPSC:EPSC + 1])
                    act(rs_, rs_, AF.Exp, ["et3"], ["et3"], scale=-0.5)
                    tt(sq, po[:, 0:32], rs_, ALU.mult, [pok, "et3", "et2"], ["et2"])
                    tt(oT[:, 4 + h, 32 * q:32 * q + 32], sq, sgT[:, h, 32 * q:32 * q + 32], ALU.mult, ["et2", ("sgT", h)], [("oT", 4 + h)])
                pu, pk = psb[7], PS(7)
                u_raw(32, 1, 32 * q, pu, pk)
                for h in range(4):
                    tt(Uf[:, h, :], pu[:, h * 128:(h + 1) * 128], SH[q][:, h, :], ALU.add, [pk] + SHk[q], ["Uf"])
                    ts(Uf[:, h, :], Uf[:, h, :], dco[:, 28 + h:29 + h], None, ALU.mult, None, ["Uf", "dco"], ["Uf"])
                out_ops.append(stq(rs[q].rearrange("h d e -> d h e"), Uf[:, :, :], r=["Uf"]))
            for q in range(2):
                for h in range(4):
                    c15c = cols[:, C15 + h:C15 + h + 1]
                    units = []
                    for u in range(4):
                        d = dict(kt_src=KTs[q, h, u], vs_src=VSs[q, h, u], kt_keys=[("KTs", q, u)], vs_keys=[("VSs", q, h, u)])
                        if u < 3:
                            d["tiles"] = [(kt, 0, "const", c15c) for kt in range(4)]
                        else:
                            d["tiles"] = [(kt, 0, "const", c15c) for kt in range(3)] + [(3, 0, "bias", (Mh[:, h, 128:160], 32, None))]
                            d["bkeys"] = [("Mh", h)]
                        units.append(d)
                    pr = slice(32 * q, 32 * q + 32)
                    units.append(dict(sbuf=True, ktv=kst[:, h, 32 * q:32 * q + 32], vsv=vst[pr, 0, h * 128:(h + 1) * 128],
                                      kkeys=[("kst", h)], vkeys=[("vst", 0)], pbase=32 * q, nkeys=32,
                                      tiles=[(0, 0, "bias", (MS[pr, h, :], 32, None))], bkeys=["MS0", "MS1"]))
                    attention(32 * q, 32, units, h)
            out_proj(64, 1, xviews, xkeys)
            phase_norm(xviews, xkeys, 64, 1, 3, 2, True)

            def store_s(s):
                out_ops.append(stq(ysm[:, :], xviews[0], r=[xkeys[0]]))
            ffn(64, 1, xviews, xkeys, True, store_s)


    except _StopBuild:
        pass
    P.emit(final_wait_ops=out_ops)
    return nc, P


_PROG_CACHE = {}


def kernel(x_prompt, x_sample, c_prompt, c_sample, cache_k, cache_v, state_ret, w_ada, b_ada,
           g_norm1, g_norm2, w_in, g_q, g_k, lam_q1, lam_k1, lam_q2, lam_k2, g_subln, w_out,
           w_ff_gate, w_ff_up, w_ff_down, rel_bias):
    f = lambda a: np.ascontiguousarray(np.asarray(a, dtype=np.float32))
    x_prompt, x_sample, c_prompt, c_sample = f(x_prompt), f(x_sample), f(c_prompt), f(c_sample)
    cache_k, cache_v, state_ret = f(cache_k), f(cache_v), f(state_ret)
    C = _static_consts()
    if "nc" not in _PROG_CACHE:
        _PROG_CACHE["nc"] = build_program()
    nc, P = _PROG_CACHE["nc"]
    shared = dict(
        w_ada=f(w_ada)[0], b_ada=f(b_ada), w_in=f(w_in)[0],
        gqk=np.concatenate([f(g_q), f(g_k)], axis=1),
        lamv=np.concatenate([f(lam_q1), f(lam_k1), f(lam_q2), f(lam_k2)], axis=1),
        gsub=f(g_subln), w_out=f(w_out)[0], wg=f(w_ff_gate)[0], wu=f(w_ff_up)[0], wd=f(w_ff_down)[0],
        rbt=f(rel_bias), rope_s=C["rope_s"], ohr=C["ohr"], oh15=C["oh15"], maskadd=C["maskadd"], kc=C["kc"],
    )
    in_maps = []
    for c in range(8):
        b, p = c // 2, c % 2
        order = []
        for i in range(NT // 2):
            order += [2 * i + 1 - p, 2 * i + p]
        xb = x_prompt[b].reshape(NT, TT, D)[order]
        m = dict(shared)
        m.update(
            xs=np.ascontiguousarray(xb),
            xsm=np.ascontiguousarray(x_sample[2 * c:2 * c + 2].reshape(64, D)),
            crow=np.ascontiguousarray(np.concatenate([c_prompt[b:b + 1], c_sample[2 * c:2 * c + 2], f(g_norm1), f(g_norm2)], axis=0)),
            ck=np.ascontiguousarray(cache_k[0, 2 * c:2 * c + 2].reshape(2, PAST, 512)),
            cv=np.ascontiguousarray(cache_v[0, 2 * c:2 * c + 2]),
            st=np.ascontiguousarray(state_ret[0, 2 * c:2 * c + 2]),
            rope=C["rope"][p], dco=C["dco"][p],
        )
        in_maps.append(m)
    res = run_bass_kernel_spmd(nc, in_maps, core_ids=list(range(8)))
    R = res.results
    y_prompt = np.zeros((4, SEQ, D), np.float32)
    k_prompt = np.zeros((1, 4, SEQ, 4, 2, 64), np.float32)
    v_prompt = np.zeros((1, 4, SEQ, 4, 128), np.float32)
    ret_prompt = np.zeros((1, 4, 4, 128, 128), np.float32)
    y_sample = np.zeros((16, LS, D), np.float32)
    k_sample = np.zeros((1, 16, LS, 4, 2, 64), np.float32)
    v_sample = np.zeros((1, 16, LS, 4, 128), np.float32)
    ret_sample = np.zeros((1, 16, 4, 128, 128), np.float32)
    for c in range(8):
        b, p = c // 2, c % 2
        r = R[c]
        for i in range(NT // 2):
            t = 2 * i + p
            y_prompt[b, t * TT:(t + 1) * TT] = r["y"][i]
            k_prompt[0, b, t * TT:(t + 1) * TT] = r["kp"][i].reshape(TT, 4, 2, 64)
            v_prompt[0, b, t * TT:(t + 1) * TT] = r["vp"][i].reshape(TT, 4, 128)
        if p == 0:
            ret_prompt[0, b] = r["rp"]
        y_sample[2 * c:2 * c + 2] = r["ysm"].reshape(2, LS, D)
        k_sample[0, 2 * c:2 * c + 2] = r["ksm"].reshape(2, LS, 4, 2, 64)
        v_sample[0, 2 * c:2 * c + 2] = r["vsm"].reshape(2, LS, 4, 128)
        ret_sample[0, 2 * c:2 * c + 2] = r["rs"]
    return (y_prompt, y_sample, k_prompt, v_prompt, ret_prompt, k_sample, v_sample, ret_sample)
```

```python
import math
import numpy as np
import concourse.bass as bass
import concourse.mybir as mybir
from concourse.bass_utils import run_bass_kernel_spmd

F32 = mybir.dt.float32
BF16 = mybir.dt.bfloat16
AF = mybir.ActivationFunctionType
ALU = mybir.AluOpType
AX = mybir.AxisListType
AP = bass.AP

D = 1024
DIN = 3584
DFF = 2816
NKF = DFF // 128
SEQ = 8192
TT = 512
NT = SEQ // TT
PAST = 2048
LS = 32
EPS = 1e-6
LAM_INIT = 0.8 - 0.6 * math.exp(-0.3 * 0)
NEG = -30000.0
WARM_EPI = 24
WARM_NORM = 30
ENGS = ("pe", "act", "dve", "pool", "sp")


class Op:
    __slots__ = ("eng", "fn", "deps", "is_dma", "signal", "val", "sem", "waits", "know")

    def __init__(self, eng, fn, deps, is_dma):
        self.eng = eng
        self.fn = fn
        self.deps = deps
        self.is_dma = is_dma
        self.signal = is_dma
        self.val = None
        self.sem = None


class Prog:
    def __init__(self, nc, n_dma_sems=24):
        self.nc = nc
        self.ops = {e: [] for e in ENGS}
        self.all_ops = []
        self.bufs = {}
        self.n_dma_sems = n_dma_sems
        self.dma_rr = {"sp": 0, "pool": 0, "act": 0}
        self.dma_last = {}

    @staticmethod
    def _flat(keys):
        out = []
        for k in keys:
            if isinstance(k, list):
                out.extend(k)
            else:
                out.append(k)
        return out

    def _deps_for(self, reads, writes):
        deps = []
        for k in reads:
            ent = self.bufs.get(k)
            if ent is not None and ent[0] is not None:
                deps.append(ent[0])
        for k in writes:
            ent = self.bufs.get(k)
            if ent is not None:
                if ent[0] is not None:
                    deps.append(ent[0])
                deps.extend(ent[1])
        return deps

    def _commit(self, op, reads, writes):
        for k in reads:
            ent = self.bufs.setdefault(k, [None, []])
            if not op.is_dma:
                ent[1] = [r for r in ent[1] if r.is_dma or r.eng != op.eng]
            ent[1].append(op)
        for k in writes:
            self.bufs[k] = [op, []]

    def op(self, eng, fn, reads=(), writes=()):
        reads, writes = self._flat(reads), self._flat(writes)
        o = Op(eng, fn, self._deps_for(reads, writes), False)
        self.all_ops.append(o)
        self.ops[eng].append(o)
        self._commit(o, reads, writes)
        return o

    def dma(self, queue, fn, reads=(), writes=()):
        reads, writes = self._flat(reads), self._flat(writes)
        deps = self._deps_for(reads, writes)
        j = self.dma_rr[queue]
        nq = self.n_dma_sems if queue == "sp" else 4
        self.dma_rr[queue] = (j + 1) % nq
        prev = self.dma_last.get((queue, j))
        if prev is not None:
            deps.append(prev)
        o = Op(queue, fn, deps, True)
        o.sem = (queue, j)
        self.all_ops.append(o)
        self.ops[queue].append(o)
        self.dma_last[(queue, j)] = o
        self._commit(o, reads, writes)
        return o

    def emit(self, final_wait_ops=()):
        nc = self.nc
        for o in self.all_ops:
            for d in o.deps:
                if d.is_dma:
                    continue
                if d.eng == o.eng and d.eng == "pe" and not o.is_dma:
                    continue
                d.signal = True
        for o in final_wait_ops:
            if not o.is_dma:
                o.signal = True
        for e in ("pe", "act", "dve", "pool"):
            comp = [o for o in self.ops[e] if not o.is_dma]
            if comp:
                comp[-1].signal = True
        sem_h = {}
        for e in ("pe", "act", "dve", "pool"):
            sem_h[("eng", e)] = nc.alloc_semaphore(f"s_{e}")
        for q in ("sp", "pool", "act"):
            used = set(o.sem[1] for o in self.ops[q] if o.is_dma)
            for j in sorted(used):
                sem_h[(q, j)] = nc.alloc_semaphore(f"d_{q}{j}")
        cnt = {}
        for e in ENGS:
            for o in self.ops[e]:
                if o.is_dma:
                    k = o.sem
                    cnt[k] = cnt.get(k, 0) + 16
                    o.val = cnt[k]
                elif o.signal:
                    k = ("eng", e)
                    cnt[k] = cnt.get(k, 0) + 1
                    o.val = cnt[k]
                    o.sem = k
        self.stats = dict(n_ops={e: len(self.ops[e]) for e in ENGS}, max_sem=max(cnt.values()),
                          n_sig={e: cnt.get(("eng", e), 0) for e in ENGS}, n_wait={})
        prog = self
        eng_know = {e: {} for e in ENGS}
        order = {id(o): i for i, o in enumerate(self.all_ops)}
        for o in self.all_ops:
            e = o.eng
            ek = eng_know[e]
            waits = []
            deps = [d for d in o.deps if not ((not d.is_dma) and d.eng == e and e == "pe" and not o.is_dma)]
            deps.sort(key=lambda d: -order[id(d)])
            for d in deps:
                k, v = d.sem, d.val
                if ek.get(k, 0) >= v:
                    continue
                waits.append((k, v))
                for kk, vv in d.know.items():
                    if vv > ek.get(kk, 0):
                        ek[kk] = vv
                if v > ek.get(k, 0):
                    ek[k] = v
            o.waits = waits
            kn = dict(ek)
            if o.sem is not None and o.val is not None:
                if o.val > kn.get(o.sem, 0):
                    kn[o.sem] = o.val
            o.know = kn

        def run_engine(e, h):
            nwait = 0
            for o in prog.ops[e]:
                for k, v in o.waits:
                    h.wait_ge(sem_h[k], v)
                    nwait += 1
                ins = o.fn(h)
                if o.is_dma:
                    ins.then_inc(sem_h[o.sem], 16)
                elif o.signal:
                    ins.then_inc(sem_h[o.sem], 1)
            if e == "sp":
                for k, v in cnt.items():
                    h.wait_ge(sem_h[k], v)
            prog.stats["n_wait"][e] = nwait

        with nc.Block() as block:
            @block.tensor
            def _(h):
                run_engine("pe", h)

            @block.scalar
            def _(h):
                run_engine("act", h)

            @block.vector
            def _(h):
                run_engine("dve", h)

            @block.gpsimd
            def _(h):
                run_engine("pool", h)

            @block.sync
            def _(h):
                run_engine("sp", h)


def _t5_bucket_np(rel):
    import jax
    import jax.numpy as jnp
    cpu = jax.devices("cpu")[0]
    with jax.default_device(cpu):
        rel = jnp.asarray(rel, dtype=jnp.int32)
        nb = 16
        ret = jnp.where(rel > 0, nb, 0)
        n = jnp.abs(rel)
        max_exact = nb // 2
        large = max_exact + (jnp.log(jnp.maximum(n, 1).astype(jnp.float32) / max_exact)
                             / math.log(128 / max_exact) * (nb - max_exact)).astype(jnp.int32)
        large = jnp.minimum(large, nb - 1)
        out = ret + jnp.where(n < max_exact, n, large)
        return np.asarray(out)


_CONST_CACHE = {}


def _gammas():
    return (1.0 - 2.0 ** (-5.0 - np.arange(4, dtype=np.float64)))


def _rope_tables(pos, L):
    n = pos.shape[0]
    inv = (np.float32(1.0) / (np.float32(10000.0) ** np.linspace(0.0, 1.0, 64, dtype=np.float32))).astype(np.float32)
    ang = pos.astype(np.float32)[:, None] * inv[None, :]
    cos = np.cos(ang).astype(np.float64)
    sin = np.sin(ang).astype(np.float64)
    l = (np.arange(n) % L).astype(np.float64)
    lg = np.log(_gammas())
    qd = np.exp((l[:, None] + 1.0) * lg[None, :])
    kd = np.exp(-(l[:, None] + 1.0) * lg[None, :]) / math.sqrt(128.0)
    out = np.zeros((n, 4, 4, 64), np.float64)
    out[:, 0] = cos[:, None, :] * qd[:, :, None]
    out[:, 1] = sin[:, None, :] * qd[:, :, None]
    out[:, 2] = cos[:, None, :] * kd[:, :, None]
    out[:, 3] = sin[:, None, :] * kd[:, :, None]
    return out.reshape(n, 4, 256).astype(np.float32)


def _static_consts():
    if "c" in _CONST_CACHE:
        return _CONST_CACHE["c"]
    g = _gammas()
    rel = 127 - np.arange(384)
    bk = _t5_bucket_np(rel)
    ohr = np.zeros((32, 384), np.float32)
    ohr[bk, np.arange(384)] = 1.0
    oh15 = np.zeros((32, 128), np.float32)
    b_far = int(_t5_bucket_np(np.array([-200]))[0])
    oh15[b_far, :] = 1.0
    maskadd = np.zeros((128, 256), np.float32)
    k = np.arange(128)[:, None]
    c = np.arange(256)[None, :]
    maskadd[(k // 64) > (c // 64)] = NEG
    kc = np.zeros((128, 448), np.float32)
    kc[0, 0:128] = 1.0
    kc[32, 128:256] = 1.0
    m = np.arange(128)[:, None]
    l = np.arange(128)[None, :]
    kc[:, 256:384] = (l >= m).astype(np.float32)
    kc2 = np.zeros((128, 256), np.float32)
    kc2[np.arange(128), 127 - np.arange(128)] = 1.0
    kc2[np.arange(128), 128 + np.arange(128)] = 1.0
    kc = np.concatenate([kc[:, :384], kc2], axis=1)
    dco = np.zeros((2, 128, 32), np.float32)
    for p in range(2):
        a = np.ones(4) if p == 0 else g ** 512
        b = np.zeros(4) if p == 0 else g ** 512
        e = g ** 1024 if p == 0 else g ** 512
        f = g ** 512 if p == 0 else g ** 1024
        dco[p, :, 0:4] = a
        dco[p, :, 4:8] = b
        dco[p, :, 8:12] = g ** 1024
        dco[p, :, 12:16] = e
        dco[p, :, 16:20] = f
        sel = [1, 0, 0, 0, 0, NEG, 0, NEG] if p == 0 else [0, 1, 0, 1, 0, 0, 1, 0]
        dco[p, :, 20:28] = np.array(sel, np.float32)
        dco[p, :, 28:32] = g ** 32
    rope = np.zeros((2, NT, TT, 4, 256), np.float32)
    for p in range(2):
        for i in range(NT // 2):
            for j, tile in enumerate((2 * i + 1 - p, 2 * i + p)):
                pos = tile * TT + np.arange(TT)
                rope[p, 2 * i + j] = _rope_tables(pos, TT)
    rope_s = _rope_tables(PAST + (np.arange(64) % LS), LS)
    res = dict(ohr=ohr, oh15=oh15, maskadd=maskadd, kc=kc, dco=dco, rope=rope, rope_s=rope_s)
    _CONST_CACHE["c"] = res
    return res


class _StopBuild(Exception):
    pass


def build_program(n_steps=NT // 2, do_sample=True, cut=0):
    def chk(n):
        if cut == n:
            raise _StopBuild()
    nc = bass.Bass("TRN2", target_bir_lowering=False)
    P = Prog(nc)

    def din(name, shape, dt=F32):
        return nc.dram_tensor(name, list(shape), dt, kind="ExternalInput").ap()

    def dout(name, shape, dt=F32):
        return nc.dram_tensor(name, list(shape), dt, kind="ExternalOutput").ap()

    def dscr(name, shape, dt):
        return nc.dram_tensor(name, list(shape), dt, kind="Internal").ap()

    def sb(name, shape, dt):
        return nc.alloc_sbuf_tensor("s_" + name, list(shape), dt).ap()

    xs = din("xs", [NT, TT, D])
    xsm = din("xsm", [64, D])
    crow_d = din("crow", [5, D])
    ck = din("ck", [2, PAST, 512])
    cv = din("cv", [2, PAST, 4, 128])
    st = din("st", [2, 4, 128, 128])
    w_ada = din("w_ada", [D, 6 * D])
    b_ada = din("b_ada", [1, 6 * D])
    w_in = din("w_in", [D, DIN])
    gqk = din("gqk", [1, 128])
    lamv = din("lamv", [1, 256])
    gsub = din("gsub", [1, 128])
    w_out = din("w_out", [D, D])
    wg = din("wg", [D, DFF])
    wu = din("wu", [D, DFF])
    wd = din("wd", [DFF, D])
    rbt_d = din("rbt", [32, 4])
    rope_d = din("rope", [NT, TT, 4, 256])
    rope_sd = din("rope_s", [64, 4, 256])
    dco_d = din("dco", [128, 32])
    ohr_d = din("ohr", [32, 384])
    oh15_d = din("oh15", [32, 128])
    maskadd_d = din("maskadd", [128, 256])
    kc_d = din("kc", [128, 640])

    y = dout("y", [NT // 2, TT, D])
    ysm = dout("ysm", [64, D])
    kp = dout("kp", [NT // 2, TT, 512])
    vp = dout("vp", [NT // 2, TT, 512])
    rp = dout("rp", [4, 128, 128])
    ksm = dout("ksm", [64, 512])
    vsm = dout("vsm", [64, 512])
    rs = dout("rs", [2, 4, 128, 128])

    Wi = dscr("Wi", [7, 128, 8, 512], BF16)
    Wo = dscr("Wo", [2, 128, 8, 512], BF16)
    Wgu = dscr("Wgu", [11, 128, 8, 512], BF16)
    Wd = dscr("Wd", [2, 128, NKF, 512], BF16)
    KT = dscr("KT", [4, NT, 128, 512], BF16)
    VS = dscr("VS", [4, NT, 128, 4, 128], BF16)
    KTs = dscr("KTs", [2, 4, 4, 128, 512], BF16)
    VSs = dscr("VSs", [2, 4, 4, 128, 4, 128], BF16)
    urd = dscr("urd", [4, 384], F32)

    kc = sb("kc", [128, 640], F32)
    sel0 = kc[0:64, 0:128]
    sel1 = kc[0:64, 128:256]
    trif = kc[:, 256:384]
    Jf = kc[:, 384:512]
    identf = kc[:, 512:640]
    identb = sb("identb", [128, 128], BF16)
    trib = sb("trib", [128, 128], BF16)
    onesb = sb("onesb", [128, 128], BF16)
    onesf = sb("onesf", [128, 128], F32)
    dco = sb("dco", [128, 32], F32)
    Mh = sb("Mh", [128, 4, 256], F32)
    G0 = sb("G0", [128, 4, 128], F32)
    G4 = sb("G4", [128, 4, 128], F32)
    MS = sb("MS", [64, 4, 32], F32)
    cols = sb("cols", [128, 64], F32)
    C15, CC0, CC4, CCF, NLAM, GSUB, EPSC = 0, 4, 8, 12, 16, 17, 18
    modc = sb("modc", [128, 4, 8, 3], F32)
    gate1b = sb("gate1b", [128, D], F32)
    gate2b = sb("gate2b", [128, D], F32)
    gqkb = sb("gqkb", [128, 128], F32)
    cT = sb("cT", [128, 8, 5], F32)
    scT = sb("scT", [128, 8, 3], BF16)
    scB = sb("scB", [128, 8, 128], BF16)
    scS = sb("scS", [128, 8, 64], BF16)
    browb = sb("browb", [1, 1, 512], F32)
    smallr = sb("smallr", [32, 512], F32)
    xo = sb("xo", [128, 4, D], F32)
    xf = sb("xf", [128, 1, D], F32)
    crow5 = xf[0:5, 0, :]
    gfull = sb("gfull", [128, 2, 512], F32)
    hT = sb("hT", [128, 8, TT], BF16)
    rt = sb("rt", [128, 2, 4, 256], F32)
    tokb = sb("tokb", [128, 4, 512], BF16)
    qAT = sb("qAT", [128, 2, 4, TT], BF16)
    kst = sb("kst", [128, 4, TT], BF16)
    vst = sb("vst", [128, 4, 512], BF16)
    kf = sb("kf", [128, 1, 512], F32)
    vf = sb("vf", [128, 1, 512], F32)
    qdT = sb("qdT", [128, 4, TT], BF16)
    keT = sb("keT", [128, 4, TT], BF16)
    ke = sb("ke", [128, 4, 512], BF16)
    vR = sb("vR", [128, 4, 512], BF16)
    sgT = sb("sgT", [128, 4, TT], BF16)
    oT = sb("oT", [128, 8, TT], BF16)
    ffT = sb("ffT", [128, NKF, TT], BF16)
    NW = 3
    wring = sb("wring", [128, NW, 4096], BF16)
    NKB = 2
    ktb = sb("ktb", [128, NKB, 512], BF16)
    vsb = sb("vsb", [128, NKB, 512], BF16)
    Pm = sb("Pm", [128, 4, 512], BF16)
    scb = Pm
    lg = sb("lg", [128, 2, 512], F32)
    et = sb("et", [128, 4, 512], F32)
    xn = sb("xn", [128, 4, D], BF16)
    lamt = sb("lamt", [1, 128], F32)
    sqb = sb("sqb", [128, 512], BF16)
    warmb = sb("warmb", [128, 512], BF16)
    S = sb("S", [128, 4, 128], F32)
    Sst = sb("Sst", [128, 4, 128], F32)
    tmpS = sb("tmpS", [128, 4, 128], F32)
    Sb = sb("Sb", [128, 4, 128], BF16)
    Uf = sb("Uf", [128, 4, 128], F32)
    stat = sb("stat", [128, 64], F32)

    psall = nc.alloc_psum_tensor("psall", [128, 8, 512], F32).ap()
    psallb = psall.bitcast(BF16)
    psb = [psall[:, i, :] for i in range(8)]
    psbb = [psallb[:, i, :] for i in range(8)]

    def PS(i):
        return ("ps", i)

    def mm(out, lhsT, rhs, start, stop, r, w):
        return P.op("pe", lambda e: e.matmul(out, lhsT=lhsT, rhs=rhs, start=start, stop=stop), reads=r, writes=w)

    def warm(n, bank):
        for _ in range(n):
            mm(psb[bank][:, :], onesb[:, :], warmb[:, :], True, True, ["onesb", "warmb"], [PS(bank)])

    def trn(out, in_, ident, r, w):
        return P.op("pe", lambda e: e.transpose(out, in_, ident), reads=r, writes=w)

    def act(out, in_, func, r, w, scale=1.0, bias=0.0, accum=None):
        if accum is None:
            return P.op("act", lambda e: e.activation(out=out, in_=in_, func=func, bias=bias, scale=scale), reads=r, writes=w)
        return P.op("act", lambda e: e.activation(out=out, in_=in_, func=func, bias=bias, scale=scale, accum_out=accum), reads=r, writes=w)

    def tt(out, in0, in1, op, r, w, eng="dve"):
        return P.op(eng, lambda e: e.tensor_tensor(out=out, in0=in0, in1=in1, op=op), reads=r, writes=w)

    def ts(out, in0, s1, s2, op0, op1, r, w, eng="dve"):
        if op1 is None:
            return P.op(eng, lambda e: e.tensor_scalar(out=out, in0=in0, scalar1=s1, scalar2=None, op0=op0), reads=r, writes=w)
        return P.op(eng, lambda e: e.tensor_scalar(out=out, in0=in0, scalar1=s1, scalar2=s2, op0=op0, op1=op1), reads=r, writes=w)

    def stt(out, in0, scalar, in1, op0, op1, r, w):
        return P.op("dve", lambda e: e.scalar_tensor_tensor(out=out, in0=in0, scalar=scalar, in1=in1, op0=op0, op1=op1), reads=r, writes=w)

    def cp(out, in_, r, w, eng="dve"):
        return P.op(eng, lambda e: e.tensor_copy(out=out, in_=in_), reads=r, writes=w)

    def ld(out, in_, r=(), w=()):
        return P.dma("sp", lambda e: e.dma_start(out=out, in_=in_), reads=r, writes=w)

    def stq(out, in_, r=(), w=()):
        return P.dma("pool", lambda e: e.dma_start(out=out, in_=in_), reads=r, writes=w)

    ev_rr = [0]

    def evac(out, in_, r, w):
        ev_rr[0] ^= 1
        if ev_rr[0]:
            return act(out, in_, AF.Copy, r, w)
        return cp(out, in_, r, w)

    def fr(ap, dims):
        return AP(ap.tensor, ap.offset, [list(ap.ap[0])] + [list(d) for d in dims])

    out_ops = []

    def prepass_in():
        for cb in (1, 2, 4, 5, 0, 3, 6):
            src = w_in[:, cb * 512:(cb + 1) * 512].rearrange("(k p) c -> p k c", p=128)
            stq(Wi[cb], src, w=[("Wi", cb)])

    def prepass_rest():
        for cb in range(2):
            src = w_out[:, cb * 512:(cb + 1) * 512].rearrange("(k p) c -> p k c", p=128)
            stq(Wo[cb], src, w=[("Wo", cb)])
        for gb in range(11):
            stq(Wgu[gb][:, :, 0:256], wg[:, gb * 256:(gb + 1) * 256].rearrange("(k p) c -> p k c", p=128), w=[("Wgu", gb, 0)])
            stq(Wgu[gb][:, :, 256:512], wu[:, gb * 256:(gb + 1) * 256].rearrange("(k p) c -> p k c", p=128), w=[("Wgu", gb, 1)])
        for half in range(2):
            for bi, (k0, k1) in enumerate(((0, 8), (8, 16), (16, 22))):
                src = wd[k0 * 128:k1 * 128, half * 512:(half + 1) * 512].rearrange("(k p) c -> p k c", p=128)
                stq(Wd[half][:, k0:k1, :], src, w=[("Wd", half, bi)])

    wcnt = [0]

    def wslot(nk, ncol):
        r = wcnt[0] % NW
        wcnt[0] += 1
        view = wring[:, r, 0:nk * ncol].rearrange("p (k c) -> p k c", k=nk)
        return view, ("wr", r)

    def load_w(src, nk, ncol, rkeys):
        view, key = wslot(nk, ncol)
        ld(view, src, r=rkeys, w=[key])
        return view, key

    pf = {}
    WSRC = {}
    for cb_ in range(7):
        WSRC[("Wi", cb_)] = (Wi[cb_], 8, 512, [("Wi", cb_)])
    for cb_ in range(2):
        WSRC[("Wo", cb_)] = (Wo[cb_], 8, 512, [("Wo", cb_)])
    for gb_ in range(11):
        WSRC[("Wgu", gb_)] = (Wgu[gb_], 8, 512, [("Wgu", gb_, 0), ("Wgu", gb_, 1)])
    for half_ in range(2):
        for bi_, (k0_, k1_) in enumerate(((0, 8), (8, 16), (16, 22))):
            WSRC[("Wd", half_, bi_)] = (Wd[half_][:, k0_:k1_, :], k1_ - k0_, 512, [("Wd", half_, bi_)])

    def prefetch(name):
        if name not in pf:
            pf[name] = load_w(*WSRC[name])

    def get_w(name):
        if name in pf:
            return pf.pop(name)
        return load_w(*WSRC[name])

    def setup():
        ld(kc[:], kc_d[:], w=["kc"])
        ld(dco[:], dco_d[:], w=["dco"])
        ld(crow5, crow_d[:], w=[("xf", 0)])
        ld(smallr[0:1, 0:256], lamv[:], w=["lamr"])
        ld(smallr[0:1, 256:384], gsub[:], w=["gsr"])
        ld(smallr[0:1, 384:512], gqk[:], w=["gqr"])
        rbt = sb("rbt", [32, 4], F32)
        oh15 = lg[0:32, 1, 0:128]
        ohr = lg[0:32, 0, 0:384]
        ld(rbt[:], rbt_d[:], w=["rbt"])
        ld(oh15, oh15_d[:], w=[("lg", 1)])
        ld(ohr, ohr_d[:], w=[("lg", 0)])
        maskadd = et[:, 3, 0:256]
        ld(maskadd, maskadd_d[:], w=["et3"])
        P.op("dve", lambda e: e.memset(onesb[:], 1.0), writes=["onesb"])
        P.op("dve", lambda e: e.memset(warmb[:], 1.0), writes=["warmb"])
        P.op("dve", lambda e: e.memset(onesf[:], 1.0), writes=["onesf"])
        P.op("dve", lambda e: e.memset(cols[:, EPSC:EPSC + 1], EPS), writes=["epsc"])
        P.op("dve", lambda e: e.memset(S[:], 0.0), writes=["S"])
        P.op("pool", lambda e: e.memset(qAT[:].rearrange("p m h t -> p (m h t)"), 0.0), writes=[("qAT", h) for h in range(4)])
        cp(identb[:], identf, ["kc"], ["identb"])
        cp(trib[:], trif, ["kc"], ["trib"])
        tt(lamt[0:1, 0:64], smallr[0:1, 0:64], smallr[0:1, 64:128], ALU.mult, ["lamr"], ["st_a"])
        tt(lamt[0:1, 64:128], smallr[0:1, 128:192], smallr[0:1, 192:256], ALU.mult, ["lamr"], ["st_a"])
        lam2 = sb("lam2", [1, 8], F32)
        P.op("dve", lambda e: e.tensor_reduce(out=lam2[0:1, 0:2], in_=lamt[0:1, 0:128].rearrange("p (a j) -> p a j", a=2), axis=AX.X, op=ALU.add),
             reads=["st_a"], writes=["lam2a"])
        act(lam2[0:1, 2:4], lam2[0:1, 0:2], AF.Exp, ["lam2a"], ["lam2b"])
        tt(lam2[0:1, 4:5], lam2[0:1, 3:4], lam2[0:1, 2:3], ALU.subtract, ["lam2b"], ["lam2c"])
        ts(lam2[0:1, 5:6], lam2[0:1, 4:5], -LAM_INIT, None, ALU.add, None, ["lam2c"], ["lam2d"])
        P.op("dve", lambda e: e.memset(lam2[0:1, 6:7], 1.0 - LAM_INIT), writes=["lam2e"])
        pm = psb[7]
        mm(pm[:, 0:1], onesf[0:1, 0:128], lam2[0:1, 5:6], True, True, ["onesf", "lam2d"], [PS(7)])
        mm(pm[:, 1:2], smallr[0:1, 256:384], lam2[0:1, 6:7], True, True, ["gsr", "lam2e"], [PS(7)])
        mm(pm[:, 2:6], oh15, rbt[:, :], True, True, [("lg", 1), "rbt"], [PS(7)])
        mm(pm[:, 128:256], onesf[0:1, 0:128], smallr[0:1, 384:512], True, True, ["onesf", "gqr"], [PS(7)])
        cp(cols[:, NLAM:NLAM + 2], pm[:, 0:2], [PS(7)], ["nlam", "gsubc"])
        cp(cols[:, C15:C15 + 4], pm[:, 2:6], [PS(7)], ["c15"])
        cp(gqkb[:], pm[:, 128:256], [PS(7)], ["gqkb"])
        for i in range(2):
            for g in range(8):
                cp(gfull[:, i, g * 64:(g + 1) * 64], gqkb[:, i * 64:(i + 1) * 64], ["gqkb"], ["gfull"])
        ts(cols[:, CC0:CC0 + 4], cols[:, C15:C15 + 4], dco[:, 21:22], dco[:, 22:23], ALU.mult, ALU.add, ["c15", "dco"], ["cc0"])
        ts(cols[:, CC4:CC4 + 4], cols[:, C15:C15 + 4], dco[:, 24:25], dco[:, 25:26], ALU.mult, ALU.add, ["c15", "dco"], ["cc4"])
        ts(cols[:, CCF:CCF + 4], cols[:, C15:C15 + 4], dco[:, 26:27], dco[:, 27:28], ALU.mult, ALU.add, ["c15", "dco"], ["ccf"])
        pu = psb[6]
        mm(pu[0:4, 0:384], rbt[:, :], ohr, True, True, ["rbt", ("lg", 0)], [PS(6)])
        urs = et[0:4, 2, 0:384]
        cp(urs, pu[0:4, 0:384], [PS(6)], ["et2"])
        stq(urd[:], urs, r=["et2"], w=["urd"])
        hk = et[:, 0:2, :].rearrange("p a (b c) -> p (a b) c", c=256)
        ld(hk, AP(urd.tensor, 0, [[1, 128], [384, 4], [1, 256]]), r=["urd"], w=["et0", "et1"])
        for h in range(4):
            pj = psb[h % 2]
            mm(pj[:, 0:256], Jf, hk[:, h, :], True, True, ["kc", "et0", "et1"], [PS(h % 2)])
            tt(Mh[:, h, :], pj[:, 0:256], maskadd, ALU.add, [PS(h % 2), "et3"], [("Mh", h)])
            ts(G0[:, h, :], Mh[:, h, 128:256], dco[:, 20:21], cols[:, CC0 + h:CC0 + h + 1], ALU.mult, ALU.add, [("Mh", h), "dco", "cc0"], [("G0", h)])
            ts(G4[:, h, :], Mh[:, h, 128:256], dco[:, 23:24], cols[:, CC4 + h:CC4 + h + 1], ALU.mult, ALU.add, [("Mh", h), "dco", "cc4"], [("G4", h)])
        ld(MS[0:32, :, :], Mh[0:32, :, 0:32], r=[("Mh", h) for h in range(4)], w=["MS0"])
        ld(MS[32:64, :, :], Mh[0:32, :, 0:32], r=[("Mh", h) for h in range(4)], w=["MS1"])

        act(crow5[0:3, :], crow5[0:3, :], AF.Silu, [("xf", 0)], [("xf", 0)])
        pt = psb[5]
        for k in range(8):
            trn(pt[:, k * 5:(k + 1) * 5], crow5[0:5, k * 128:(k + 1) * 128], identf[0:5, 0:5], [("xf", 0), "kc"], [PS(5)])
        cp(cT[:].rearrange("p k v -> p (k v)"), pt[:, 0:40], [PS(5)], ["cT"])
        cp(scT[:], cT[:, :, 0:3], ["cT"], ["scT"])
        cp(scB[:], fr(cT[:, 0, 0:1], [[5, 8], [0, 128]]), ["cT"], ["scB"])
        cp(scS[:, :, 0:32], fr(cT[:, 0, 1:2], [[5, 8], [0, 32]]), ["cT"], ["scS0"])
        cp(scS[:, :, 32:64], fr(cT[:, 0, 2:3], [[5, 8], [0, 32]]), ["cT"], ["scS1"])
        psm = psb[4]
        psmv = psm[:, 0:96].rearrange("p (a c v) -> p a c v", a=4, c=8)
        for kind, base in ((0, 0), (1, 2), (2, 6), (3, 8)):
            for half in range(2):
                cbk = base + half
                view, key = wslot(8, 512)
                stq(view, w_ada[:, cbk * 512:(cbk + 1) * 512].rearrange("(k p) c -> p k c", p=128), w=[key])
                bslot = 0
                ld(browb[0:1, bslot, :], b_ada[0:1, cbk * 512:(cbk + 1) * 512], w=[("brow", bslot)])
                for e in range(4):
                    c = half * 4 + e
                    for k in range(8):
                        mm(psmv[:, kind, c, :], view[:, k, e * 128:(e + 1) * 128], scT[:, k, :], k == 0, False, [key, "scT"], [PS(4)])
                    mm(psmv[:, kind, c, :], browb[0:1, bslot, e * 128:(e + 1) * 128], onesf[0:1, 0:3], False, True, [("brow", bslot), "onesf"], [PS(4)])
        cp(modc[:, 0], psmv[:, 0], [PS(4)], ["modc0"])
        cp(modc[:, 2], psmv[:, 2], [PS(4)], ["modc2"])
        for c in range(8):
            ts(modc[:, 1, c, :], psmv[:, 1, c, :], 1.0, cT[:, c, 3:4], ALU.add, ALU.mult, [PS(4), "cT"], ["modc1"])
            ts(modc[:, 3, c, :], psmv[:, 3, c, :], 1.0, cT[:, c, 4:5], ALU.add, ALU.mult, [PS(4), "cT"], ["modc3"])

    def gates(sample):
        rows = 64 if sample else 128
        for gi, (gt, base) in enumerate(((gate1b, 4), (gate2b, 10))):
            for half in range(2):
                cbk = base + half
                view, key = wslot(8, 512)
                stq(view, w_ada[:, cbk * 512:(cbk + 1) * 512].rearrange("(k p) c -> p k c", p=128), w=[key])
                bslot = 0
                ld(browb[0:1, bslot, :], b_ada[0:1, cbk * 512:(cbk + 1) * 512], w=[("brow", bslot)])
                pg = psb[(gi * 2 + half) % 4]
                pk = PS((gi * 2 + half) % 4)
                for k in range(8):
                    lhs = scS[:, k, :] if sample else scB[:, k, :]
                    mm(pg[0:rows, :], lhs, view[:, k, :], k == 0, False, [key, "scB", "scS0", "scS1"], [pk])
                mm(pg[0:rows, :], onesf[0:1, 0:rows], browb[0:1, bslot, :], False, True, [("brow", bslot), "onesf"], [pk])
                act(gt[0:rows, half * 512:(half + 1) * 512], pg[0:rows, :], AF.Copy, [pk], [("gate", gi, half)])

    acc_rr = [0]

    ACC_BANKS = (0, 1, 2, 3, 6)

    def next_acc():
        i = ACC_BANKS[acc_rr[0] % len(ACC_BANKS)]
        acc_rr[0] += 1
        return psb[i], PS(i)

    tb_rr = [0]

    def next_tb():
        i = tb_rr[0] % 2
        tb_rr[0] += 1
        return psbb[4 + i], PS(4 + i)

    def rstd_small(dst, src, scale, rows, rk, key):
        act(dst, src, AF.Ln, list(rk) + ["epsc"], [key], scale=scale, bias=cols[0:rows, EPSC:EPSC + 1])
        act(dst, dst, AF.Exp, [key], [key], scale=-0.5)

    def norm_part(xv, xk, rows, subs, sc0, junk_pm=False):
        n = len(subs)
        if junk_pm:
            junk = Pm[0:rows, 0:2, :].rearrange("p a c -> p (a c)")
            jk = [("Pm", 0), ("Pm", 1)]
        else:
            junk = lg[0:rows, :, :].rearrange("p a c -> p (a c)")
            jk = [("lg", 0), ("lg", 1)]
        for i, s in enumerate(subs):
            act(junk, xv[i], AF.Square, [xk[i]], jk + [("ssq", sc0 + i)], accum=stat[0:rows, sc0 + i:sc0 + i + 1])
        rstd_small(stat[0:rows, 8 + sc0:8 + sc0 + n], stat[0:rows, sc0:sc0 + n], 1.0 / D, rows,
                   [("ssq", sc0 + i) for i in range(n)], ("rstd", sc0))
        for i, s in enumerate(subs):
            ts(xn[0:rows, s, :], xv[i], stat[0:rows, 8 + sc0 + i:9 + sc0 + i], None, ALU.mult, None, [xk[i], ("rstd", sc0)], [("xn", s)])

    def transp_part(rows, nsub, kG, kS, sample):
        T = rows * nsub
        mk = ["modc0", "modc1", "modc2", "modc3"]
        for cpair in range(4):
            tbb, tk = next_tb()
            for half in range(2):
                c = 2 * cpair + half
                tb = tbb[:, half * 512:(half + 1) * 512]
                for s in range(nsub):
                    trn(tb[:, s * rows:(s + 1) * rows], xn[0:rows, s, c * 128:(c + 1) * 128], identb[0:rows, 0:rows], [("xn", s), "identb"], [tk])
            for half in range(2):
                c = 2 * cpair + half
                tb = tbb[:, half * 512:(half + 1) * 512]
                if not sample:
                    act(hT[:, c, 0:T], tb[:, 0:T], AF.Identity, [tk] + mk, [("hT", c)],
                        scale=modc[:, kG, c, 0:1], bias=modc[:, kS, c, 0:1])
                else:
                    for q in range(2):
                        act(hT[:, c, q * 32:(q + 1) * 32], tb[:, q * 32:(q + 1) * 32], AF.Identity, [tk] + mk, [("hT", c)],
                            scale=modc[:, kG, c, 1 + q:2 + q], bias=modc[:, kS, c, 1 + q:2 + q])

    def phase_norm(xviews, xkeys, rows, nsub, kG, kS, sample):
        if not sample:
            warm(WARM_NORM, 7)
        norm_part(xviews, xkeys, rows, list(range(nsub)), 0)
        transp_part(rows, nsub, kG, kS, sample)

    def hT_keys(sample):
        return [("hT", c) for c in range(8)]

    def qknorm(acc, ak, rows, goff, dest, dk, par=0):
        e0, e1 = 2 * par, 2 * par + 1
        k0, k1 = f"et{e0}", f"et{e1}"
        sc = 16 + 8 * par
        rk = ("r8", par)
        sq = et[0:rows, e0, :]
        act(sq, acc[0:rows, :], AF.Square, [ak], [k0])
        P.op("dve", lambda e: e.tensor_reduce(out=stat[0:rows, sc:sc + 8], in_=sq.rearrange("p (g j) -> p g j", g=8), axis=AX.X, op=ALU.add),
             reads=[k0], writes=[rk])
        rstd_small(stat[0:rows, sc:sc + 8], stat[0:rows, sc:sc + 8], 1.0 / 64, rows, [rk], rk)
        t1 = et[0:rows, e1, :]
        tt(t1.rearrange("p (g j) -> p g j", g=8), acc[0:rows, :].rearrange("p (g j) -> p g j", g=8),
           fr(stat[0:rows, sc:sc + 1], [[1, 8], [0, 64]]), ALU.mult, [ak, rk], [k1])
        tt(dest, t1, gfull[0:rows, goff // 64, :], ALU.mult, [k1, "gfull"], [dk])

    def rope(acc, ak, rows, tabC, tabS, tkey, dest, dk, par=0):
        e0, e1 = 2 * par, 2 * par + 1
        k0, k1 = f"et{e0}", f"et{e1}"
        xe = acc[0:rows, 0:512:2]
        xo_ = acc[0:rows, 1:512:2]
        t1 = et[0:rows, e0, 0:256]
        t2 = et[0:rows, e0, 256:512]
        t3 = et[0:rows, e1, 0:256]
        t4 = et[0:rows, e1, 256:512]
        tt(t1, xe, tabC, ALU.mult, [ak, tkey], [k0])
        tt(t2, xo_, tabS, ALU.mult, [ak, tkey], [k0])
        tt(dest[:, 0:512:2], t1, t2, ALU.subtract, [k0], [dk])
        tt(t3, xe, tabS, ALU.mult, [ak, tkey], [k1])
        tt(t4, xo_, tabC, ALU.mult, [ak, tkey], [k1])
        tt(dest[:, 1:512:2], t3, t4, ALU.add, [k1], [dk])

    def transp_heads(src, skeys, rows, nsub, dest, dname, split=False):
        T = rows * nsub
        for hp in range(2):
            tbb, tk = next_tb()
            for half in range(2):
                h = 2 * hp + half
                tb = tbb[:, half * 512:(half + 1) * 512]
                for s in range(nsub):
                    trn(tb[:, s * rows:(s + 1) * rows], src[0:rows, s, h * 128:(h + 1) * 128], identb[0:rows, 0:rows], list(skeys(s)) + ["identb"], [tk])
            for half in range(2):
                h = 2 * hp + half
                if split:
                    act(dest[0:64, 0, h, 0:T], tbb[0:64, half * 512:half * 512 + T], AF.Copy, [tk], [(dname, h)])
                    act(dest[64:128, 1, h, 0:T], tbb[64:128, half * 512:half * 512 + T], AF.Copy, [tk], [(dname, h)])
                else:
                    act(dest[:, h, 0:T], tbb[:, half * 512:half * 512 + T], AF.Copy, [tk], [(dname, h)])

    def in_proj(rows, nsub, own, sample, tabsrc, tile_i, hook_after_first=None, next_w=None):
        T = rows * nsub
        hk_ = hT_keys(sample)
        cbs = (1, 2, 0, 4, 5, 3, 6) if own else (1, 2, 4, 5)
        pending = [None]

        def flush():
            if pending[0] is not None:
                pending[0]()
                pending[0] = None
        for ci, cb in enumerate(cbs):
            W, wkey = get_w(("Wi", cb))
            if ci + 1 < len(cbs):
                prefetch(("Wi", cbs[ci + 1]))
            elif next_w is not None:
                prefetch(next_w)
            if cb == 6:
                for e in range(4):
                    acc, ak = next_acc()
                    for k in range(8):
                        mm(acc[:, 0:T], W[:, k, e * 128:(e + 1) * 128], hT[:, k, 0:T], k == 0, k == 7, [wkey] + hk_, [ak])
                    act(sgT[:, e, 0:T], acc[:, 0:T], AF.Silu, [ak], [("sgT", e)])
                flush()
                continue
            for s in range(nsub):
                acc, ak = next_acc()
                for k in range(8):
                    mm(acc[0:rows, :], hT[:, k, s * rows:(s + 1) * rows], W[:, k, :], k == 0, k == 7, [wkey] + hk_, [ak])
                chk(31)
                if cb == 0:
                    qknorm(acc, ak, rows, 0, tokb[0:rows, s, :], ("tokb", s), s % 2)
                elif cb == 1:
                    if own:
                        kfv = kf[0:rows, 0, :]
                        qknorm(acc, ak, rows, 64, kfv, "kf", s % 2)
                        if sample:
                            out_ops.append(stq(ksm[:, :], kfv, r=["kf"]))
                        else:
                            out_ops.append(stq(kp[tile_i, s * 128:(s + 1) * 128, :], kfv, r=["kf"]))
                        act(tokb[0:rows, s, :], kfv, AF.Copy, ["kf"], [("tokb", s)])
                    else:
                        qknorm(acc, ak, rows, 64, tokb[0:rows, s, :], ("tokb", s), s % 2)
                elif cb == 2:
                    if own:
                        vfv = vf[0:rows, 0, :]
                        act(vfv, acc[0:rows, :], AF.Copy, [ak], ["vf"])
                        if sample:
                            out_ops.append(stq(vsm[:, :], vfv, r=["vf"]))
                        else:
                            out_ops.append(stq(vp[tile_i, s * 128:(s + 1) * 128, :], vfv, r=["vf"]))
                        cp(vst[0:rows, s, :], vfv, ["vf"], [("vst", s)])
                    else:
                        evac(vst[0:rows, s, :], acc[0:rows, :], [ak], [("vst", s)])
                elif cb in (3, 4):
                    rs_ = s % 2
                    ld(rt[0:rows, rs_], tabsrc(s), w=[("rt", rs_)])
                    if cb == 3:
                        rope(acc, ak, rows, rt[0:rows, rs_, 0, :], rt[0:rows, rs_, 1, :], ("rt", rs_), tokb[0:rows, s, :], ("tokb", s), s % 2)
                    else:
                        rope(acc, ak, rows, rt[0:rows, rs_, 2, :], rt[0:rows, rs_, 3, :], ("rt", rs_), ke[0:rows, s, :], ("ke", s), s % 2)
                elif cb == 5:
                    evac(vR[0:rows, s, :], acc[0:rows, :], [ak], [("vR", s)])
            if hook_after_first is not None and cb == cbs[0]:
                hook_after_first()
            flush()
            if cb == 0:
                pending[0] = lambda: transp_heads(tokb, lambda s: [("tokb", s)], rows, nsub, qAT, "qAT", split=True)
            elif cb == 1:
                pending[0] = lambda: transp_heads(tokb, lambda s: [("tokb", s)], rows, nsub, kst, "kst")
            elif cb == 3:
                pending[0] = lambda: transp_heads(tokb, lambda s: [("tokb", s)], rows, nsub, qdT, "qdT")
            elif cb == 4 and own:
                pending[0] = lambda: transp_heads(ke, lambda s: [("ke", s)], rows, nsub, keT, "keT")
        flush()

    def store_kv(unit):
        stq(KT[:, unit].rearrange("h p c -> p h c"), kst[:, :, :], r=[("kst", h) for h in range(4)], w=[("KT", unit)])
        for s in range(4):
            dst = VS[:, unit, :, s, :].rearrange("h p d -> p h d")
            stq(dst, vst[:, s, :].rearrange("p (h d) -> p h d", h=4), r=[("vst", s)], w=[("VS", unit, s)])

    def u_raw(rows, nsub, pbase, pu, pk):
        for h in range(4):
            for s in range(nsub):
                mm(pu[:, h * 128:(h + 1) * 128], ke[pbase:pbase + rows, s, h * 128:(h + 1) * 128], vR[pbase:pbase + rows, s, h * 128:(h + 1) * 128],
                   s == 0, s == nsub - 1, [("ke", s), ("vR", s)], [pk])

    def retention_prompt():
        def scores(h):
            for j in range(4):
                pj, pk = next_acc()
                n = 512 - 128 * j
                mm(pj[:, 0:n], keT[:, h, 128 * j:128 * j + 128], qdT[:, h, 128 * j:512], True, True, [("keT", h), ("qdT", h)], [pk])
                tt(scb[:, j, 0:128], pj[:, 0:128], trib[:, :], ALU.mult, [pk, "trib"], [("Pm", j)])
                if n > 128:
                    cp(scb[:, j, 128:n], pj[:, 128:n], [pk], [("Pm", j)])

        def av(h, po, pok):
            mm(po[:, :], Sb[:, h, :], qdT[:, h, :], True, False, ["Sb", ("qdT", h)], [pok])
            for j in range(4):
                n = 512 - 128 * j
                mm(po[:, 128 * j:512], vR[:, j, h * 128:(h + 1) * 128], scb[:, j, 0:n], False, j == 3,
                   [("vR", j), ("Pm", j)], [pok])

        scores(0)
        for h in range(4):
            po, pok = psb[4 + h % 2], PS(4 + h % 2)
            av(h, po, pok)
            if h + 1 < 4:
                scores(h + 1)
            ret_epilogue(po, pok, h, 512)

    def ret_epilogue(po, pok, h, T):
        sq = et[:, 2, 0:T]
        act(sqb[:, 0:T], po[:, 0:T], AF.Square, [pok], ["sqb"])
        pss, psk = psb[7], PS(7)
        mm(pss[:, 0:T], onesb[:, :], sqb[:, 0:T], True, True, ["onesb", "sqb"], [psk])
        rs_ = et[:, 3, 0:T]
        act(rs_, pss[:, 0:T], AF.Ln, [psk, "epsc"], ["et3"], scale=1.0 / 128, bias=cols[:, EPSC:EPSC + 1])
        act(rs_, rs_, AF.Exp, ["et3"], ["et3"], scale=-0.5)
        tt(sq, po[:, 0:T], rs_, ALU.mult, [pok, "et3", "et2"], ["et2"])
        tt(oT[:, 4 + h, 0:T], sq, sgT[:, h, 0:T], ALU.mult, ["et2", ("sgT", h)], [("oT", 4 + h)])

    kv_rr = [0]

    def attention(qc0, N, units, h, first_full=True):
        pO = [psb[4], psb[5]]
        pOk = [PS(4), PS(5)]
        pS = [psb[6], psb[7]]
        pSk = [PS(6), PS(7)]
        tiles = []
        for u in units:
            for (kt, c0, kind, arg) in u["tiles"]:
                tiles.append((u, kt, c0, kind, arg))
        n_tiles = len(tiles)
        info = {}

        def unit_ops(u):
            if id(u) in info:
                return info[id(u)]
            if u.get("sbuf"):
                ktv, vsv = u["ktv"], u["vsv"]
                r = (lambda m, kt: ktv, lambda kt: vsv, u["kkeys"], u["vkeys"], u["pbase"], u["nkeys"])
            else:
                slot = kv_rr[0] % NKB
                kv_rr[0] += 1
                ld(ktb[:, slot, :], u["kt_src"], r=u["kt_keys"], w=[("ktb", slot)])
                ld(vsb[:, slot, :].rearrange("p (k d) -> p k d", k=4), u["vs_src"], r=u["vs_keys"], w=[("vsb", slot)])
                r = (lambda m, kt: ktb[:, slot, kt * 128:(kt + 1) * 128],
                     lambda kt: vsb[:, slot, kt * 128:(kt + 1) * 128], [("ktb", slot)], [("vsb", slot)], 0, 128)
            info[id(u)] = r
            return r

        def emit_qk(t):
            u, kt, c0, kind, arg = tiles[t]
            lk, lv, kkeys, vkeys, pb, nkeys = unit_ops(u)
            pr = slice(pb, pb + nkeys)
            for m in range(2):
                bi = 2 * (t % 2) + m
                mm(psb[bi][pr, c0:N], lk(m, kt), qAT[:, m, h, qc0 + c0:qc0 + N], True, True, kkeys + [("qAT", h)], [PS(bi)])

        def emit_exp(t):
            u, kt, c0, kind, arg = tiles[t]
            lk, lv, kkeys, vkeys, pb, nkeys = unit_ops(u)
            pr = slice(pb, pb + nkeys)
            n = N - c0
            for m in range(2):
                bi = 2 * (t % 2) + m
                pq, pqk = psb[bi], PS(bi)
                pm = Pm[pr, bi, :]
                pmk = ("Pm", bi)
                if kind == "const":
                    if m == 0:
                        b0 = 2 * (t % 2)
                        act(Pm[pr, b0:b0 + 2, c0:N], psall[pr, b0:b0 + 2, c0:N], AF.Exp, [PS(b0), PS(b0 + 1), "c15", "cc0", "cc4", "ccf"],
                            [("Pm", b0), ("Pm", b0 + 1)], scale=0.125, bias=arg[pr, :])
                else:
                    btile, w_, ccol = arg
                    w_ = min(w_, n)
                    lgv = lg[pr, m, 0:w_]
                    stt(lgv, pq[pr, c0:c0 + w_], 0.125, btile[:, 0:w_], ALU.mult, ALU.add, [pqk] + u.get("bkeys", []), [("lg", m)])
                    act(pm[:, c0:c0 + w_], lgv, AF.Exp, [("lg", m)], [pmk])
                    if w_ < n:
                        act(pm[:, c0 + w_:N], pq[pr, c0 + w_:N], AF.Exp, [pqk, "c15", "cc0", "cc4", "ccf"], [pmk], scale=0.125, bias=ccol[pr, :])

        def emit_av(t):
            u, kt, c0, kind, arg = tiles[t]
            lk, lv, kkeys, vkeys, pb, nkeys = unit_ops(u)
            pr = slice(pb, pb + nkeys)
            first, last = (t == 0), (t == n_tiles - 1)
            for m in range(2):
                bi = 2 * (t % 2) + m
                mm(pO[m][:, c0:N], lv(kt), Pm[pr, bi, c0:N], first, last, vkeys + [("Pm", bi)], [pOk[m]])
            for m in range(2):
                bi = 2 * (t % 2) + m
                mm(pS[m][:, c0:N], onesb[pr, :], Pm[pr, bi, c0:N], first, last, ["onesb", ("Pm", bi)], [pSk[m]])

        emit_qk(0)
        for t in range(n_tiles):
            emit_exp(t)
            if t + 1 < n_tiles:
                emit_qk(t + 1)
            emit_av(t)
        T = N
        rinv = lg[:, :, 0:T]
        act(rinv, psall[:, 6:8, 0:T], AF.Ln, [pSk[0], pSk[1]], [("lg", 0), ("lg", 1)])
        act(rinv, rinv, AF.Exp, [("lg", 0), ("lg", 1)], [("lg", 0), ("lg", 1)], scale=-1.0)
        r0 = lg[:, 0, 0:T]
        r1 = lg[:, 1, 0:T]
        a_ = et[:, 1, 0:T]
        b_ = et[:, 2, 0:T]
        tt(a_, pO[0][:, 0:T], r0, ALU.mult, [pOk[0], ("lg", 0)], ["et1"])
        tt(b_, pO[1][:, 0:T], r1, ALU.mult, [pOk[1], ("lg", 1)], ["et2"])
        stt(a_, b_, cols[:, NLAM:NLAM + 1], a_, ALU.mult, ALU.add, ["et1", "et2", "nlam"], ["et1"])
        act(sqb[:, 0:T], a_, AF.Square, ["et1", "et2"], ["sqb"])
        if N == 512:
            warm(WARM_EPI, 3)
        pss, psk = psb[0], PS(0)
        mm(pss[:, 0:T], onesb[:, :], sqb[:, 0:T], True, True, ["onesb", "sqb"], [psk])
        rs_ = et[:, 3, 0:T]
        act(rs_, pss[:, 0:T], AF.Ln, [psk, "epsc"], ["et3"], scale=1.0 / 128, bias=cols[:, EPSC:EPSC + 1])
        act(rs_, rs_, AF.Exp, ["et3"], ["et3"], scale=-0.5)
        stt(oT[:, h, qc0:qc0 + T], a_, cols[:, GSUB:GSUB + 1], rs_, ALU.mult, ALU.mult, ["et1", "et3", "gsubc"], [("oT", h)])

    def prompt_units(step, h):
        units = []
        c15c = cols[:, C15 + h:C15 + h + 1]
        cc0c = cols[:, CC0 + h:CC0 + h + 1]
        cc4c = cols[:, CC4 + h:CC4 + h + 1]
        ccfc = cols[:, CCF + h:CCF + h + 1]
        for u in range(2 * step + 2):
            d = dict(kt_src=KT[h, u], vs_src=VS[h, u], kt_keys=[("KT", u)], vs_keys=[("VS", u, s) for s in range(4)])
            if u == 2 * step + 1:
                d["tiles"] = [(kt, 128 * kt, "bias", (Mh[:, h, :], 256, c15c)) for kt in range(4)]
                d["bkeys"] = [("Mh", h)]
            elif u == 2 * step:
                d["tiles"] = [(kt, 0, "const", ccfc) for kt in range(3)] + [(3, 0, "bias", (G4[:, h, :], 128, cc4c))]
                d["bkeys"] = [("G4", h)]
            elif u == 2 * step - 2:
                d["tiles"] = [(kt, 0, "const", c15c) for kt in range(3)] + [(3, 0, "bias", (G0[:, h, :], 128, cc0c))]
                d["bkeys"] = [("G0", h)]
            else:
                d["tiles"] = [(kt, 0, "const", c15c) for kt in range(4)]
            units.append(d)
        return units

    def out_proj(rows, nsub, xviews, xkeys):
        okeys = [("oT", i) for i in range(8)]
        for cb2 in range(2):
            W, wkey = get_w(("Wo", cb2))
            for s in range(nsub):
                acc, ak = next_acc()
                for k in range(8):
                    mm(acc[0:rows, :], oT[:, k, s * rows:(s + 1) * rows], W[:, k, :], k == 0, k == 7, [wkey] + okeys, [ak])
                t = et[0:rows, (cb2 * nsub + s) % 2, :]
                tkey = f"et{(cb2 * nsub + s) % 2}"
                tt(t, acc[0:rows, :], gate1b[0:rows, cb2 * 512:(cb2 + 1) * 512], ALU.mult, [ak, ("gate", 0, cb2)], [tkey])
                xv = xviews[s][:, cb2 * 512:(cb2 + 1) * 512]
                tt(xv, t, xv, ALU.add, [tkey, xkeys[s]], [xkeys[s]])

    def ffn(rows, nsub, xviews, xkeys, sample, store, gb_hooks=None, mid_hook=None, next_w=None):
        T = rows * nsub
        hk_ = hT_keys(sample)
        for gb in range(11):
            W, wkey = get_w(("Wgu", gb))
            prefetch(("Wgu", gb + 1) if gb + 1 < 11 else ("Wd", 0, 0))
            if gb_hooks is not None and gb in gb_hooks:
                gb_hooks[gb]()
            for fc in range(2):
                pg, pgk = next_acc()
                pu, puk = next_acc()
                for k in range(8):
                    mm(pg[:, 0:T], W[:, k, fc * 128:(fc + 1) * 128], hT[:, k, 0:T], k == 0, k == 7, [wkey] + hk_, [pgk])
                for k in range(8):
                    mm(pu[:, 0:T], W[:, k, 256 + fc * 128:256 + (fc + 1) * 128], hT[:, k, 0:T], k == 0, k == 7, [wkey] + hk_, [puk])
                sgi = (2 * gb + fc) % 2
                sg = lg[:, sgi, 0:T]
                act(sg, pg[:, 0:T], AF.Silu, [pgk], [("lg", sgi)])
                tt(ffT[:, 2 * gb + fc, 0:T], pu[:, 0:T], sg, ALU.mult, [puk, ("lg", sgi)], [("ffT", 2 * gb + fc)])
        fkeys = [("ffT", i) for i in range(NKF)]
        if not sample:
            chk(10)
        if mid_hook is not None:
            mid_hook()
        for half in range(2):
            accs = [next_acc() for s in range(nsub)]
            for bi, (k0, k1) in enumerate(((0, 8), (8, 16), (16, 22))):
                W, wkey = get_w(("Wd", half, bi))
                nxt = ("Wd", half, bi + 1) if bi < 2 else (("Wd", 1, 0) if half == 0 else next_w)
                if nxt is not None:
                    prefetch(nxt)
                for s in range(nsub):
                    acc, ak = accs[s]
                    for kk in range(k0, k1):
                        mm(acc[0:rows, :], ffT[:, kk, s * rows:(s + 1) * rows], W[:, kk - k0, :], kk == 0, kk == NKF - 1, [wkey] + fkeys, [ak])
            for s in range(nsub):
                acc, ak = accs[s]
                t = et[0:rows, s % 2, :]
                tkey = f"et{s % 2}"
                tt(t, acc[0:rows, :], gate2b[0:rows, half * 512:(half + 1) * 512], ALU.mult, [ak, ("gate", 1, half)], [tkey])
                xv = xviews[s][:, half * 512:(half + 1) * 512]
                tt(xv, t, xv, ALU.add, [tkey, xkeys[s]], [xkeys[s]])
                if half == 1:
                    store(s)

    setup()
    prepass_in()
    gates(sample=False)

    def sample_cache_chunk(q, u):
        for h in range(4):
            src = cv[q, u * 512:(u + 1) * 512, h, :].rearrange("(kt p) d -> p kt d", p=128)
            stq(VSs[q, h, u], src, w=[("VSs", q, h, u)])
        kcb = vst
        stq(kcb[:, :, :], ck[q, u * 512:(u + 1) * 512, :].rearrange("(kt p) c -> p kt c", p=128), w=[("vst", s) for s in range(4)])
        for hp in range(2):
            tbb, tk = next_tb()
            for half in range(2):
                h = 2 * hp + half
                for kt in range(4):
                    trn(tbb[:, half * 512 + kt * 128:half * 512 + (kt + 1) * 128], kcb[:, kt, h * 128:(h + 1) * 128], identb[:, :], [("vst", kt), "identb"], [tk])
            for half in range(2):
                h = 2 * hp + half
                act(kst[:, h, :], tbb[:, half * 512:(half + 1) * 512], AF.Copy, [tk], [("kst", h)])
        stq(KTs[q, :, u].rearrange("h p c -> p h c"), kst[:, :, :], r=[("kst", h) for h in range(4)], w=[("KTs", q, u)])

    sample_chunks = [(q, u) for q in range(2) for u in range(4)]

    try:
        def foreign_sub(slot, s):
            ld(xf[:, 0, :], xs[slot, s * 128:(s + 1) * 128, :], w=[("xf", 0)])
            norm_part([xf[:, 0, :]], [("xf", 0)], 128, [s], 24 + s, junk_pm=True)

        def foreign_part1(slot):
            for s in range(4):
                foreign_sub(slot, s)

        def foreign_part2():
            transp_part(128, 4, 1, 0, False)

        if n_steps > 0:
            foreign_part1(0)
            foreign_part2()
        for step in range(n_steps):
            slotF, slotO = 2 * step, 2 * step + 1
            xviews = [xo[:, s, :] for s in range(4)]
            xkeys = [("xo", s) for s in range(4)]
            def own_norm(xviews=xviews, xkeys=xkeys, slotO=slotO):
                for s in range(4):
                    ld(xviews[s], xs[slotO, s * 128:(s + 1) * 128, :], w=[xkeys[s]])
                norm_part(xviews, xkeys, 128, [0, 1, 2, 3], 0)
            chk(1)
            in_proj(128, 4, False, False, lambda s, slot=slotF: rope_d[slot, s * 128:(s + 1) * 128], step, hook_after_first=own_norm, next_w=("Wi", 1))
            chk(2)
            store_kv(slotF)
            pu, pk = psb[7], PS(7)
            u_raw(128, 4, 0, pu, pk)
            cp(Uf[:].rearrange("p h e -> p (h e)"), pu[:, :], [pk], ["Uf"])
            for h in range(4):
                ts(tmpS[:, h, :], S[:, h, :], dco[:, h:h + 1], None, ALU.mult, None, ["S", "dco"], [("tmpS", h)])
                stt(Sst[:, h, :], Uf[:, h, :], dco[:, 4 + h:5 + h], tmpS[:, h, :], ALU.mult, ALU.add, ["Uf", "dco", ("tmpS", h)], [("Sst", h)])
            cp(Sb[:].rearrange("p h e -> p (h e)"), Sst[:].rearrange("p h e -> p (h e)"), [("Sst", h) for h in range(4)], ["Sb"])
            chk(3)
            transp_part(128, 4, 1, 0, False)
            chk(4)
            in_proj(128, 4, True, False, lambda s, slot=slotO: rope_d[slot, s * 128:(s + 1) * 128], step)
            chk(5)
            store_kv(slotO)
            if step == 0:
                prepass_rest()
            if do_sample and sample_chunks and (step >= 1 or n_steps == 1):
                for _ in range(8 if n_steps == 1 else (2 if len(sample_chunks) > 8 - step else 1)):
                    if sample_chunks:
                        sample_cache_chunk(*sample_chunks.pop(0))
            retention_prompt()
            pu, pk = psb[7], PS(7)
            u_raw(128, 4, 0, pu, pk)
            for h in range(4):
                ts(tmpS[:, h, :], S[:, h, :], dco[:, 8 + h:9 + h], None, ALU.mult, None, ["S", "dco"], [("tmpS", h)])
                stt(tmpS[:, h, :], Uf[:, h, :], dco[:, 16 + h:17 + h], tmpS[:, h, :], ALU.mult, ALU.add, ["Uf", "dco", ("tmpS", h)], [("tmpS", h)])
                stt(S[:, h, :], pu[:, h * 128:(h + 1) * 128], dco[:, 12 + h:13 + h], tmpS[:, h, :], ALU.mult, ALU.add, [pk, "dco", ("tmpS", h)], ["S"])
            chk(6)
            prefetch(("Wo", 0))
            prefetch(("Wo", 1))
            prefetch(("Wgu", 0))
            for h in range(4):
                attention(0, 512, prompt_units(step, h), h)
            chk(7)
            out_proj(128, 4, xviews, xkeys)
            chk(8)
            phase_norm(xviews, xkeys, 128, 4, 3, 2, False)
            chk(9)

            def store(s, step=step, xviews=xviews, xkeys=xkeys):
                out_ops.append(stq(y[step, s * 128:(s + 1) * 128, :], xviews[s], r=[xkeys[s]]))
            if step + 1 < n_steps:
                nslot = 2 * (step + 1)
                ffn(128, 4, xviews, xkeys, False, store,
                    gb_hooks={1 + 2 * s_: (lambda nslot=nslot, s_=s_: foreign_sub(nslot, s_)) for s_ in range(4)}, mid_hook=foreign_part2,
                    next_w=("Wi", 1))
            else:
                ffn(128, 4, xviews, xkeys, False, store)
        out_ops.append(stq(rp[:].rearrange("h d e -> d h e"), S[:, :, :], r=["S"]))

        if do_sample:
            while sample_chunks:
                sample_cache_chunk(*sample_chunks.pop(0))
            gates(sample=True)
            xviews = [xo[0:64, 0, :]]
            xkeys = [("xo", 0)]
            ld(xviews[0], xsm[:, :], w=[xkeys[0]])
            phase_norm(xviews, xkeys, 64, 1, 1, 0, True)
            in_proj(64, 1, True, True, lambda s: rope_sd[:, :, :], 0)
            SH = [Sst, tmpS]
            SHk = [[("Sst", h) for h in range(4)], [("tmpS", h) for h in range(4)]]
            for q in range(2):
                ld(SH[q][:, :, :], st[q].rearrange("h d e -> d h e"), w=SHk[q])
            for q in range(2):
                shb = Sb
                cp(shb[:].rearrange("p h e -> p (h e)"), SH[q][:].rearrange("p h e -> p (h e)"), SHk[q], ["Sb"])
                pr = slice(32 * q, 32 * q + 32)
                for h in range(4):
                    pj, pk = next_acc()
                    mm(pj[pr, 0:32], keT[:, h, 32 * q:32 * q + 32], qdT[:, h, 32 * q:32 * q + 32], True, True, [("keT", h), ("qdT", h)], [pk])
                    tt(scb[pr, 0, 0:32], pj[pr, 0:32], trib[pr, 32 * q:32 * q + 32], ALU.mult, [pk, "trib"], [("Pm", 0)])
                    po, pok = psb[6], PS(6)
                    mm(po[:, 0:32], shb[:, h, :], qdT[:, h, 32 * q:32 * q + 32], True, False, ["Sb", ("qdT", h)], [pok])
                    mm(po[:, 0:32], vR[pr, 0, h * 128:(h + 1) * 128], scb[pr, 0, 0:32], False, True, [("vR", 0), ("Pm", 0)], [pok])
                    sq = et[:, 2, 0:32]
                    act(sq, po[:, 0:32], AF.Square, [pok], ["et2"])
                    pss, psk = psb[7], PS(7)
                    mm(pss[:, 0:32], onesf[:, :], sq, True, True, ["onesf", "et2"], [psk])
                    rs_ = et[:, 3, 0:32]
                    act(rs_, pss[:, 0:32], AF.Ln, [psk, "epsc"], ["et3"], scale=1.0 / 128, bias=cols[:, EPSC:EPSC + 1])
                    act(rs_, rs_, AF.Exp, ["et3"], ["et3"], scale=-0.5)
                    tt(sq, po[:, 0:32], rs_, ALU.mult, [pok, "et3", "et2"], ["et2"])
                    tt(oT[:, 4 + h, 32 * q:32 * q + 32], sq, sgT[:, h, 32 * q:32 * q + 32], ALU.mult, ["et2", ("sgT", h)], [("oT", 4 + h)])
                pu, pk = psb[7], PS(7)
                u_raw(32, 1, 32 * q, pu, pk)
                for h in range(4):
                    tt(Uf[:, h, :], pu[:, h * 128:(h + 1) * 128], SH[q][:, h, :], ALU.add, [pk] + SHk[q], ["Uf"])
                    ts(Uf[:, h, :], Uf[:, h, :], dco[:, 28 + h:29 + h], None, ALU.mult, None, ["Uf", "dco"], ["Uf"])
                out_ops.append(stq(rs[q].rearrange("h d e -> d h e"), Uf[:, :, :], r=["Uf"]))
            for q in range(2):
                for h in range(4):
                    c15c = cols[:, C15 + h:C15 + h + 1]
                    units = []
                    for u in range(4):
                        d = dict(kt_src=KTs[q, h, u], vs_src=VSs[q, h, u], kt_keys=[("KTs", q, u)], vs_keys=[("VSs", q, h, u)])
                        if u < 3:
                            d["tiles"] = [(kt, 0, "const", c15c) for kt in range(4)]
                        else:
                            d["tiles"] = [(kt, 0, "const", c15c) for kt in range(3)] + [(3, 0, "bias", (Mh[:, h, 128:160], 32, None))]
                            d["bkeys"] = [("Mh", h)]
                        units.append(d)
                    pr = slice(32 * q, 32 * q + 32)
                    units.append(dict(sbuf=True, ktv=kst[:, h, 32 * q:32 * q + 32], vsv=vst[pr, 0, h * 128:(h + 1) * 128],
                                      kkeys=[("kst", h)], vkeys=[("vst", 0)], pbase=32 * q, nkeys=32,
                                      tiles=[(0, 0, "bias", (MS[pr, h, :], 32, None))], bkeys=["MS0", "MS1"]))
                    attention(32 * q, 32, units, h)
            out_proj(64, 1, xviews, xkeys)
            phase_norm(xviews, xkeys, 64, 1, 3, 2, True)

            def store_s(s):
                out_ops.append(stq(ysm[:, :], xviews[0], r=[xkeys[0]]))
            ffn(64, 1, xviews, xkeys, True, store_s)


    except _StopBuild:
        pass
    P.emit(final_wait_ops=out_ops)
    return nc, P


_PROG_CACHE = {}


def kernel(x_prompt, x_sample, c_prompt, c_sample, cache_k, cache_v, state_ret, w_ada, b_ada,
           g_norm1, g_norm2, w_in, g_q, g_k, lam_q1, lam_k1, lam_q2, lam_k2, g_subln, w_out,
           w_ff_gate, w_ff_up, w_ff_down, rel_bias):
    f = lambda a: np.ascontiguousarray(np.asarray(a, dtype=np.float32))
    x_prompt, x_sample, c_prompt, c_sample = f(x_prompt), f(x_sample), f(c_prompt), f(c_sample)
    cache_k, cache_v, state_ret = f(cache_k), f(cache_v), f(state_ret)
    C = _static_consts()
    if "nc" not in _PROG_CACHE:
        _PROG_CACHE["nc"] = build_program()
    nc, P = _PROG_CACHE["nc"]
    shared = dict(
        w_ada=f(w_ada)[0], b_ada=f(b_ada), w_in=f(w_in)[0],
        gqk=np.concatenate([f(g_q), f(g_k)], axis=1),
        lamv=np.concatenate([f(lam_q1), f(lam_k1), f(lam_q2), f(lam_k2)], axis=1),
        gsub=f(g_subln), w_out=f(w_out)[0], wg=f(w_ff_gate)[0], wu=f(w_ff_up)[0], wd=f(w_ff_down)[0],
        rbt=f(rel_bias), rope_s=C["rope_s"], ohr=C["ohr"], oh15=C["oh15"], maskadd=C["maskadd"], kc=C["kc"],
    )
    in_maps = []
    for c in range(8):
        b, p = c // 2, c % 2
        order = []
        for i in range(NT // 2):
            order += [2 * i + 1 - p, 2 * i + p]
        xb = x_prompt[b].reshape(NT, TT, D)[order]
        m = dict(shared)
        m.update(
            xs=np.ascontiguousarray(xb),
            xsm=np.ascontiguousarray(x_sample[2 * c:2 * c + 2].reshape(64, D)),
            crow=np.ascontiguousarray(np.concatenate([c_prompt[b:b + 1], c_sample[2 * c:2 * c + 2], f(g_norm1), f(g_norm2)], axis=0)),
            ck=np.ascontiguousarray(cache_k[0, 2 * c:2 * c + 2].reshape(2, PAST, 512)),
            cv=np.ascontiguousarray(cache_v[0, 2 * c:2 * c + 2]),
            st=np.ascontiguousarray(state_ret[0, 2 * c:2 * c + 2]),
            rope=C["rope"][p], dco=C["dco"][p],
        )
        in_maps.append(m)
    res = run_bass_kernel_spmd(nc, in_maps, core_ids=list(range(8)))
    R = res.results
    y_prompt = np.zeros((4, SEQ, D), np.float32)
    k_prompt = np.zeros((1, 4, SEQ, 4, 2, 64), np.float32)
    v_prompt = np.zeros((1, 4, SEQ, 4, 128), np.float32)
    ret_prompt = np.zeros((1, 4, 4, 128, 128), np.float32)
    y_sample = np.zeros((16, LS, D), np.float32)
    k_sample = np.zeros((1, 16, LS, 4, 2, 64), np.float32)
    v_sample = np.zeros((1, 16, LS, 4, 128), np.float32)
    ret_sample = np.zeros((1, 16, 4, 128, 128), np.float32)
    for c in range(8):
        b, p = c // 2, c % 2
        r = R[c]
        for i in range(NT // 2):
            t = 2 * i + p
            y_prompt[b, t * TT:(t + 1) * TT] = r["y"][i]
            k_prompt[0, b, t * TT:(t + 1) * TT] = r["kp"][i].reshape(TT, 4, 2, 64)
            v_prompt[0, b, t * TT:(t + 1) * TT] = r["vp"][i].reshape(TT, 4, 128)
        if p == 0:
            ret_prompt[0, b] = r["rp"]
        y_sample[2 * c:2 * c + 2] = r["ysm"].reshape(2, LS, D)
        k_sample[0, 2 * c:2 * c + 2] = r["ksm"].reshape(2, LS, 4, 2, 64)
        v_sample[0, 2 * c:2 * c + 2] = r["vsm"].reshape(2, LS, 4, 128)
        ret_sample[0, 2 * c:2 * c + 2] = r["rs"]
    return (y_prompt, y_sample, k_prompt, v_prompt, ret_prompt, k_sample, v_sample, ret_sample)
```

```python
import math
import numpy as np
import concourse.bass as bass
import concourse.mybir as mybir
from concourse.bass_utils import run_bass_kernel_spmd

F32 = mybir.dt.float32
BF16 = mybir.dt.bfloat16
AF = mybir.ActivationFunctionType
ALU = mybir.AluOpType
AX = mybir.AxisListType
AP = bass.AP

D = 1024
DIN = 3584
DFF = 2816
NKF = DFF // 128
SEQ = 8192
TT = 512
NT = SEQ // TT
PAST = 2048
LS = 32
EPS = 1e-6
LAM_INIT = 0.8 - 0.6 * math.exp(-0.3 * 0)
NEG = -30000.0
WARM_EPI = 24
WARM_NORM = 30
ENGS = ("pe", "act", "dve", "pool", "sp")


class Op:
    __slots__ = ("eng", "fn", "deps", "is_dma", "signal", "val", "sem", "waits", "know")

    def __init__(self, eng, fn, deps, is_dma):
        self.eng = eng
        self.fn = fn
        self.deps = deps
        self.is_dma = is_dma
        self.signal = is_dma
        self.val = None
        self.sem = None


class Prog:
    def __init__(self, nc, n_dma_sems=24):
        self.nc = nc
        self.ops = {e: [] for e in ENGS}
        self.all_ops = []
        self.bufs = {}
        self.n_dma_sems = n_dma_sems
        self.dma_rr = {"sp": 0, "pool": 0, "act": 0}
        self.dma_last = {}

    @staticmethod
    def _flat(keys):
        out = []
        for k in keys:
            if isinstance(k, list):
                out.extend(k)
            else:
                out.append(k)
        return out

    def _deps_for(self, reads, writes):
        deps = []
        for k in reads:
            ent = self.bufs.get(k)
            if ent is not None and ent[0] is not None:
                deps.append(ent[0])
        for k in writes:
            ent = self.bufs.get(k)
            if ent is not None:
                if ent[0] is not None:
                    deps.append(ent[0])
                deps.extend(ent[1])
        return deps

    def _commit(self, op, reads, writes):
        for k in reads:
            ent = self.bufs.setdefault(k, [None, []])
            if not op.is_dma:
                ent[1] = [r for r in ent[1] if r.is_dma or r.eng != op.eng]
            ent[1].append(op)
        for k in writes:
            self.bufs[k] = [op, []]

    def op(self, eng, fn, reads=(), writes=()):
        reads, writes = self._flat(reads), self._flat(writes)
        o = Op(eng, fn, self._deps_for(reads, writes), False)
        self.all_ops.append(o)
        self.ops[eng].append(o)
        self._commit(o, reads, writes)
        return o

    def dma(self, queue, fn, reads=(), writes=()):
        reads, writes = self._flat(reads), self._flat(writes)
        deps = self._deps_for(reads, writes)
        j = self.dma_rr[queue]
        nq = self.n_dma_sems if queue == "sp" else 4
        self.dma_rr[queue] = (j + 1) % nq
        prev = self.dma_last.get((queue, j))
        if prev is not None:
            deps.append(prev)
        o = Op(queue, fn, deps, True)
        o.sem = (queue, j)
        self.all_ops.append(o)
        self.ops[queue].append(o)
        self.dma_last[(queue, j)] = o
        self._commit(o, reads, writes)
        return o

    def emit(self, final_wait_ops=()):
        nc = self.nc
        for o in self.all_ops:
            for d in o.deps:
                if d.is_dma:
                    continue
                if d.eng == o.eng and d.eng == "pe" and not o.is_dma:
                    continue
                d.signal = True
        for o in final_wait_ops:
            if not o.is_dma:
                o.signal = True
        for e in ("pe", "act", "dve", "pool"):
            comp = [o for o in self.ops[e] if not o.is_dma]
            if comp:
                comp[-1].signal = True
        sem_h = {}
        for e in ("pe", "act", "dve", "pool"):
            sem_h[("eng", e)] = nc.alloc_semaphore(f"s_{e}")
        for q in ("sp", "pool", "act"):
            used = set(o.sem[1] for o in self.ops[q] if o.is_dma)
            for j in sorted(used):
                sem_h[(q, j)] = nc.alloc_semaphore(f"d_{q}{j}")
        cnt = {}
        for e in ENGS:
            for o in self.ops[e]:
                if o.is_dma:
                    k = o.sem
                    cnt[k] = cnt.get(k, 0) + 16
                    o.val = cnt[k]
                elif o.signal:
                    k = ("eng", e)
                    cnt[k] = cnt.get(k, 0) + 1
                    o.val = cnt[k]
                    o.sem = k
        self.stats = dict(n_ops={e: len(self.ops[e]) for e in ENGS}, max_sem=max(cnt.values()),
                          n_sig={e: cnt.get(("eng", e), 0) for e in ENGS}, n_wait={})
        prog = self
        eng_know = {e: {} for e in ENGS}
        order = {id(o): i for i, o in enumerate(self.all_ops)}
        for o in self.all_ops:
            e = o.eng
            ek = eng_know[e]
            waits = []
            deps = [d for d in o.deps if not ((not d.is_dma) and d.eng == e and e == "pe" and not o.is_dma)]
            deps.sort(key=lambda d: -order[id(d)])
            for d in deps:
                k, v = d.sem, d.val
                if ek.get(k, 0) >= v:
                    continue
                waits.append((k, v))
                for kk, vv in d.know.items():
                    if vv > ek.get(kk, 0):
                        ek[kk] = vv
                if v > ek.get(k, 0):
                    ek[k] = v
            o.waits = waits
            kn = dict(ek)
            if o.sem is not None and o.val is not None:
                if o.val > kn.get(o.sem, 0):
                    kn[o.sem] = o.val
            o.know = kn

        def run_engine(e, h):
            nwait = 0
            for o in prog.ops[e]:
                for k, v in o.waits:
                    h.wait_ge(sem_h[k], v)
                    nwait += 1
                ins = o.fn(h)
                if o.is_dma:
                    ins.then_inc(sem_h[o.sem], 16)
                elif o.signal:
                    ins.then_inc(sem_h[o.sem], 1)
            if e == "sp":
                for k, v in cnt.items():
                    h.wait_ge(sem_h[k], v)
            prog.stats["n_wait"][e] = nwait

        with nc.Block() as block:
            @block.tensor
            def _(h):
                run_engine("pe", h)

            @block.scalar
            def _(h):
                run_engine("act", h)

            @block.vector
            def _(h):
                run_engine("dve", h)

            @block.gpsimd
            def _(h):
                run_engine("pool", h)

            @block.sync
            def _(h):
                run_engine("sp", h)


def _t5_bucket_np(rel):
    import jax
    import jax.numpy as jnp
    cpu = jax.devices("cpu")[0]
    with jax.default_device(cpu):
        rel = jnp.asarray(rel, dtype=jnp.int32)
        nb = 16
        ret = jnp.where(rel > 0, nb, 0)
        n = jnp.abs(rel)
        max_exact = nb // 2
        large = max_exact + (jnp.log(jnp.maximum(n, 1).astype(jnp.float32) / max_exact)
                             / math.log(128 / max_exact) * (nb - max_exact)).astype(jnp.int32)
        large = jnp.minimum(large, nb - 1)
        out = ret + jnp.where(n < max_exact, n, large)
        return np.asarray(out)


_CONST_CACHE = {}


def _gammas():
    return (1.0 - 2.0 ** (-5.0 - np.arange(4, dtype=np.float64)))


def _rope_tables(pos, L):
    n = pos.shape[0]
    inv = (np.float32(1.0) / (np.float32(10000.0) ** np.linspace(0.0, 1.0, 64, dtype=np.float32))).astype(np.float32)
    ang = pos.astype(np.float32)[:, None] * inv[None, :]
    cos = np.cos(ang).astype(np.float64)
    sin = np.sin(ang).astype(np.float64)
    l = (np.arange(n) % L).astype(np.float64)
    lg = np.log(_gammas())
    qd = np.exp((l[:, None] + 1.0) * lg[None, :])
    kd = np.exp(-(l[:, None] + 1.0) * lg[None, :]) / math.sqrt(128.0)
    out = np.zeros((n, 4, 4, 64), np.float64)
    out[:, 0] = cos[:, None, :] * qd[:, :, None]
    out[:, 1] = sin[:, None, :] * qd[:, :, None]
    out[:, 2] = cos[:, None, :] * kd[:, :, None]
    out[:, 3] = sin[:, None, :] * kd[:, :, None]
    return out.reshape(n, 4, 256).astype(np.float32)


def _static_consts():
    if "c" in _CONST_CACHE:
        return _CONST_CACHE["c"]
    g = _gammas()
    rel = 127 - np.arange(384)
    bk = _t5_bucket_np(rel)
    ohr = np.zeros((32, 384), np.float32)
    ohr[bk, np.arange(384)] = 1.0
    oh15 = np.zeros((32, 128), np.float32)
    b_far = int(_t5_bucket_np(np.array([-200]))[0])
    oh15[b_far, :] = 1.0
    maskadd = np.zeros((128, 256), np.float32)
    k = np.arange(128)[:, None]
    c = np.arange(256)[None, :]
    maskadd[(k // 64) > (c // 64)] = NEG
    kc = np.zeros((128, 448), np.float32)
    kc[0, 0:128] = 1.0
    kc[32, 128:256] = 1.0
    m = np.arange(128)[:, None]
    l = np.arange(128)[None, :]
    kc[:, 256:384] = (l >= m).astype(np.float32)
    kc2 = np.zeros((128, 256), np.float32)
    kc2[np.arange(128), 127 - np.arange(128)] = 1.0
    kc2[np.arange(128), 128 + np.arange(128)] = 1.0
    kc = np.concatenate([kc[:, :384], kc2], axis=1)
    dco = np.zeros((2, 128, 32), np.float32)
    for p in range(2):
        a = np.ones(4) if p == 0 else g ** 512
        b = np.zeros(4) if p == 0 else g ** 512
        e = g ** 1024 if p == 0 else g ** 512
        f = g ** 512 if p == 0 else g ** 1024
        dco[p, :, 0:4] = a
        dco[p, :, 4:8] = b
        dco[p, :, 8:12] = g ** 1024
        dco[p, :, 12:16] = e
        dco[p, :, 16:20] = f
        sel = [1, 0, 0, 0, 0, NEG, 0, NEG] if p == 0 else [0, 1, 0, 1, 0, 0, 1, 0]
        dco[p, :, 20:28] = np.array(sel, np.float32)
        dco[p, :, 28:32] = g ** 32
    rope = np.zeros((2, NT, TT, 4, 256), np.float32)
    for p in range(2):
        for i in range(NT // 2):
            for j, tile in enumerate((2 * i + 1 - p, 2 * i + p)):
                pos = tile * TT + np.arange(TT)
                rope[p, 2 * i + j] = _rope_tables(pos, TT)
    rope_s = _rope_tables(PAST + (np.arange(64) % LS), LS)
    res = dict(ohr=ohr, oh15=oh15, maskadd=maskadd, kc=kc, dco=dco, rope=rope, rope_s=rope_s)
    _CONST_CACHE["c"] = res
    return res


class _StopBuild(Exception):
    pass


def build_program(n_steps=NT // 2, do_sample=True, cut=0):
    def chk(n):
        if cut == n:
            raise _StopBuild()
    nc = bass.Bass("TRN2", target_bir_lowering=False)
    P = Prog(nc)

    def din(name, shape, dt=F32):
        return nc.dram_tensor(name, list(shape), dt, kind="ExternalInput").ap()

    def dout(name, shape, dt=F32):
        return nc.dram_tensor(name, list(shape), dt, kind="ExternalOutput").ap()

    def dscr(name, shape, dt):
        return nc.dram_tensor(name, list(shape), dt, kind="Internal").ap()

    def sb(name, shape, dt):
        return nc.alloc_sbuf_tensor("s_" + name, list(shape), dt).ap()

    xs = din("xs", [NT, TT, D])
    xsm = din("xsm", [64, D])
    crow_d = din("crow", [5, D])
    ck = din("ck", [2, PAST, 512])
    cv = din("cv", [2, PAST, 4, 128])
    st = din("st", [2, 4, 128, 128])
    w_ada = din("w_ada", [D, 6 * D])
    b_ada = din("b_ada", [1, 6 * D])
    w_in = din("w_in", [D, DIN])
    gqk = din("gqk", [1, 128])
    lamv = din("lamv", [1, 256])
    gsub = din("gsub", [1, 128])
    w_out = din("w_out", [D, D])
    wg = din("wg", [D, DFF])
    wu = din("wu", [D, DFF])
    wd = din("wd", [DFF, D])
    rbt_d = din("rbt", [32, 4])
    rope_d = din("rope", [NT, TT, 4, 256])
    rope_sd = din("rope_s", [64, 4, 256])
    dco_d = din("dco", [128, 32])
    ohr_d = din("ohr", [32, 384])
    oh15_d = din("oh15", [32, 128])
    maskadd_d = din("maskadd", [128, 256])
    kc_d = din("kc", [128, 640])

    y = dout("y", [NT // 2, TT, D])
    ysm = dout("ysm", [64, D])
    kp = dout("kp", [NT // 2, TT, 512])
    vp = dout("vp", [NT // 2, TT, 512])
    rp = dout("rp", [4, 128, 128])
    ksm = dout("ksm", [64, 512])
    vsm = dout("vsm", [64, 512])
    rs = dout("rs", [2, 4, 128, 128])

    Wi = dscr("Wi", [7, 128, 8, 512], BF16)
    Wo = dscr("Wo", [2, 128, 8, 512], BF16)
    Wgu = dscr("Wgu", [11, 128, 8, 512], BF16)
    Wd = dscr("Wd", [2, 128, NKF, 512], BF16)
    KT = dscr("KT", [4, NT, 128, 512], BF16)
    VS = dscr("VS", [4, NT, 128, 4, 128], BF16)
    KTs = dscr("KTs", [2, 4, 4, 128, 512], BF16)
    VSs = dscr("VSs", [2, 4, 4, 128, 4, 128], BF16)
    urd = dscr("urd", [4, 384], F32)

    kc = sb("kc", [128, 640], F32)
    sel0 = kc[0:64, 0:128]
    sel1 = kc[0:64, 128:256]
    trif = kc[:, 256:384]
    Jf = kc[:, 384:512]
    identf = kc[:, 512:640]
    identb = sb("identb", [128, 128], BF16)
    trib = sb("trib", [128, 128], BF16)
    onesb = sb("onesb", [128, 128], BF16)
    onesf = sb("onesf", [128, 128], F32)
    dco = sb("dco", [128, 32], F32)
    Mh = sb("Mh", [128, 4, 256], F32)
    G0 = sb("G0", [128, 4, 128], F32)
    G4 = sb("G4", [128, 4, 128], F32)
    MS = sb("MS", [64, 4, 32], F32)
    cols = sb("cols", [128, 64], F32)
    C15, CC0, CC4, CCF, NLAM, GSUB, EPSC = 0, 4, 8, 12, 16, 17, 18
    modc = sb("modc", [128, 4, 8, 3], F32)
    gate1b = sb("gate1b", [128, D], F32)
    gate2b = sb("gate2b", [128, D], F32)
    gqkb = sb("gqkb", [128, 128], F32)
    cT = sb("cT", [128, 8, 5], F32)
    scT = sb("scT", [128, 8, 3], BF16)
    scB = sb("scB", [128, 8, 128], BF16)
    scS = sb("scS", [128, 8, 64], BF16)
    browb = sb("browb", [1, 1, 512], F32)
    smallr = sb("smallr", [32, 512], F32)
    xo = sb("xo", [128, 4, D], F32)
    xf = sb("xf", [128, 1, D], F32)
    crow5 = xf[0:5, 0, :]
    gfull = sb("gfull", [128, 2, 512], F32)
    hT = sb("hT", [128, 8, TT], BF16)
    rt = sb("rt", [128, 2, 4, 256], F32)
    tokb = sb("tokb", [128, 4, 512], BF16)
    qAT = sb("qAT", [128, 2, 4, TT], BF16)
    kst = sb("kst", [128, 4, TT], BF16)
    vst = sb("vst", [128, 4, 512], BF16)
    kf = sb("kf", [128, 1, 512], F32)
    vf = sb("vf", [128, 1, 512], F32)
    qdT = sb("qdT", [128, 4, TT], BF16)
    keT = sb("keT", [128, 4, TT], BF16)
    ke = sb("ke", [128, 4, 512], BF16)
    vR = sb("vR", [128, 4, 512], BF16)
    sgT = sb("sgT", [128, 4, TT], BF16)
    oT = sb("oT", [128, 8, TT], BF16)
    ffT = sb("ffT", [128, NKF, TT], BF16)
    NW = 3
    wring = sb("wring", [128, NW, 4096], BF16)
    NKB = 2
    ktb = sb("ktb", [128, NKB, 512], BF16)
    vsb = sb("vsb", [128, NKB, 512], BF16)
    Pm = sb("Pm", [128, 4, 512], BF16)
    scb = Pm
    lg = sb("lg", [128, 2, 512], F32)
    et = sb("et", [128, 4, 512], F32)
    xn = sb("xn", [128, 4, D], BF16)
    lamt = sb("lamt", [1, 128], F32)
    sqb = sb("sqb", [128, 512], BF16)
    warmb = sb("warmb", [128, 512], BF16)
    S = sb("S", [128, 4, 128], F32)
    Sst = sb("Sst", [128, 4, 128], F32)
    tmpS = sb("tmpS", [128, 4, 128], F32)
    Sb = sb("Sb", [128, 4, 128], BF16)
    Uf = sb("Uf", [128, 4, 128], F32)
    stat = sb("stat", [128, 64], F32)

    psall = nc.alloc_psum_tensor("psall", [128, 8, 512], F32).ap()
    psallb = psall.bitcast(BF16)
    psb = [psall[:, i, :] for i in range(8)]
    psbb = [psallb[:, i, :] for i in range(8)]

    def PS(i):
        return ("ps", i)

    def mm(out, lhsT, rhs, start, stop, r, w):
        return P.op("pe", lambda e: e.matmul(out, lhsT=lhsT, rhs=rhs, start=start, stop=stop), reads=r, writes=w)

    def warm(n, bank):
        for _ in range(n):
            mm(psb[bank][:, :], onesb[:, :], warmb[:, :], True, True, ["onesb", "warmb"], [PS(bank)])

    def trn(out, in_, ident, r, w):
        return P.op("pe", lambda e: e.transpose(out, in_, ident), reads=r, writes=w)

    def act(out, in_, func, r, w, scale=1.0, bias=0.0, accum=None):
        if accum is None:
            return P.op("act", lambda e: e.activation(out=out, in_=in_, func=func, bias=bias, scale=scale), reads=r, writes=w)
        return P.op("act", lambda e: e.activation(out=out, in_=in_, func=func, bias=bias, scale=scale, accum_out=accum), reads=r, writes=w)

    def tt(out, in0, in1, op, r, w, eng="dve"):
        return P.op(eng, lambda e: e.tensor_tensor(out=out, in0=in0, in1=in1, op=op), reads=r, writes=w)

    def ts(out, in0, s1, s2, op0, op1, r, w, eng="dve"):
        if op1 is None:
            return P.op(eng, lambda e: e.tensor_scalar(out=out, in0=in0, scalar1=s1, scalar2=None, op0=op0), reads=r, writes=w)
        return P.op(eng, lambda e: e.tensor_scalar(out=out, in0=in0, scalar1=s1, scalar2=s2, op0=op0, op1=op1), reads=r, writes=w)

    def stt(out, in0, scalar, in1, op0, op1, r, w):
        return P.op("dve", lambda e: e.scalar_tensor_tensor(out=out, in0=in0, scalar=scalar, in1=in1, op0=op0, op1=op1), reads=r, writes=w)

    def cp(out, in_, r, w, eng="dve"):
        return P.op(eng, lambda e: e.tensor_copy(out=out, in_=in_), reads=r, writes=w)

    def ld(out, in_, r=(), w=()):
        return P.dma("sp", lambda e: e.dma_start(out=out, in_=in_), reads=r, writes=w)

    def stq(out, in_, r=(), w=()):
        return P.dma("pool", lambda e: e.dma_start(out=out, in_=in_), reads=r, writes=w)

    ev_rr = [0]

    def evac(out, in_, r, w):
        ev_rr[0] ^= 1
        if ev_rr[0]:
            return act(out, in_, AF.Copy, r, w)
        return cp(out, in_, r, w)

    def fr(ap, dims):
        return AP(ap.tensor, ap.offset, [list(ap.ap[0])] + [list(d) for d in dims])

    out_ops = []

    def prepass_in():
        for cb in (1, 2, 4, 5, 0, 3, 6):
            src = w_in[:, cb * 512:(cb + 1) * 512].rearrange("(k p) c -> p k c", p=128)
            stq(Wi[cb], src, w=[("Wi", cb)])

    def prepass_rest():
        for cb in range(2):
            src = w_out[:, cb * 512:(cb + 1) * 512].rearrange("(k p) c -> p k c", p=128)
            stq(Wo[cb], src, w=[("Wo", cb)])
        for gb in range(11):
            stq(Wgu[gb][:, :, 0:256], wg[:, gb * 256:(gb + 1) * 256].rearrange("(k p) c -> p k c", p=128), w=[("Wgu", gb, 0)])
            stq(Wgu[gb][:, :, 256:512], wu[:, gb * 256:(gb + 1) * 256].rearrange("(k p) c -> p k c", p=128), w=[("Wgu", gb, 1)])
        for half in range(2):
            for bi, (k0, k1) in enumerate(((0, 8), (8, 16), (16, 22))):
                src = wd[k0 * 128:k1 * 128, half * 512:(half + 1) * 512].rearrange("(k p) c -> p k c", p=128)
                stq(Wd[half][:, k0:k1, :], src, w=[("Wd", half, bi)])

    wcnt = [0]

    def wslot(nk, ncol):
        r = wcnt[0] % NW
        wcnt[0] += 1
        view = wring[:, r, 0:nk * ncol].rearrange("p (k c) -> p k c", k=nk)
        return view, ("wr", r)

    def load_w(src, nk, ncol, rkeys):
        view, key = wslot(nk, ncol)
        ld(view, src, r=rkeys, w=[key])
        return view, key

    pf = {}
    WSRC = {}
    for cb_ in range(7):
        WSRC[("Wi", cb_)] = (Wi[cb_], 8, 512, [("Wi", cb_)])
    for cb_ in range(2):
        WSRC[("Wo", cb_)] = (Wo[cb_], 8, 512, [("Wo", cb_)])
    for gb_ in range(11):
        WSRC[("Wgu", gb_)] = (Wgu[gb_], 8, 512, [("Wgu", gb_, 0), ("Wgu", gb_, 1)])
    for half_ in range(2):
        for bi_, (k0_, k1_) in enumerate(((0, 8), (8, 16), (16, 22))):
            WSRC[("Wd", half_, bi_)] = (Wd[half_][:, k0_:k1_, :], k1_ - k0_, 512, [("Wd", half_, bi_)])

    def prefetch(name):
        if name not in pf:
            pf[name] = load_w(*WSRC[name])

    def get_w(name):
        if name in pf:
            return pf.pop(name)
        return load_w(*WSRC[name])

    def setup():
        ld(kc[:], kc_d[:], w=["kc"])
        ld(dco[:], dco_d[:], w=["dco"])
        ld(crow5, crow_d[:], w=[("xf", 0)])
        ld(smallr[0:1, 0:256], lamv[:], w=["lamr"])
        ld(smallr[0:1, 256:384], gsub[:], w=["gsr"])
        ld(smallr[0:1, 384:512], gqk[:], w=["gqr"])
        rbt = sb("rbt", [32, 4], F32)
        oh15 = lg[0:32, 1, 0:128]
        ohr = lg[0:32, 0, 0:384]
        ld(rbt[:], rbt_d[:], w=["rbt"])
        ld(oh15, oh15_d[:], w=[("lg", 1)])
        ld(ohr, ohr_d[:], w=[("lg", 0)])
        maskadd = et[:, 3, 0:256]
        ld(maskadd, maskadd_d[:], w=["et3"])
        P.op("dve", lambda e: e.memset(onesb[:], 1.0), writes=["onesb"])
        P.op("dve", lambda e: e.memset(warmb[:], 1.0), writes=["warmb"])
        P.op("dve", lambda e: e.memset(onesf[:], 1.0), writes=["onesf"])
        P.op("dve", lambda e: e.memset(cols[:, EPSC:EPSC + 1], EPS), writes=["epsc"])
        P.op("dve", lambda e: e.memset(S[:], 0.0), writes=["S"])
        P.op("pool", lambda e: e.memset(qAT[:].rearrange("p m h t -> p (m h t)"), 0.0), writes=[("qAT", h) for h in range(4)])
        cp(identb[:], identf, ["kc"], ["identb"])
        cp(trib[:], trif, ["kc"], ["trib"])
        tt(lamt[0:1, 0:64], smallr[0:1, 0:64], smallr[0:1, 64:128], ALU.mult, ["lamr"], ["st_a"])
        tt(lamt[0:1, 64:128], smallr[0:1, 128:192], smallr[0:1, 192:256], ALU.mult, ["lamr"], ["st_a"])
        lam2 = sb("lam2", [1, 8], F32)
        P.op("dve", lambda e: e.tensor_reduce(out=lam2[0:1, 0:2], in_=lamt[0:1, 0:128].rearrange("p (a j) -> p a j", a=2), axis=AX.X, op=ALU.add),
             reads=["st_a"], writes=["lam2a"])
        act(lam2[0:1, 2:4], lam2[0:1, 0:2], AF.Exp, ["lam2a"], ["lam2b"])
        tt(lam2[0:1, 4:5], lam2[0:1, 3:4], lam2[0:1, 2:3], ALU.subtract, ["lam2b"], ["lam2c"])
        ts(lam2[0:1, 5:6], lam2[0:1, 4:5], -LAM_INIT, None, ALU.add, None, ["lam2c"], ["lam2d"])
        P.op("dve", lambda e: e.memset(lam2[0:1, 6:7], 1.0 - LAM_INIT), writes=["lam2e"])
        pm = psb[7]
        mm(pm[:, 0:1], onesf[0:1, 0:128], lam2[0:1, 5:6], True, True, ["onesf", "lam2d"], [PS(7)])
        mm(pm[:, 1:2], smallr[0:1, 256:384], lam2[0:1, 6:7], True, True, ["gsr", "lam2e"], [PS(7)])
        mm(pm[:, 2:6], oh15, rbt[:, :], True, True, [("lg", 1), "rbt"], [PS(7)])
        mm(pm[:, 128:256], onesf[0:1, 0:128], smallr[0:1, 384:512], True, True, ["onesf", "gqr"], [PS(7)])
        cp(cols[:, NLAM:NLAM + 2], pm[:, 0:2], [PS(7)], ["nlam", "gsubc"])
        cp(cols[:, C15:C15 + 4], pm[:, 2:6], [PS(7)], ["c15"])
        cp(gqkb[:], pm[:, 128:256], [PS(7)], ["gqkb"])
        for i in range(2):
            for g in range(8):
                cp(gfull[:, i, g * 64:(g + 1) * 64], gqkb[:, i * 64:(i + 1) * 64], ["gqkb"], ["gfull"])
        ts(cols[:, CC0:CC0 + 4], cols[:, C15:C15 + 4], dco[:, 21:22], dco[:, 22:23], ALU.mult, ALU.add, ["c15", "dco"], ["cc0"])
        ts(cols[:, CC4:CC4 + 4], cols[:, C15:C15 + 4], dco[:, 24:25], dco[:, 25:26], ALU.mult, ALU.add, ["c15", "dco"], ["cc4"])
        ts(cols[:, CCF:CCF + 4], cols[:, C15:C15 + 4], dco[:, 26:27], dco[:, 27:28], ALU.mult, ALU.add, ["c15", "dco"], ["ccf"])
        pu = psb[6]
        mm(pu[0:4, 0:384], rbt[:, :], ohr, True, True, ["rbt", ("lg", 0)], [PS(6)])
        urs = et[0:4, 2, 0:384]
        cp(urs, pu[0:4, 0:384], [PS(6)], ["et2"])
        stq(urd[:], urs, r=["et2"], w=["urd"])
        hk = et[:, 0:2, :].rearrange("p a (b c) -> p (a b) c", c=256)
        ld(hk, AP(urd.tensor, 0, [[1, 128], [384, 4], [1, 256]]), r=["urd"], w=["et0", "et1"])
        for h in range(4):
            pj = psb[h % 2]
            mm(pj[:, 0:256], Jf, hk[:, h, :], True, True, ["kc", "et0", "et1"], [PS(h % 2)])
            tt(Mh[:, h, :], pj[:, 0:256], maskadd, ALU.add, [PS(h % 2), "et3"], [("Mh", h)])
            ts(G0[:, h, :], Mh[:, h, 128:256], dco[:, 20:21], cols[:, CC0 + h:CC0 + h + 1], ALU.mult, ALU.add, [("Mh", h), "dco", "cc0"], [("G0", h)])
            ts(G4[:, h, :], Mh[:, h, 128:256], dco[:, 23:24], cols[:, CC4 + h:CC4 + h + 1], ALU.mult, ALU.add, [("Mh", h), "dco", "cc4"], [("G4", h)])
        ld(MS[0:32, :, :], Mh[0:32, :, 0:32], r=[("Mh", h) for h in range(4)], w=["MS0"])
        ld(MS[32:64, :, :], Mh[0:32, :, 0:32], r=[("Mh", h) for h in range(4)], w=["MS1"])

        act(crow5[0:3, :], crow5[0:3, :], AF.Silu, [("xf", 0)], [("xf", 0)])
        pt = psb[5]
        for k in range(8):
            trn(pt[:, k * 5:(k + 1) * 5], crow5[0:5, k * 128:(k + 1) * 128], identf[0:5, 0:5], [("xf", 0), "kc"], [PS(5)])
        cp(cT[:].rearrange("p k v -> p (k v)"), pt[:, 0:40], [PS(5)], ["cT"])
        cp(scT[:], cT[:, :, 0:3], ["cT"], ["scT"])
        cp(scB[:], fr(cT[:, 0, 0:1], [[5, 8], [0, 128]]), ["cT"], ["scB"])
        cp(scS[:, :, 0:32], fr(cT[:, 0, 1:2], [[5, 8], [0, 32]]), ["cT"], ["scS0"])
        cp(scS[:, :, 32:64], fr(cT[:, 0, 2:3], [[5, 8], [0, 32]]), ["cT"], ["scS1"])
        psm = psb[4]
        psmv = psm[:, 0:96].rearrange("p (a c v) -> p a c v", a=4, c=8)
        for kind, base in ((0, 0), (1, 2), (2, 6), (3, 8)):
            for half in range(2):
                cbk = base + half
                view, key = wslot(8, 512)
                stq(view, w_ada[:, cbk * 512:(cbk + 1) * 512].rearrange("(k p) c -> p k c", p=128), w=[key])
                bslot = 0
                ld(browb[0:1, bslot, :], b_ada[0:1, cbk * 512:(cbk + 1) * 512], w=[("brow", bslot)])
                for e in range(4):
                    c = half * 4 + e
                    for k in range(8):
                        mm(psmv[:, kind, c, :], view[:, k, e * 128:(e + 1) * 128], scT[:, k, :], k == 0, False, [key, "scT"], [PS(4)])
                    mm(psmv[:, kind, c, :], browb[0:1, bslot, e * 128:(e + 1) * 128], onesf[0:1, 0:3], False, True, [("brow", bslot), "onesf"], [PS(4)])
        cp(modc[:, 0], psmv[:, 0], [PS(4)], ["modc0"])
        cp(modc[:, 2], psmv[:, 2], [PS(4)], ["modc2"])
        for c in range(8):
            ts(modc[:, 1, c, :], psmv[:, 1, c, :], 1.0, cT[:, c, 3:4], ALU.add, ALU.mult, [PS(4), "cT"], ["modc1"])
            ts(modc[:, 3, c, :], psmv[:, 3, c, :], 1.0, cT[:, c, 4:5], ALU.add, ALU.mult, [PS(4), "cT"], ["modc3"])

    def gates(sample):
        rows = 64 if sample else 128
        for gi, (gt, base) in enumerate(((gate1b, 4), (gate2b, 10))):
            for half in range(2):
                cbk = base + half
                view, key = wslot(8, 512)
                stq(view, w_ada[:, cbk * 512:(cbk + 1) * 512].rearrange("(k p) c -> p k c", p=128), w=[key])
                bslot = 0
                ld(browb[0:1, bslot, :], b_ada[0:1, cbk * 512:(cbk + 1) * 512], w=[("brow", bslot)])
                pg = psb[(gi * 2 + half) % 4]
                pk = PS((gi * 2 + half) % 4)
                for k in range(8):
                    lhs = scS[:, k, :] if sample else scB[:, k, :]
                    mm(pg[0:rows, :], lhs, view[:, k, :], k == 0, False, [key, "scB", "scS0", "scS1"], [pk])
                mm(pg[0:rows, :], onesf[0:1, 0:rows], browb[0:1, bslot, :], False, True, [("brow", bslot), "onesf"], [pk])
                act(gt[0:rows, half * 512:(half + 1) * 512], pg[0:rows, :], AF.Copy, [pk], [("gate", gi, half)])

    acc_rr = [0]

    ACC_BANKS = (0, 1, 2, 3, 6, 7)

    def next_acc():
        i = ACC_BANKS[acc_rr[0] % len(ACC_BANKS)]
        acc_rr[0] += 1
        return psb[i], PS(i)

    tb_rr = [0]

    def next_tb():
        i = tb_rr[0] % 2
        tb_rr[0] += 1
        return psbb[4 + i], PS(4 + i)

    def rstd_small(dst, src, scale, rows, rk, key):
        act(dst, src, AF.Ln, list(rk) + ["epsc"], [key], scale=scale, bias=cols[0:rows, EPSC:EPSC + 1])
        act(dst, dst, AF.Exp, [key], [key], scale=-0.5)

    def norm_part(xv, xk, rows, subs, sc0, junk_pm=False):
        n = len(subs)
        if junk_pm:
            junk = Pm[0:rows, 0:2, :].rearrange("p a c -> p (a c)")
            jk = [("Pm", 0), ("Pm", 1)]
        else:
            junk = lg[0:rows, :, :].rearrange("p a c -> p (a c)")
            jk = [("lg", 0), ("lg", 1)]
        for i, s in enumerate(subs):
            act(junk, xv[i], AF.Square, [xk[i]], jk + [("ssq", sc0 + i)], accum=stat[0:rows, sc0 + i:sc0 + i + 1])
        rstd_small(stat[0:rows, 8 + sc0:8 + sc0 + n], stat[0:rows, sc0:sc0 + n], 1.0 / D, rows,
                   [("ssq", sc0 + i) for i in range(n)], ("rstd", sc0))
        for i, s in enumerate(subs):
            ts(xn[0:rows, s, :], xv[i], stat[0:rows, 8 + sc0 + i:9 + sc0 + i], None, ALU.mult, None, [xk[i], ("rstd", sc0)], [("xn", s)])

    def transp_part(rows, nsub, kG, kS, sample):
        T = rows * nsub
        mk = ["modc0", "modc1", "modc2", "modc3"]
        for cpair in range(4):
            tbb, tk = next_tb()
            for half in range(2):
                c = 2 * cpair + half
                tb = tbb[:, half * 512:(half + 1) * 512]
                for s in range(nsub):
                    trn(tb[:, s * rows:(s + 1) * rows], xn[0:rows, s, c * 128:(c + 1) * 128], identb[0:rows, 0:rows], [("xn", s), "identb"], [tk])
            for half in range(2):
                c = 2 * cpair + half
                tb = tbb[:, half * 512:(half + 1) * 512]
                if not sample:
                    act(hT[:, c, 0:T], tb[:, 0:T], AF.Identity, [tk] + mk, [("hT", c)],
                        scale=modc[:, kG, c, 0:1], bias=modc[:, kS, c, 0:1])
                else:
                    for q in range(2):
                        act(hT[:, c, q * 32:(q + 1) * 32], tb[:, q * 32:(q + 1) * 32], AF.Identity, [tk] + mk, [("hT", c)],
                            scale=modc[:, kG, c, 1 + q:2 + q], bias=modc[:, kS, c, 1 + q:2 + q])

    def phase_norm(xviews, xkeys, rows, nsub, kG, kS, sample):
        if not sample:
            warm(WARM_NORM, 7)
        norm_part(xviews, xkeys, rows, list(range(nsub)), 0)
        transp_part(rows, nsub, kG, kS, sample)

    def hT_keys(sample):
        return [("hT", c) for c in range(8)]

    def qknorm(acc, ak, rows, goff, dest, dk, par=0):
        e0, e1 = 2 * par, 2 * par + 1
        k0, k1 = f"et{e0}", f"et{e1}"
        sc = 16 + 8 * par
        rk = ("r8", par)
        sq = et[0:rows, e0, :]
        act(sq, acc[0:rows, :], AF.Square, [ak], [k0])
        P.op("dve", lambda e: e.tensor_reduce(out=stat[0:rows, sc:sc + 8], in_=sq.rearrange("p (g j) -> p g j", g=8), axis=AX.X, op=ALU.add),
             reads=[k0], writes=[rk])
        rstd_small(stat[0:rows, sc:sc + 8], stat[0:rows, sc:sc + 8], 1.0 / 64, rows, [rk], rk)
        t1 = et[0:rows, e1, :]
        tt(t1.rearrange("p (g j) -> p g j", g=8), acc[0:rows, :].rearrange("p (g j) -> p g j", g=8),
           fr(stat[0:rows, sc:sc + 1], [[1, 8], [0, 64]]), ALU.mult, [ak, rk], [k1])
        tt(dest, t1, gfull[0:rows, goff // 64, :], ALU.mult, [k1, "gfull"], [dk])

    def rope(acc, ak, rows, tabC, tabS, tkey, dest, dk, par=0):
        e0, e1 = 2 * par, 2 * par + 1
        k0, k1 = f"et{e0}", f"et{e1}"
        xe = acc[0:rows, 0:512:2]
        xo_ = acc[0:rows, 1:512:2]
        t1 = et[0:rows, e0, 0:256]
        t2 = et[0:rows, e0, 256:512]
        t3 = et[0:rows, e1, 0:256]
        t4 = et[0:rows, e1, 256:512]
        tt(t1, xe, tabC, ALU.mult, [ak, tkey], [k0])
        tt(t2, xo_, tabS, ALU.mult, [ak, tkey], [k0])
        tt(dest[:, 0:512:2], t1, t2, ALU.subtract, [k0], [dk])
        tt(t3, xe, tabS, ALU.mult, [ak, tkey], [k1])
        tt(t4, xo_, tabC, ALU.mult, [ak, tkey], [k1])
        tt(dest[:, 1:512:2], t3, t4, ALU.add, [k1], [dk])

    def transp_heads(src, skeys, rows, nsub, dest, dname, split=False):
        T = rows * nsub
        for hp in range(2):
            tbb, tk = next_tb()
            for half in range(2):
                h = 2 * hp + half
                tb = tbb[:, half * 512:(half + 1) * 512]
                for s in range(nsub):
                    trn(tb[:, s * rows:(s + 1) * rows], src[0:rows, s, h * 128:(h + 1) * 128], identb[0:rows, 0:rows], list(skeys(s)) + ["identb"], [tk])
            for half in range(2):
                h = 2 * hp + half
                if split:
                    act(dest[0:64, 0, h, 0:T], tbb[0:64, half * 512:half * 512 + T], AF.Copy, [tk], [(dname, h)])
                    act(dest[64:128, 1, h, 0:T], tbb[64:128, half * 512:half * 512 + T], AF.Copy, [tk], [(dname, h)])
                else:
                    act(dest[:, h, 0:T], tbb[:, half * 512:half * 512 + T], AF.Copy, [tk], [(dname, h)])

    def in_proj(rows, nsub, own, sample, tabsrc, tile_i, hook_after_first=None, next_w=None):
        T = rows * nsub
        hk_ = hT_keys(sample)
        cbs = (1, 2, 0, 4, 5, 3, 6) if own else (1, 2, 4, 5)
        pending = [None]

        def flush():
            if pending[0] is not None:
                pending[0]()
                pending[0] = None
        for ci, cb in enumerate(cbs):
            W, wkey = get_w(("Wi", cb))
            if ci + 1 < len(cbs):
                prefetch(("Wi", cbs[ci + 1]))
            elif next_w is not None:
                prefetch(next_w)
            if cb == 6:
                for e in range(4):
                    acc, ak = next_acc()
                    for k in range(8):
                        mm(acc[:, 0:T], W[:, k, e * 128:(e + 1) * 128], hT[:, k, 0:T], k == 0, k == 7, [wkey] + hk_, [ak])
                    act(sgT[:, e, 0:T], acc[:, 0:T], AF.Silu, [ak], [("sgT", e)])
                flush()
                continue
            for s in range(nsub):
                acc, ak = next_acc()
                for k in range(8):
                    mm(acc[0:rows, :], hT[:, k, s * rows:(s + 1) * rows], W[:, k, :], k == 0, k == 7, [wkey] + hk_, [ak])
                chk(31)
                if cb == 0:
                    qknorm(acc, ak, rows, 0, tokb[0:rows, s, :], ("tokb", s), s % 2)
                elif cb == 1:
                    if own:
                        kfv = kf[0:rows, 0, :]
                        qknorm(acc, ak, rows, 64, kfv, "kf", s % 2)
                        if sample:
                            out_ops.append(stq(ksm[:, :], kfv, r=["kf"]))
                        else:
                            out_ops.append(stq(kp[tile_i, s * 128:(s + 1) * 128, :], kfv, r=["kf"]))
                        act(tokb[0:rows, s, :], kfv, AF.Copy, ["kf"], [("tokb", s)])
                    else:
                        qknorm(acc, ak, rows, 64, tokb[0:rows, s, :], ("tokb", s), s % 2)
                elif cb == 2:
                    if own:
                        vfv = vf[0:rows, 0, :]
                        act(vfv, acc[0:rows, :], AF.Copy, [ak], ["vf"])
                        if sample:
                            out_ops.append(stq(vsm[:, :], vfv, r=["vf"]))
                        else:
                            out_ops.append(stq(vp[tile_i, s * 128:(s + 1) * 128, :], vfv, r=["vf"]))
                        cp(vst[0:rows, s, :], vfv, ["vf"], [("vst", s)])
                    else:
                        evac(vst[0:rows, s, :], acc[0:rows, :], [ak], [("vst", s)])
                elif cb in (3, 4):
                    rs_ = s % 2
                    ld(rt[0:rows, rs_], tabsrc(s), w=[("rt", rs_)])
                    if cb == 3:
                        rope(acc, ak, rows, rt[0:rows, rs_, 0, :], rt[0:rows, rs_, 1, :], ("rt", rs_), tokb[0:rows, s, :], ("tokb", s), s % 2)
                    else:
                        rope(acc, ak, rows, rt[0:rows, rs_, 2, :], rt[0:rows, rs_, 3, :], ("rt", rs_), ke[0:rows, s, :], ("ke", s), s % 2)
                elif cb == 5:
                    evac(vR[0:rows, s, :], acc[0:rows, :], [ak], [("vR", s)])
            if hook_after_first is not None and cb == cbs[0]:
                hook_after_first()
            flush()
            if cb == 0:
                pending[0] = lambda: transp_heads(tokb, lambda s: [("tokb", s)], rows, nsub, qAT, "qAT", split=True)
            elif cb == 1:
                pending[0] = lambda: transp_heads(tokb, lambda s: [("tokb", s)], rows, nsub, kst, "kst")
            elif cb == 3:
                pending[0] = lambda: transp_heads(tokb, lambda s: [("tokb", s)], rows, nsub, qdT, "qdT")
            elif cb == 4 and own:
                pending[0] = lambda: transp_heads(ke, lambda s: [("ke", s)], rows, nsub, keT, "keT")
        flush()

    def store_kv(unit):
        stq(KT[:, unit].rearrange("h p c -> p h c"), kst[:, :, :], r=[("kst", h) for h in range(4)], w=[("KT", unit)])
        for s in range(4):
            dst = VS[:, unit, :, s, :].rearrange("h p d -> p h d")
            stq(dst, vst[:, s, :].rearrange("p (h d) -> p h d", h=4), r=[("vst", s)], w=[("VS", unit, s)])

    def u_raw(rows, nsub, pbase, pu, pk):
        for h in range(4):
            for s in range(nsub):
                mm(pu[:, h * 128:(h + 1) * 128], ke[pbase:pbase + rows, s, h * 128:(h + 1) * 128], vR[pbase:pbase + rows, s, h * 128:(h + 1) * 128],
                   s == 0, s == nsub - 1, [("ke", s), ("vR", s)], [pk])

    def retention_prompt():
        def scores(h):
            for j in range(4):
                pj, pk = next_acc()
                n = 512 - 128 * j
                mm(pj[:, 0:n], keT[:, h, 128 * j:128 * j + 128], qdT[:, h, 128 * j:512], True, True, [("keT", h), ("qdT", h)], [pk])
                tt(scb[:, j, 0:128], pj[:, 0:128], trib[:, :], ALU.mult, [pk, "trib"], [("Pm", j)])
                if n > 128:
                    cp(scb[:, j, 128:n], pj[:, 128:n], [pk], [("Pm", j)])

        def av(h, po, pok):
            mm(po[:, :], Sb[:, h, :], qdT[:, h, :], True, False, ["Sb", ("qdT", h)], [pok])
            for j in range(4):
                n = 512 - 128 * j
                mm(po[:, 128 * j:512], vR[:, j, h * 128:(h + 1) * 128], scb[:, j, 0:n], False, j == 3,
                   [("vR", j), ("Pm", j)], [pok])

        scores(0)
        for h in range(4):
            po, pok = psb[4 + h % 2], PS(4 + h % 2)
            av(h, po, pok)
            if h + 1 < 4:
                scores(h + 1)
            ret_epilogue(po, pok, h, 512)

    def ret_epilogue(po, pok, h, T):
        sq = et[:, 2, 0:T]
        act(sqb[:, 0:T], po[:, 0:T], AF.Square, [pok], ["sqb"])
        pss, psk = psb[7], PS(7)
        mm(pss[:, 0:T], onesb[:, :], sqb[:, 0:T], True, True, ["onesb", "sqb"], [psk])
        rs_ = et[:, 3, 0:T]
        act(rs_, pss[:, 0:T], AF.Ln, [psk, "epsc"], ["et3"], scale=1.0 / 128, bias=cols[:, EPSC:EPSC + 1])
        act(rs_, rs_, AF.Exp, ["et3"], ["et3"], scale=-0.5)
        tt(sq, po[:, 0:T], rs_, ALU.mult, [pok, "et3", "et2"], ["et2"])
        tt(oT[:, 4 + h, 0:T], sq, sgT[:, h, 0:T], ALU.mult, ["et2", ("sgT", h)], [("oT", 4 + h)])

    kv_rr = [0]

    def attention(qc0, N, units, h, first_full=True):
        pO = [psb[4], psb[5]]
        pOk = [PS(4), PS(5)]
        pS = [psb[6], psb[7]]
        pSk = [PS(6), PS(7)]
        tiles = []
        for u in units:
            for (kt, c0, kind, arg) in u["tiles"]:
                tiles.append((u, kt, c0, kind, arg))
        n_tiles = len(tiles)
        info = {}

        def unit_ops(u):
            if id(u) in info:
                return info[id(u)]
            if u.get("sbuf"):
                ktv, vsv = u["ktv"], u["vsv"]
                r = (lambda m, kt: ktv, lambda kt: vsv, u["kkeys"], u["vkeys"], u["pbase"], u["nkeys"])
            else:
                slot = kv_rr[0] % NKB
                kv_rr[0] += 1
                ld(ktb[:, slot, :], u["kt_src"], r=u["kt_keys"], w=[("ktb", slot)])
                ld(vsb[:, slot, :].rearrange("p (k d) -> p k d", k=4), u["vs_src"], r=u["vs_keys"], w=[("vsb", slot)])
                r = (lambda m, kt: ktb[:, slot, kt * 128:(kt + 1) * 128],
                     lambda kt: vsb[:, slot, kt * 128:(kt + 1) * 128], [("ktb", slot)], [("vsb", slot)], 0, 128)
            info[id(u)] = r
            return r

        def emit_qk(t):
            u, kt, c0, kind, arg = tiles[t]
            lk, lv, kkeys, vkeys, pb, nkeys = unit_ops(u)
            pr = slice(pb, pb + nkeys)
            for m in range(2):
                bi = 2 * (t % 2) + m
                mm(psb[bi][pr, c0:N], lk(m, kt), qAT[:, m, h, qc0 + c0:qc0 + N], True, True, kkeys + [("qAT", h)], [PS(bi)])

        def emit_exp(t):
            u, kt, c0, kind, arg = tiles[t]
            lk, lv, kkeys, vkeys, pb, nkeys = unit_ops(u)
            pr = slice(pb, pb + nkeys)
            n = N - c0
            for m in range(2):
                bi = 2 * (t % 2) + m
                pq, pqk = psb[bi], PS(bi)
                pm = Pm[pr, bi, :]
                pmk = ("Pm", bi)
                if kind == "const":
                    if m == 0:
                        b0 = 2 * (t % 2)
                        act(Pm[pr, b0:b0 + 2, c0:N], psall[pr, b0:b0 + 2, c0:N], AF.Exp, [PS(b0), PS(b0 + 1), "c15", "cc0", "cc4", "ccf"],
                            [("Pm", b0), ("Pm", b0 + 1)], scale=0.125, bias=arg[pr, :])
                else:
                    btile, w_, ccol = arg
                    w_ = min(w_, n)
                    lgv = lg[pr, m, 0:w_]
                    stt(lgv, pq[pr, c0:c0 + w_], 0.125, btile[:, 0:w_], ALU.mult, ALU.add, [pqk] + u.get("bkeys", []), [("lg", m)])
                    act(pm[:, c0:c0 + w_], lgv, AF.Exp, [("lg", m)], [pmk])
                    if w_ < n:
                        act(pm[:, c0 + w_:N], pq[pr, c0 + w_:N], AF.Exp, [pqk, "c15", "cc0", "cc4", "ccf"], [pmk], scale=0.125, bias=ccol[pr, :])

        def emit_av(t):
            u, kt, c0, kind, arg = tiles[t]
            lk, lv, kkeys, vkeys, pb, nkeys = unit_ops(u)
            pr = slice(pb, pb + nkeys)
            first, last = (t == 0), (t == n_tiles - 1)
            for m in range(2):
                bi = 2 * (t % 2) + m
                mm(pO[m][:, c0:N], lv(kt), Pm[pr, bi, c0:N], first, last, vkeys + [("Pm", bi)], [pOk[m]])
            for m in range(2):
                bi = 2 * (t % 2) + m
                mm(pS[m][:, c0:N], onesb[pr, :], Pm[pr, bi, c0:N], first, last, ["onesb", ("Pm", bi)], [pSk[m]])

        emit_qk(0)
        for t in range(n_tiles):
            emit_exp(t)
            if t + 1 < n_tiles:
                emit_qk(t + 1)
            emit_av(t)
        T = N
        rinv = lg[:, :, 0:T]
        act(rinv, psall[:, 6:8, 0:T], AF.Ln, [pSk[0], pSk[1]], [("lg", 0), ("lg", 1)])
        act(rinv, rinv, AF.Exp, [("lg", 0), ("lg", 1)], [("lg", 0), ("lg", 1)], scale=-1.0)
        r0 = lg[:, 0, 0:T]
        r1 = lg[:, 1, 0:T]
        a_ = et[:, 1, 0:T]
        b_ = et[:, 2, 0:T]
        tt(a_, pO[0][:, 0:T], r0, ALU.mult, [pOk[0], ("lg", 0)], ["et1"])
        tt(b_, pO[1][:, 0:T], r1, ALU.mult, [pOk[1], ("lg", 1)], ["et2"])
        stt(a_, b_, cols[:, NLAM:NLAM + 1], a_, ALU.mult, ALU.add, ["et1", "et2", "nlam"], ["et1"])
        act(sqb[:, 0:T], a_, AF.Square, ["et1", "et2"], ["sqb"])
        if N == 512:
            warm(WARM_EPI, 3)
        pss, psk = psb[0], PS(0)
        mm(pss[:, 0:T], onesb[:, :], sqb[:, 0:T], True, True, ["onesb", "sqb"], [psk])
        rs_ = et[:, 3, 0:T]
        act(rs_, pss[:, 0:T], AF.Ln, [psk, "epsc"], ["et3"], scale=1.0 / 128, bias=cols[:, EPSC:EPSC + 1])
        act(rs_, rs_, AF.Exp, ["et3"], ["et3"], scale=-0.5)
        stt(oT[:, h, qc0:qc0 + T], a_, cols[:, GSUB:GSUB + 1], rs_, ALU.mult, ALU.mult, ["et1", "et3", "gsubc"], [("oT", h)])

    def prompt_units(step, h):
        units = []
        c15c = cols[:, C15 + h:C15 + h + 1]
        cc0c = cols[:, CC0 + h:CC0 + h + 1]
        cc4c = cols[:, CC4 + h:CC4 + h + 1]
        ccfc = cols[:, CCF + h:CCF + h + 1]
        for u in range(2 * step + 2):
            d = dict(kt_src=KT[h, u], vs_src=VS[h, u], kt_keys=[("KT", u)], vs_keys=[("VS", u, s) for s in range(4)])
            if u == 2 * step + 1:
                d["tiles"] = [(kt, 128 * kt, "bias", (Mh[:, h, :], 256, c15c)) for kt in range(4)]
                d["bkeys"] = [("Mh", h)]
            elif u == 2 * step:
                d["tiles"] = [(kt, 0, "const", ccfc) for kt in range(3)] + [(3, 0, "bias", (G4[:, h, :], 128, cc4c))]
                d["bkeys"] = [("G4", h)]
            elif u == 2 * step - 2:
                d["tiles"] = [(kt, 0, "const", c15c) for kt in range(3)] + [(3, 0, "bias", (G0[:, h, :], 128, cc0c))]
                d["bkeys"] = [("G0", h)]
            else:
                d["tiles"] = [(kt, 0, "const", c15c) for kt in range(4)]
            units.append(d)
        return units

    def out_proj(rows, nsub, xviews, xkeys):
        okeys = [("oT", i) for i in range(8)]
        for cb2 in range(2):
            W, wkey = get_w(("Wo", cb2))
            for s in range(nsub):
                acc, ak = next_acc()
                for k in range(8):
                    mm(acc[0:rows, :], oT[:, k, s * rows:(s + 1) * rows], W[:, k, :], k == 0, k == 7, [wkey] + okeys, [ak])
                t = et[0:rows, (cb2 * nsub + s) % 2, :]
                tkey = f"et{(cb2 * nsub + s) % 2}"
                tt(t, acc[0:rows, :], gate1b[0:rows, cb2 * 512:(cb2 + 1) * 512], ALU.mult, [ak, ("gate", 0, cb2)], [tkey])
                xv = xviews[s][:, cb2 * 512:(cb2 + 1) * 512]
                tt(xv, t, xv, ALU.add, [tkey, xkeys[s]], [xkeys[s]])

    def ffn(rows, nsub, xviews, xkeys, sample, store, gb_hooks=None, mid_hook=None, next_w=None):
        T = rows * nsub
        hk_ = hT_keys(sample)
        for gb in range(11):
            W, wkey = get_w(("Wgu", gb))
            prefetch(("Wgu", gb + 1) if gb + 1 < 11 else ("Wd", 0, 0))
            if gb_hooks is not None and gb in gb_hooks:
                gb_hooks[gb]()
            for fc in range(2):
                pg, pgk = next_acc()
                pu, puk = next_acc()
                for k in range(8):
                    mm(pg[:, 0:T], W[:, k, fc * 128:(fc + 1) * 128], hT[:, k, 0:T], k == 0, k == 7, [wkey] + hk_, [pgk])
                for k in range(8):
                    mm(pu[:, 0:T], W[:, k, 256 + fc * 128:256 + (fc + 1) * 128], hT[:, k, 0:T], k == 0, k == 7, [wkey] + hk_, [puk])
                sgi = (2 * gb + fc) % 2
                sg = lg[:, sgi, 0:T]
                act(sg, pg[:, 0:T], AF.Silu, [pgk], [("lg", sgi)])
                tt(ffT[:, 2 * gb + fc, 0:T], pu[:, 0:T], sg, ALU.mult, [puk, ("lg", sgi)], [("ffT", 2 * gb + fc)])
        fkeys = [("ffT", i) for i in range(NKF)]
        if not sample:
            chk(10)
        if mid_hook is not None:
            mid_hook()
        for half in range(2):
            accs = [next_acc() for s in range(nsub)]
            for bi, (k0, k1) in enumerate(((0, 8), (8, 16), (16, 22))):
                W, wkey = get_w(("Wd", half, bi))
                nxt = ("Wd", half, bi + 1) if bi < 2 else (("Wd", 1, 0) if half == 0 else next_w)
                if nxt is not None:
                    prefetch(nxt)
                for s in range(nsub):
                    acc, ak = accs[s]
                    for kk in range(k0, k1):
                        mm(acc[0:rows, :], ffT[:, kk, s * rows:(s + 1) * rows], W[:, kk - k0, :], kk == 0, kk == NKF - 1, [wkey] + fkeys, [ak])
            for s in range(nsub):
                acc, ak = accs[s]
                t = et[0:rows, s % 2, :]
                tkey = f"et{s % 2}"
                tt(t, acc[0:rows, :], gate2b[0:rows, half * 512:(half + 1) * 512], ALU.mult, [ak, ("gate", 1, half)], [tkey])
                xv = xviews[s][:, half * 512:(half + 1) * 512]
                tt(xv, t, xv, ALU.add, [tkey, xkeys[s]], [xkeys[s]])
                if half == 1:
                    store(s)

    setup()
    prepass_in()
    gates(sample=False)

    def sample_cache_chunk(q, u):
        for h in range(4):
            src = cv[q, u * 512:(u + 1) * 512, h, :].rearrange("(kt p) d -> p kt d", p=128)
            stq(VSs[q, h, u], src, w=[("VSs", q, h, u)])
        kcb = vst
        stq(kcb[:, :, :], ck[q, u * 512:(u + 1) * 512, :].rearrange("(kt p) c -> p kt c", p=128), w=[("vst", s) for s in range(4)])
        for hp in range(2):
            tbb, tk = next_tb()
            for half in range(2):
                h = 2 * hp + half
                for kt in range(4):
                    trn(tbb[:, half * 512 + kt * 128:half * 512 + (kt + 1) * 128], kcb[:, kt, h * 128:(h + 1) * 128], identb[:, :], [("vst", kt), "identb"], [tk])
            for half in range(2):
                h = 2 * hp + half
                act(kst[:, h, :], tbb[:, half * 512:(half + 1) * 512], AF.Copy, [tk], [("kst", h)])
        stq(KTs[q, :, u].rearrange("h p c -> p h c"), kst[:, :, :], r=[("kst", h) for h in range(4)], w=[("KTs", q, u)])

    sample_chunks = [(q, u) for q in range(2) for u in range(4)]

    try:
        def foreign_sub(slot, s):
            ld(xf[:, 0, :], xs[slot, s * 128:(s + 1) * 128, :], w=[("xf", 0)])
            norm_part([xf[:, 0, :]], [("xf", 0)], 128, [s], 24 + s, junk_pm=True)

        def foreign_part1(slot):
            for s in range(4):
                foreign_sub(slot, s)

        def foreign_part2():
            transp_part(128, 4, 1, 0, False)

        if n_steps > 0:
            foreign_part1(0)
            foreign_part2()
        for step in range(n_steps):
            slotF, slotO = 2 * step, 2 * step + 1
            xviews = [xo[:, s, :] for s in range(4)]
            xkeys = [("xo", s) for s in range(4)]
            def own_norm(xviews=xviews, xkeys=xkeys, slotO=slotO):
                for s in range(4):
                    ld(xviews[s], xs[slotO, s * 128:(s + 1) * 128, :], w=[xkeys[s]])
                norm_part(xviews, xkeys, 128, [0, 1, 2, 3], 0)
            chk(1)
            in_proj(128, 4, False, False, lambda s, slot=slotF: rope_d[slot, s * 128:(s + 1) * 128], step, hook_after_first=own_norm, next_w=("Wi", 1))
            chk(2)
            store_kv(slotF)
            pu, pk = psb[7], PS(7)
            u_raw(128, 4, 0, pu, pk)
            cp(Uf[:].rearrange("p h e -> p (h e)"), pu[:, :], [pk], ["Uf"])
            for h in range(4):
                ts(tmpS[:, h, :], S[:, h, :], dco[:, h:h + 1], None, ALU.mult, None, ["S", "dco"], [("tmpS", h)])
                stt(Sst[:, h, :], Uf[:, h, :], dco[:, 4 + h:5 + h], tmpS[:, h, :], ALU.mult, ALU.add, ["Uf", "dco", ("tmpS", h)], [("Sst", h)])
            cp(Sb[:].rearrange("p h e -> p (h e)"), Sst[:].rearrange("p h e -> p (h e)"), [("Sst", h) for h in range(4)], ["Sb"])
            chk(3)
            transp_part(128, 4, 1, 0, False)
            chk(4)
            in_proj(128, 4, True, False, lambda s, slot=slotO: rope_d[slot, s * 128:(s + 1) * 128], step)
            chk(5)
            store_kv(slotO)
            if step == 0:
                prepass_rest()
            if do_sample and sample_chunks and (step >= 1 or n_steps == 1):
                for _ in range(8 if n_steps == 1 else (2 if len(sample_chunks) > 8 - step else 1)):
                    if sample_chunks:
                        sample_cache_chunk(*sample_chunks.pop(0))
            retention_prompt()
            pu, pk = psb[7], PS(7)
            u_raw(128, 4, 0, pu, pk)
            for h in range(4):
                ts(tmpS[:, h, :], S[:, h, :], dco[:, 8 + h:9 + h], None, ALU.mult, None, ["S", "dco"], [("tmpS", h)])
                stt(tmpS[:, h, :], Uf[:, h, :], dco[:, 16 + h:17 + h], tmpS[:, h, :], ALU.mult, ALU.add, ["Uf", "dco", ("tmpS", h)], [("tmpS", h)])
                stt(S[:, h, :], pu[:, h * 128:(h + 1) * 128], dco[:, 12 + h:13 + h], tmpS[:, h, :], ALU.mult, ALU.add, [pk, "dco", ("tmpS", h)], ["S"])
            chk(6)
            prefetch(("Wo", 0))
            prefetch(("Wo", 1))
            prefetch(("Wgu", 0))
            for h in range(4):
                attention(0, 512, prompt_units(step, h), h)
            chk(7)
            out_proj(128, 4, xviews, xkeys)
            chk(8)
            phase_norm(xviews, xkeys, 128, 4, 3, 2, False)
            chk(9)

            def store(s, step=step, xviews=xviews, xkeys=xkeys):
                out_ops.append(stq(y[step, s * 128:(s + 1) * 128, :], xviews[s], r=[xkeys[s]]))
            if step + 1 < n_steps:
                nslot = 2 * (step + 1)
                ffn(128, 4, xviews, xkeys, False, store,
                    gb_hooks={1 + 2 * s_: (lambda nslot=nslot, s_=s_: foreign_sub(nslot, s_)) for s_ in range(4)}, mid_hook=foreign_part2,
                    next_w=("Wi", 1))
            else:
                ffn(128, 4, xviews, xkeys, False, store)
        out_ops.append(stq(rp[:].rearrange("h d e -> d h e"), S[:, :, :], r=["S"]))

        if do_sample:
            while sample_chunks:
                sample_cache_chunk(*sample_chunks.pop(0))
            gates(sample=True)
            xviews = [xo[0:64, 0, :]]
            xkeys = [("xo", 0)]
            ld(xviews[0], xsm[:, :], w=[xkeys[0]])
            phase_norm(xviews, xkeys, 64, 1, 1, 0, True)
            in_proj(64, 1, True, True, lambda s: rope_sd[:, :, :], 0)
            SH = [Sst, tmpS]
            SHk = [[("Sst", h) for h in range(4)], [("tmpS", h) for h in range(4)]]
            for q in range(2):
                ld(SH[q][:, :, :], st[q].rearrange("h d e -> d h e"), w=SHk[q])
            for q in range(2):
                shb = Sb
                cp(shb[:].rearrange("p h e -> p (h e)"), SH[q][:].rearrange("p h e -> p (h e)"), SHk[q], ["Sb"])
                pr = slice(32 * q, 32 * q + 32)
                for h in range(4):
                    pj, pk = next_acc()
                    mm(pj[pr, 0:32], keT[:, h, 32 * q:32 * q + 32], qdT[:, h, 32 * q:32 * q + 32], True, True, [("keT", h), ("qdT", h)], [pk])
                    tt(scb[pr, 0, 0:32], pj[pr, 0:32], trib[pr, 32 * q:32 * q + 32], ALU.mult, [pk, "trib"], [("Pm", 0)])
                    po, pok = psb[6], PS(6)
                    mm(po[:, 0:32], shb[:, h, :], qdT[:, h, 32 * q:32 * q + 32], True, False, ["Sb", ("qdT", h)], [pok])
                    mm(po[:, 0:32], vR[pr, 0, h * 128:(h + 1) * 128], scb[pr, 0, 0:32], False, True, [("vR", 0), ("Pm", 0)], [pok])
                    sq = et[:, 2, 0:32]
                    act(sq, po[:, 0:32], AF.Square, [pok], ["et2"])
                    pss, psk = psb[7], PS(7)
                    mm(pss[:, 0:32], onesf[:, :], sq, True, True, ["onesf", "et2"], [psk])
                    rs_ = et[:, 3, 0:32]
                    act(rs_, pss[:, 0:32], AF.Ln, [psk, "epsc"], ["et3"], scale=1.0 / 128, bias=cols[:, EPSC:EPSC + 1])
                    act(rs_, rs_, AF.Exp, ["et3"], ["et3"], scale=-0.5)
                    tt(sq, po[:, 0:32], rs_, ALU.mult, [pok, "et3", "et2"], ["et2"])
                    tt(oT[:, 4 + h, 32 * q:32 * q + 32], sq, sgT[:, h, 32 * q:32 * q + 32], ALU.mult, ["et2", ("sgT", h)], [("oT", 4 + h)])
                pu, pk = psb[7], PS(7)
                u_raw(32, 1, 32 * q, pu, pk)
                for h in range(4):
                    tt(Uf[:, h, :], pu[:, h * 128:(h + 1) * 128], SH[q][:, h, :], ALU.add, [pk] + SHk[q], ["Uf"])
                    ts(Uf[:, h, :], Uf[:, h, :], dco[:, 28 + h:29 + h], None, ALU.mult, None, ["Uf", "dco"], ["Uf"])
                out_ops.append(stq(rs[q].rearrange("h d e -> d h e"), Uf[:, :, :], r=["Uf"]))
            for q in range(2):
                for h in range(4):
                    c15c = cols[:, C15 + h:C15 + h + 1]
                    units = []
                    for u in range(4):
                        d = dict(kt_src=KTs[q, h, u], vs_src=VSs[q, h, u], kt_keys=[("KTs", q, u)], vs_keys=[("VSs", q, h, u)])
                        if u < 3:
                            d["tiles"] = [(kt, 0, "const", c15c) for kt in range(4)]
                        else:
                            d["tiles"] = [(kt, 0, "const", c15c) for kt in range(3)] + [(3, 0, "bias", (Mh[:, h, 128:160], 32, None))]
                            d["bkeys"] = [("Mh", h)]
                        units.append(d)
                    pr = slice(32 * q, 32 * q + 32)
                    units.append(dict(sbuf=True, ktv=kst[:, h, 32 * q:32 * q + 32], vsv=vst[pr, 0, h * 128:(h + 1) * 128],
                                      kkeys=[("kst", h)], vkeys=[("vst", 0)], pbase=32 * q, nkeys=32,
                                      tiles=[(0, 0, "bias", (MS[pr, h, :], 32, None))], bkeys=["MS0", "MS1"]))
                    attention(32 * q, 32, units, h)
            out_proj(64, 1, xviews, xkeys)
            phase_norm(xviews, xkeys, 64, 1, 3, 2, True)

            def store_s(s):
                out_ops.append(stq(ysm[:, :], xviews[0], r=[xkeys[0]]))
            ffn(64, 1, xviews, xkeys, True, store_s)


    except _StopBuild:
        pass
    P.emit(final_wait_ops=out_ops)
    return nc, P


_PROG_CACHE = {}


def kernel(x_prompt, x_sample, c_prompt, c_sample, cache_k, cache_v, state_ret, w_ada, b_ada,
           g_norm1, g_norm2, w_in, g_q, g_k, lam_q1, lam_k1, lam_q2, lam_k2, g_subln, w_out,
           w_ff_gate, w_ff_up, w_ff_down, rel_bias):
    f = lambda a: np.ascontiguousarray(np.asarray(a, dtype=np.float32))
    x_prompt, x_sample, c_prompt, c_sample = f(x_prompt), f(x_sample), f(c_prompt), f(c_sample)
    cache_k, cache_v, state_ret = f(cache_k), f(cache_v), f(state_ret)
    C = _static_consts()
    if "nc" not in _PROG_CACHE:
        _PROG_CACHE["nc"] = build_program()
    nc, P = _PROG_CACHE["nc"]
    shared = dict(
        w_ada=f(w_ada)[0], b_ada=f(b_ada), w_in=f(w_in)[0],
        gqk=np.concatenate([f(g_q), f(g_k)], axis=1),
        lamv=np.concatenate([f(lam_q1), f(lam_k1), f(lam_q2), f(lam_k2)], axis=1),
        gsub=f(g_subln), w_out=f(w_out)[0], wg=f(w_ff_gate)[0], wu=f(w_ff_up)[0], wd=f(w_ff_down)[0],
        rbt=f(rel_bias), rope_s=C["rope_s"], ohr=C["ohr"], oh15=C["oh15"], maskadd=C["maskadd"], kc=C["kc"],
    )
    in_maps = []
    for c in range(8):
        b, p = c // 2, c % 2
        order = []
        for i in range(NT // 2):
            order += [2 * i + 1 - p, 2 * i + p]
        xb = x_prompt[b].reshape(NT, TT, D)[order]
        m = dict(shared)
        m.update(
            xs=np.ascontiguousarray(xb),
            xsm=np.ascontiguousarray(x_sample[2 * c:2 * c + 2].reshape(64, D)),
            crow=np.ascontiguousarray(np.concatenate([c_prompt[b:b + 1], c_sample[2 * c:2 * c + 2], f(g_norm1), f(g_norm2)], axis=0)),
            ck=np.ascontiguousarray(cache_k[0, 2 * c:2 * c + 2].reshape(2, PAST, 512)),
            cv=np.ascontiguousarray(cache_v[0, 2 * c:2 * c + 2]),
            st=np.ascontiguousarray(state_ret[0, 2 * c:2 * c + 2]),
            rope=C["rope"][p], dco=C["dco"][p],
        )
        in_maps.append(m)
    res = run_bass_kernel_spmd(nc, in_maps, core_ids=list(range(8)))
    R = res.results
    y_prompt = np.zeros((4, SEQ, D), np.float32)
    k_prompt = np.zeros((1, 4, SEQ, 4, 2, 64), np.float32)
    v_prompt = np.zeros((1, 4, SEQ, 4, 128), np.float32)
    ret_prompt = np.zeros((1, 4, 4, 128, 128), np.float32)
    y_sample = np.zeros((16, LS, D), np.float32)
    k_sample = np.zeros((1, 16, LS, 4, 2, 64), np.float32)
    v_sample = np.zeros((1, 16, LS, 4, 128), np.float32)
    ret_sample = np.zeros((1, 16, 4, 128, 128), np.float32)
    for c in range(8):
        b, p = c // 2, c % 2
        r = R[c]
        for i in range(NT // 2):
            t = 2 * i + p
            y_prompt[b, t * TT:(t + 1) * TT] = r["y"][i]
            k_prompt[0, b, t * TT:(t + 1) * TT] = r["kp"][i].reshape(TT, 4, 2, 64)
            v_prompt[0, b, t * TT:(t + 1) * TT] = r["vp"][i].reshape(TT, 4, 128)
        if p == 0:
            ret_prompt[0, b] = r["rp"]
        y_sample[2 * c:2 * c + 2] = r["ysm"].reshape(2, LS, D)
        k_sample[0, 2 * c:2 * c + 2] = r["ksm"].reshape(2, LS, 4, 2, 64)
        v_sample[0, 2 * c:2 * c + 2] = r["vsm"].reshape(2, LS, 4, 128)
        ret_sample[0, 2 * c:2 * c + 2] = r["rs"]
    return (y_prompt, y_sample, k_prompt, v_prompt, ret_prompt, k_sample, v_sample, ret_sample)
```
